# Optimizing a Trainium2 kernel written in Bass

```python
import math
import jax, jax.numpy as jnp
from jax import lax
import numpy as np

D_MODEL = 1024
BATCH = 4
SEQ = 4096
DEPTH = 4

D_MIX = D_MODEL
GROUP_WIDTH = D_MIX // 4
POOL_WINDOWS = (2, 4, 8, 16)
POOL_GROUPS = 4
POOL_DIM = GROUP_WIDTH // POOL_GROUPS
RET_HEADS = 4
RET_HEAD_DIM = GROUP_WIDTH // RET_HEADS
RET_CHUNK = 128
ROPE_BASE = 10000.0
SGU_GROUPS = 4
SGU_DIM = GROUP_WIDTH // SGU_GROUPS
SGU_CHUNK = 128
DN_HEADS = 4
DN_HEAD_DIM = GROUP_WIDTH // DN_HEADS
DN_CONV = 4
DN_CHUNK = 64
D_FF = 2816
FFN_CONV = 3
NORM_EPS = 1e-6

IN_SIZES = (GROUP_WIDTH, GROUP_WIDTH, GROUP_WIDTH, GROUP_WIDTH, GROUP_WIDTH,
            GROUP_WIDTH, GROUP_WIDTH, 3 * GROUP_WIDTH, GROUP_WIDTH, DN_HEADS, DN_HEADS)
IN_SPLITS = tuple(sum(IN_SIZES[:i + 1]) for i in range(len(IN_SIZES) - 1))
P_IN = sum(IN_SIZES)

kernel_name = 'hybrid_parallel_heads_block'


def rms_norm(x, g):
    xf = x.astype(jnp.float32)
    y = xf * lax.rsqrt(jnp.mean(xf * xf, axis=-1, keepdims=True) + NORM_EPS)
    return (y * g.astype(jnp.float32)).astype(x.dtype)


def rms_normalize(xf):
    return xf * lax.rsqrt(jnp.mean(xf * xf, axis=-1, keepdims=True) + NORM_EPS)


def layer_norm(xf, g, b):
    mu = jnp.mean(xf, axis=-1, keepdims=True)
    xc = xf - mu
    var = jnp.mean(xc * xc, axis=-1, keepdims=True)
    return xc * lax.rsqrt(var + NORM_EPS) * g.astype(jnp.float32) + b.astype(jnp.float32)


def l2_normalize(xf):
    return xf * lax.rsqrt(jnp.sum(xf * xf, axis=-1, keepdims=True) + NORM_EPS)


def causal_dwconv(x, w):
    k = w.shape[0]
    rhs = w.astype(x.dtype)[:, None, :]
    return lax.conv_general_dilated(x, rhs, window_strides=(1,), padding=[(k - 1, 0)],
                                    dimension_numbers=('NWC', 'WIO', 'NWC'),
                                    feature_group_count=x.shape[-1])


def rope_tables(seq, dim):
    inv = 1.0 / (ROPE_BASE ** (jnp.arange(0, dim, 2, dtype=jnp.float32) / dim))
    ang = jnp.arange(seq, dtype=jnp.float32)[:, None] * inv[None, :]
    return jnp.cos(ang), jnp.sin(ang)


def apply_rope(x, cos, sin):
    half = x.shape[-1] // 2
    x1, x2 = x[..., :half], x[..., half:]
    c = cos[None, :, None, :]
    s = sin[None, :, None, :]
    return jnp.concatenate([x1 * c - x2 * s, x2 * c + x1 * s], axis=-1)


def pool_mixer(a, pool_w, pool_scale):
    bsz, seq, _ = a.shape
    af = a.astype(jnp.float32).reshape(bsz, seq, POOL_GROUPS, POOL_DIM)
    cs = jnp.cumsum(af, axis=1)
    outs = []
    for gi, win in enumerate(POOL_WINDOWS):
        c = cs[:, :, gi]
        lower = jnp.pad(c[:, :seq - win], ((0, 0), (win, 0), (0, 0)))
        cnt = jnp.minimum(jnp.arange(1, seq + 1), win).astype(jnp.float32)[None, :, None]
        outs.append((c - lower) / cnt - af[:, :, gi])
    d = jnp.stack(outs, axis=2)
    y = jnp.einsum('bsgc,gcd->bsgd', d, pool_w.astype(jnp.float32))
    return y.reshape(bsz, seq, GROUP_WIDTH) * pool_scale.astype(jnp.float32)


def retention(q, k, v):
    bsz, seq, nh, dk = q.shape
    dv = v.shape[-1]
    c = RET_CHUNK
    n = seq // c
    log_gamma = jnp.log(1.0 - 2.0 ** (-5.0 - jnp.arange(nh, dtype=jnp.float32)))
    pos = jnp.arange(c, dtype=jnp.float32)
    diff = pos[:, None] - pos[None, :]
    dmat = jnp.where(diff >= 0, jnp.exp(log_gamma[:, None, None] * jnp.maximum(diff, 0.0)), 0.0)
    qc = q.reshape(bsz, n, c, nh, dk)
    kc = k.reshape(bsz, n, c, nh, dk)
    vc = v.reshape(bsz, n, c, nh, dv)
    scores = jnp.einsum('bnshd,bnthd->bnhst', qc, kc) * dmat
    o_inner = jnp.einsum('bnhst,bnthe->bnshe', scores, vc)
    k_w = jnp.exp(log_gamma[None, :] * (c - 1.0 - pos)[:, None])
    kv = jnp.einsum('bnthd,th,bnthe->nbhde', kc, k_w, vc)
    g_chunk = jnp.exp(log_gamma * c)[None, :, None, None]

    def step(state, kv_n):
        return state * g_chunk + kv_n, state

    _, prev = lax.scan(step, jnp.zeros((bsz, nh, dk, dv), jnp.float32), kv)
    q_w = jnp.exp(log_gamma[None, :] * (pos + 1.0)[:, None])
    o_cross = jnp.einsum('bnshd,sh,nbhde->bnshe', qc, q_w, prev)
    return (o_inner + o_cross).reshape(bsz, seq, nh, dv)


def retention_mixer(q_in, k_in, v_in, g_in, cos, sin):
    bsz, seq, _ = q_in.shape
    shp = (bsz, seq, RET_HEADS, RET_HEAD_DIM)
    q = apply_rope(q_in.astype(jnp.float32).reshape(shp), cos, sin)
    k = apply_rope(k_in.astype(jnp.float32).reshape(shp), cos, sin) * RET_HEAD_DIM ** -0.5
    v = v_in.astype(jnp.float32).reshape(shp)
    o = rms_normalize(retention(q, k, v))
    return o.reshape(bsz, seq, GROUP_WIDTH) * jax.nn.silu(g_in.astype(jnp.float32))


def spatial_gating_mixer(u, v, ln_g, ln_b, ws, bs):
    bsz, seq, _ = u.shape
    n = seq // SGU_CHUNK
    u = jax.nn.gelu(u.astype(jnp.float32), approximate=True)
    v = layer_norm(jax.nn.gelu(v.astype(jnp.float32), approximate=True), ln_g, ln_b)
    vr = v.reshape(bsz, n, SGU_CHUNK, SGU_GROUPS, SGU_DIM)
    mask = jnp.tril(jnp.ones((SGU_CHUNK, SGU_CHUNK), jnp.float32))
    wm = ws.astype(jnp.float32) * mask
    s = jnp.einsum('gts,bnsgc->bntgc', wm, vr) + bs.astype(jnp.float32).T[:, :, None]
    return u * s.reshape(bsz, seq, GROUP_WIDTH)


def gated_delta_rule(q, k, v, g, beta):
    bsz, seq, nh, dk = q.shape
    dv = v.shape[-1]
    c = DN_CHUNK
    n = seq // c

    def chunks(t):
        return jnp.moveaxis(t.reshape((bsz, n, c, nh) + t.shape[3:]), 3, 1)

    qc = chunks(q) * dk ** -0.5
    kc = chunks(k)
    vc = chunks(v)
    bc = chunks(beta)
    decay = jnp.cumsum(chunks(g), axis=-1)
    idx = jnp.arange(c)
    incl = idx[:, None] >= idx[None, :]
    strict = idx[:, None] > idx[None, :]
    diff = decay[..., :, None] - decay[..., None, :]
    dmask = jnp.where(incl, jnp.exp(jnp.where(incl, diff, 0.0)), 0.0)
    k_beta = kc * bc[..., None]
    v_beta = vc * bc[..., None]
    lmat = jnp.where(strict, jnp.einsum('bhnid,bhnjd->bhnij', k_beta, kc) * dmask, 0.0)
    amat = lmat + jnp.eye(c, dtype=jnp.float32)
    rhs = jnp.concatenate([v_beta, k_beta * jnp.exp(decay)[..., None]], axis=-1)
    sol = lax.linalg.triangular_solve(amat, rhs, left_side=True, lower=True, unit_diagonal=True)
    u_c = sol[..., :dv]
    w_c = sol[..., dv:]
    attn = jnp.einsum('bhnid,bhnjd->bhnij', qc, kc) * dmask
    q_dec = qc * jnp.exp(decay)[..., None]
    d_last = decay[..., -1]
    k_tail = kc * jnp.exp(d_last[..., None] - decay)[..., None]

    def step(state, inp):
        u_n, w_n, qd_n, a_n, kt_n, dl_n = inp
        v_new = u_n - jnp.einsum('bhcd,bhde->bhce', w_n, state)
        o_n = jnp.einsum('bhcd,bhde->bhce', qd_n, state) + jnp.einsum('bhij,bhje->bhie', a_n, v_new)
        state = state * jnp.exp(dl_n)[..., None, None] + jnp.einsum('bhcd,bhce->bhde', kt_n, v_new)
        return state, o_n

    xs = tuple(jnp.moveaxis(t, 2, 0) for t in (u_c, w_c, q_dec, attn, k_tail, d_last))
    _, o = lax.scan(step, jnp.zeros((bsz, nh, dk, dv), jnp.float32), xs)
    return jnp.transpose(o, (1, 0, 3, 2, 4)).reshape(bsz, seq, nh, dv)


def deltanet_mixer(qkv_in, z_in, b_in, a_in, conv_w, a_log, dt_bias, norm_g):
    bsz, seq, _ = qkv_in.shape
    shp = (bsz, seq, DN_HEADS, DN_HEAD_DIM)
    qkv = jax.nn.silu(causal_dwconv(qkv_in, conv_w).astype(jnp.float32))
    q, k, v = jnp.split(qkv, 3, axis=-1)
    q = l2_normalize(q.reshape(shp))
    k = l2_normalize(k.reshape(shp))
    v = v.reshape(shp)
    beta = jax.nn.sigmoid(b_in.astype(jnp.float32))
    g = -jnp.exp(a_log.astype(jnp.float32)) * jax.nn.softplus(a_in.astype(jnp.float32) + dt_bias.astype(jnp.float32))
    o = gated_delta_rule(q, k, v, g, beta)
    o = rms_normalize(o) * norm_g.astype(jnp.float32)
    return o.reshape(bsz, seq, GROUP_WIDTH) * jax.nn.silu(z_in.astype(jnp.float32))


def conv_ffn(h, w_up, conv_w, conv_b, w_down):
    a, b = jnp.split(h @ w_up, 2, axis=-1)
    a = causal_dwconv(a, conv_w) + conv_b.astype(a.dtype)
    return (jax.nn.gelu(a, approximate=True) * b) @ w_down


def hybrid_layer(x, g_pre_mix, g_post_mix, g_pre_ffn, g_post_ffn, w_in, pool_w, pool_scale,
                 sgu_ln_g, sgu_ln_b, sgu_ws, sgu_bs, dn_conv_w, dn_a_log, dn_dt_bias, dn_norm_g,
                 w_out, ffn_w_up, ffn_conv_w, ffn_conv_b, ffn_w_down, cos, sin):
    dt = x.dtype
    h = rms_norm(x, g_pre_mix)
    p = h @ w_in
    a_pool, r_q, r_k, r_v, r_g, s_u, s_v, d_qkv, d_z, d_b, d_a = jnp.split(p, IN_SPLITS, axis=-1)
    y_a = pool_mixer(a_pool, pool_w, pool_scale)
    y_b = retention_mixer(r_q, r_k, r_v, r_g, cos, sin)
    y_c = spatial_gating_mixer(s_u, s_v, sgu_ln_g, sgu_ln_b, sgu_ws, sgu_bs)
    y_d = deltanet_mixer(d_qkv, d_z, d_b, d_a, dn_conv_w, dn_a_log, dn_dt_bias, dn_norm_g)
    y = jnp.concatenate([y_a, y_b, y_c, y_d], axis=-1).astype(dt)
    x = x + rms_norm(y @ w_out, g_post_mix)
    h = rms_norm(x, g_pre_ffn)
    x = x + rms_norm(conv_ffn(h, ffn_w_up, ffn_conv_w, ffn_conv_b, ffn_w_down), g_post_ffn)
    return x


def setup_inputs(seed: int = 0) -> dict:
    key = jax.random.key(seed)
    ks = jax.random.split(key, 21)
    L = DEPTH
    D = D_MODEL

    def nrm(k, shape, scale):
        return jax.random.normal(k, shape, jnp.float32) * scale

    dt_init = jnp.exp(jax.random.uniform(ks[14], (L, DN_HEADS), jnp.float32, math.log(1e-3), math.log(1e-1)))
    return {
        'x': nrm(ks[0], (BATCH, SEQ, D), 1.0),
        'norm_pre_mix': 1.0 + nrm(ks[1], (L, D), 0.02),
        'norm_post_mix': 1.0 + nrm(ks[2], (L, D), 0.02),
        'norm_pre_ffn': 1.0 + nrm(ks[3], (L, D), 0.02),
        'norm_post_ffn': 1.0 + nrm(ks[4], (L, D), 0.02),
        'w_in': nrm(ks[5], (L, D, P_IN), D ** -0.5),
        'pool_w': nrm(ks[6], (L, POOL_GROUPS, POOL_DIM, POOL_DIM), POOL_DIM ** -0.5),
        'pool_scale': 1.0 + nrm(ks[7], (L, GROUP_WIDTH), 0.02),
        'sgu_ln_g': 1.0 + nrm(ks[8], (L, GROUP_WIDTH), 0.02),
        'sgu_ln_b': nrm(ks[9], (L, GROUP_WIDTH), 0.02),
        'sgu_ws': nrm(ks[10], (L, SGU_GROUPS, SGU_CHUNK, SGU_CHUNK), SGU_CHUNK ** -0.5),
        'sgu_bs': 1.0 + nrm(ks[11], (L, SGU_GROUPS, SGU_CHUNK), 0.1),
        'dn_conv_w': nrm(ks[12], (L, DN_CONV, 3 * GROUP_WIDTH), DN_CONV ** -0.5),
        'dn_a_log': jnp.log(jax.random.uniform(ks[13], (L, DN_HEADS), jnp.float32, 1.0, 16.0)),
        'dn_dt_bias': dt_init + jnp.log(-jnp.expm1(-dt_init)),
        'dn_norm_g': 1.0 + nrm(ks[15], (L, DN_HEAD_DIM), 0.02),
        'w_out': nrm(ks[16], (L, D_MIX, D), D_MIX ** -0.5),
        'ffn_w_up': nrm(ks[17], (L, D, 2 * D_FF), D ** -0.5),
        'ffn_conv_w': nrm(ks[18], (L, FFN_CONV, D_FF), FFN_CONV ** -0.5),
        'ffn_conv_b': nrm(ks[19], (L, D_FF), 0.02),
        'ffn_w_down': nrm(ks[20], (L, D_FF, D), D_FF ** -0.5),
    }


def reference(x, norm_pre_mix, norm_post_mix, norm_pre_ffn, norm_post_ffn, w_in, pool_w, pool_scale,
              sgu_ln_g, sgu_ln_b, sgu_ws, sgu_bs, dn_conv_w, dn_a_log, dn_dt_bias, dn_norm_g,
              w_out, ffn_w_up, ffn_conv_w, ffn_conv_b, ffn_w_down):
    cos, sin = rope_tables(x.shape[1], RET_HEAD_DIM)
    for l in range(DEPTH):
        x = hybrid_layer(x, norm_pre_mix[l], norm_post_mix[l], norm_pre_ffn[l], norm_post_ffn[l],
                         w_in[l], pool_w[l], pool_scale[l], sgu_ln_g[l], sgu_ln_b[l], sgu_ws[l], sgu_bs[l],
                         dn_conv_w[l], dn_a_log[l], dn_dt_bias[l], dn_norm_g[l], w_out[l],
                         ffn_w_up[l], ffn_conv_w[l], ffn_conv_b[l], ffn_w_down[l], cos, sin)
    return x
```

```python
import numpy as np
from contextlib import ExitStack
import concourse.bass as bass
import concourse.mybir as mybir
from concourse.bass_utils import run_bass_kernel_spmd

F32 = mybir.dt.float32
BF16 = mybir.dt.bfloat16
AF = mybir.ActivationFunctionType
ALU = mybir.AluOpType

D = 1024
PIN = 2824
DFF = 2816
EPS = 1e-6
NFF = DFF // 128


class Buf:
    __slots__ = ("w", "r", "psum", "name")

    def __init__(self, name, psum=False):
        self.w = None
        self.r = {}
        self.psum = psum
        self.name = name


class T:
    __slots__ = ("t", "b")

    def __init__(self, t, b):
        self.t = t
        self.b = b

    def __getitem__(self, key):
        return self.t[key]


class K:
    def __init__(self, nc, es):
        self.nc = nc
        self.es = es
        self.E = {"pe": nc.tensor, "dve": nc.vector, "act": nc.scalar, "pool": nc.gpsimd, "sp": nc.sync}
        self.semh = {}
        self.cnt = {}
        for e in ("pe", "dve", "act", "pool"):
            self.semh[e] = es.enter_context(nc.semaphore("s_" + e))
            self.cnt[e] = 0
        self.seen = {e: {} for e in self.E}
        self.ndma = {"sp": 8, "act": 4}
        self.dma_i = {"sp": 0, "act": 0}
        for q, n in self.ndma.items():
            for j in range(n):
                self.semh[("dma", q, j)] = es.enter_context(nc.semaphore("d_%s%d" % (q, j)))
        self.ps_f = []
        self.ps_b = []
        self.ps_fi = 0
        self.ps_bi = 0
        self.nwait = 0
        self.nins = 0

    def sb(self, name, shape, dt, es=None):
        self.nsb = getattr(self, "nsb", 0) + 1
        name = "sb%d_%s" % (self.nsb, name)
        t = (es or self.es).enter_context(self.nc.sbuf_tensor(name, list(shape), dt))
        return T(t, Buf(name))

    def init_psum(self, nf=6, nb=2):
        for i in range(nf):
            t = self.es.enter_context(self.nc.psum_tensor("psf%d" % i, [128, 512], F32))
            self.ps_f.append(T(t, Buf("psf%d" % i, psum=True)))
        for i in range(nb):
            t = self.es.enter_context(self.nc.psum_tensor("psb%d" % i, [128, 1024], BF16))
            self.ps_b.append(T(t, Buf("psb%d" % i, psum=True)))
        self.work = self.ps_f[3:6]
        self.proj = self.ps_f[0:2]
        self.pj_i = 0
        self.dedicated = self.ps_f[2]

    def ps(self):
        p = self.work[self.ps_fi % len(self.work)]
        self.ps_fi += 1
        return p

    def psj(self):
        p = self.proj[self.pj_i % 2]
        self.pj_i += 1
        return p

    def psb(self):
        p = self.ps_b[self.ps_bi % len(self.ps_b)]
        self.ps_bi += 1
        return p

    def _wait(self, eng, key, val):
        if self.seen[eng].get(key, 0) >= val:
            return
        self.E[eng].wait_ge(self.semh[key], val)
        self.seen[eng][key] = val
        self.nwait += 1

    def _deps(self, eng, reads, writes):
        deps = {}

        def add(tok):
            if tok is None:
                return
            k_, v = tok
            if deps.get(k_, 0) < v:
                deps[k_] = v

        for t in reads:
            b = t.b
            add(b.w)
            if b.psum:
                for k_, v in b.r.items():
                    add((k_, v))
        for t in writes:
            b = t.b
            add(b.w)
            for k_, v in b.r.items():
                add((k_, v))
        for k_, v in deps.items():
            if eng == "pe" and k_ == "pe":
                continue
            self._wait(eng, k_, v)

    def _mark(self, tok, reads, writes):
        k_, v = tok
        for t in reads:
            b = t.b
            if b.psum:
                b.w = tok
                b.r = {}
            else:
                if b.r.get(k_, 0) < v:
                    b.r[k_] = v
        for t in writes:
            b = t.b
            b.w = tok
            b.r = {}

    def op(self, eng, fn, r=(), w=(), inc=True):
        self._deps(eng, r, w)
        ins = fn(self.E[eng])
        self.nins += 1
        if inc:
            self.cnt[eng] += 1
            ins.then_inc(self.semh[eng], 1)
            tok = (eng, self.cnt[eng])
        else:
            tok = (eng, self.cnt[eng] + 1)
        self._mark(tok, r, w)
        return ins

    def dma(self, out, in_, r=(), w=(), q="sp"):
        i = self.dma_i[q]
        n = self.ndma[q]
        j = i % n
        key = ("dma", q, j)
        if i >= n:
            self._wait(q, key, 16 * (i // n))
        self._deps(q, r, w)
        ins = self.E[q].dma_start(out=out, in_=in_)
        ins.then_inc(self.semh[key], 16)
        self.dma_i[q] = i + 1
        self.nins += 1
        tok = (key, 16 * (i // n + 1))
        self._mark(tok, r, w)
        return tok

    def collective(self, groups, ins_ap, outs_ap, r, w):
        if "cc" not in self.semh:
            self.semh["cc"] = self.es.enter_context(self.nc.semaphore("s_cc"))
            self.ncc = 0
        self._deps("pool", r, w)
        ins = self.nc.gpsimd.collective_compute("AllGather", ALU.bypass, replica_groups=groups,
                                                ins=[ins_ap], outs=[outs_ap])
        self.ncc += 1
        ins.then_inc(self.semh["cc"], 1)
        self.nins += 1
        self._mark(("cc", self.ncc), r, w)

    def barrier(self):
        toks = [(e, self.cnt[e]) for e in ("pe", "dve", "act", "pool") if self.cnt[e] > 0]
        for q, n in self.ndma.items():
            i = self.dma_i[q]
            for j in range(n):
                cj = (i - j + n - 1) // n if i > j else 0
                if cj > 0:
                    toks.append((("dma", q, j), 16 * cj))
        for eng in ("pe", "dve", "act", "pool", "sp"):
            for key, val in toks:
                if key != eng:
                    self._wait(eng, key, val)

    def mm(self, out, lhsT, rhs, start, stop, r, w, inc):
        return self.op("pe", lambda e: e.matmul(out, lhsT=lhsT, rhs=rhs, start=start, stop=stop,
                                                skip_group_check=True), r, w, inc=inc)

    def tr(self, out, in_, ident, r, w, inc):
        return self.op("pe", lambda e: e.transpose(out, in_, ident), r, w, inc=inc)

    def tt(self, eng, out, in0, in1, op, r, w):
        return self.op(eng, lambda e: e.tensor_tensor(out=out, in0=in0, in1=in1, op=op), r, w)

    def ts(self, eng, out, in0, s1, s2, op0, op1, r, w):
        if s2 is None:
            return self.op(eng, lambda e: e.tensor_scalar(out=out, in0=in0, scalar1=s1, scalar2=None, op0=op0), r, w)
        return self.op(eng, lambda e: e.tensor_scalar(out=out, in0=in0, scalar1=s1, scalar2=s2, op0=op0, op1=op1), r, w)

    def stt(self, eng, out, in0, scalar, in1, op0, op1, r, w):
        return self.op(eng, lambda e: e.scalar_tensor_tensor(out=out, in0=in0, scalar=scalar, in1=in1,
                                                             op0=op0, op1=op1), r, w)

    def act(self, out, in_, func, r, w, **kw):
        return self.op("act", lambda e: e.activation(out=out, in_=in_, func=func, **kw), r, w)

    def cp(self, eng, out, in_, r, w):
        if eng == "act":
            return self.op("act", lambda e: e.activation(out=out, in_=in_, func=AF.Copy), r, w)
        return self.op(eng, lambda e: e.tensor_copy(out=out, in_=in_), r, w)

    def memset(self, eng, ap, val, w):
        return self.op(eng, lambda e: e.memset(ap, val), (), w)


def bc(ap, shape):
    return ap.to_broadcast(list(shape))


def const_layout(T_):
    NT = T_ // 128
    off = {}
    c = 0
    for name, n in (("ident", 128), ("triu", 128), ("negmask", 128), ("strict", 128), ("ones", 128),
                    ("blk", 128), ("dq", 4), ("dk", 4), ("g128", 4), ("nhalf", 8)):
        off[name] = (c, n)
        c += n
    return off, c


def rope_layout(T_):
    NT = T_ // 128
    return {"cos": (0, NT * 32), "sin": (NT * 32, NT * 32), "nsin": (2 * NT * 32, NT * 32)}, 3 * NT * 32


PM_OFF = {"ptf": (0, 512), "ptr": (512, 512), "ptp": (1024, 512)}
PM_DEV = {"pt0": (0, 512), "ptr": (512, 512), "ptp": (1024, 512)}
NST = 830
GROUPS = [[0, 1], [2, 3], [4, 5], [6, 7]]


def host_consts(T_, pos0=0):
    off, n = const_layout(T_)
    roff, rn = rope_layout(T_)
    NT = T_ // 128
    C = np.zeros((128, n), np.float32)
    R = np.zeros((128, rn), np.float32)
    PM = np.zeros((128, 1536), np.float32)

    def put(name, arr):
        if name in off:
            o, m = off[name]
            C[:, o:o + m] = np.asarray(arr, np.float32).reshape(128, m)
        elif name in roff:
            o, m = roff[name]
            R[:, o:o + m] = np.asarray(arr, np.float32).reshape(128, m)
        else:
            o, m = PM_OFF[name]
            PM[:, o:o + m] = np.asarray(arr, np.float32).reshape(128, m)

    p = np.arange(128)
    put("ident", np.eye(128))
    put("triu", (p[:, None] <= p[None, :]))
    put("negmask", np.where(p[:, None] >= p[None, :], 0.0, 1e30))
    put("strict", (p[:, None] > p[None, :]))
    put("ones", np.ones((128, 128)))
    put("blk", (p[:, None] // 64 == p[None, :] // 64))
    inv = (1.0 / (np.float32(10000.0) ** (np.arange(0, 64, 2, dtype=np.float32) / np.float32(64)))).astype(np.float32)
    pos = np.arange(pos0, pos0 + T_).astype(np.float32)
    ang = (pos[:, None] * inv[None, :]).astype(np.float32)
    cos = np.cos(ang).astype(np.float32).reshape(NT, 128, 32).transpose(1, 0, 2)
    sin = np.sin(ang).astype(np.float32).reshape(NT, 128, 32).transpose(1, 0, 2)
    put("cos", cos)
    put("sin", sin)
    put("nsin", -sin)
    lg = np.log(1.0 - 2.0 ** (-5.0 - np.arange(4, dtype=np.float64)))
    put("dq", np.exp(lg[None, :] * (p[:, None] + 1.0)))
    put("dk", np.exp(-lg[None, :] * (p[:, None] + 1.0)) * 0.125)
    put("g128", np.broadcast_to(np.exp(lg * 128.0)[None, :], (128, 4)))
    put("nhalf", np.full((128, 8), -0.5))
    wins = (2, 4, 8, 16)
    ptf = np.zeros((128, 4, 128))
    ptr = np.zeros((128, 4, 128))
    ptp = np.zeros((128, 4, 128))
    for g, w in enumerate(wins):
        for t in range(128):
            for s in range(max(0, t - w + 1), t + 1):
                ptf[s, g, t] += 1.0 / min(t + 1, w)
                ptr[s, g, t] += 1.0 / w
            ptf[t, g, t] -= 1.0
            ptr[t, g, t] -= 1.0
            for srel in range(t - w + 1, 0):
                ptp[128 + srel, g, t] += 1.0 / w
    put("ptf", ptf)
    put("ptr", ptr)
    put("ptp", ptp)
    return C, R, PM


LP_A = (("gpre", 8), ("gpost", 1024), ("poolw", 512), ("poolsc", 4), ("lng", 256), ("lnb", 256),
        ("wsT", 512), ("bs", 4), ("convw", 24), ("alog", 4), ("dtb", 4), ("normg", 64))
LP_B = (("gpre", 8), ("gpost", 1024), ("fcw", 66), ("fcb", 22))


def lay(spec):
    off = {}
    c = 0
    for name, n in spec:
        off[name] = (c, n)
        c += n
    return off, c


def host_params(inp, l):
    offA, nA = lay(LP_A)
    offB, nB = lay(LP_B)
    A = np.zeros((128, nA), np.float32)
    B = np.zeros((128, nB), np.float32)

    def put(M, off, name, arr):
        o, m = off[name]
        M[:, o:o + m] = np.asarray(arr, np.float32).reshape(128, m)

    def rep(v):
        return np.broadcast_to(np.asarray(v, np.float32).reshape(1, -1), (128, np.asarray(v).size))

    put(A, offA, "gpre", inp["norm_pre_mix"][l].reshape(8, 128).T)
    put(A, offA, "gpost", rep(inp["norm_post_mix"][l]))
    pw = np.zeros((128, 4, 128), np.float32)
    for g in range(4):
        pw[:64, g, :64] = inp["pool_w"][l][g]
    put(A, offA, "poolw", pw)
    psc = np.zeros((128, 4), np.float32)
    psc[:64, :] = inp["pool_scale"][l].reshape(4, 64).T
    put(A, offA, "poolsc", psc)
    put(A, offA, "lng", rep(inp["sgu_ln_g"][l]))
    put(A, offA, "lnb", rep(inp["sgu_ln_b"][l]))
    put(A, offA, "wsT", inp["sgu_ws"][l].transpose(2, 0, 1))
    put(A, offA, "bs", inp["sgu_bs"][l].T)
    put(A, offA, "convw", inp["dn_conv_w"][l].reshape(4, 6, 128).transpose(2, 1, 0))
    put(A, offA, "alog", rep(inp["dn_a_log"][l]))
    put(A, offA, "dtb", rep(inp["dn_dt_bias"][l]))
    put(A, offA, "normg", rep(inp["dn_norm_g"][l]))
    put(B, offB, "gpre", inp["norm_pre_ffn"][l].reshape(8, 128).T)
    put(B, offB, "gpost", rep(inp["norm_post_ffn"][l]))
    put(B, offB, "fcw", inp["ffn_conv_w"][l].reshape(3, NFF, 128).transpose(2, 1, 0))
    put(B, offB, "fcb", inp["ffn_conv_b"][l].reshape(NFF, 128).T)
    return A, B


def build(T_, L, mixers=("pool", "ret", "sgu", "dn"), do_ffn=True, skew=False):
    NT = T_ // 128
    nc = bass.Bass("TRN2", target_bir_lowering=False)
    coff, ncst = const_layout(T_)
    offA, nA = lay(LP_A)
    offB, nB = lay(LP_B)

    x_in = nc.dram_tensor("x", [T_, D], F32, kind="ExternalInput").ap()
    w_in_d = nc.dram_tensor("w_in", [L, D, PIN], F32, kind="ExternalInput").ap()
    w_out_d = nc.dram_tensor("w_out", [L, D, D], F32, kind="ExternalInput").ap()
    w_up_d = nc.dram_tensor("w_up", [L, D, 2 * DFF], F32, kind="ExternalInput").ap()
    w_dn_d = nc.dram_tensor("w_down", [L, DFF, D], F32, kind="ExternalInput").ap()
    cst_d = nc.dram_tensor("cst", [128, ncst], F32, kind="ExternalInput").ap()
    roff, nrope = rope_layout(T_)
    rope_d = nc.dram_tensor("ropet", [128, nrope], F32, kind="ExternalInput").ap()
    pm_d = nc.dram_tensor("pmat", [128, 1536], F32, kind="ExternalInput").ap()
    lpa_d = nc.dram_tensor("lpa", [L, 128, nA], F32, kind="ExternalInput").ap()
    lpb_d = nc.dram_tensor("lpb", [L, 128, nB], F32, kind="ExternalInput").ap()
    flg_d = nc.dram_tensor("flags", [128, 8], F32, kind="ExternalInput").ap()
    st_out_t = nc.dram_tensor("st_out", [128, NST], F32)
    st_all_t = nc.dram_tensor("st_all", [256, NST], F32)
    st_out, st_all = st_out_t.ap(), st_all_t.ap()
    stb_out = T(None, Buf("st_out"))
    stb_all = T(None, Buf("st_all"))
    out_d = nc.dram_tensor("out", [T_, D], F32, kind="ExternalOutput").ap()
    xm_d = nc.dram_tensor("xm_scr", [T_, D], F32, kind="Internal").ap()
    xs_d = nc.dram_tensor("xs_scr", [T_, D], F32, kind="Internal").ap()

    class DT:
        def __init__(self, name, ap):
            self.ap = ap
            self.tiles = [T(None, Buf("%s%d" % (name, i))) for i in range(NT)]

    dx_in, dx_m, dx_s, dx_o = DT("xin", x_in), DT("xm", xm_d), DT("xs", xs_d), DT("xo", out_d)
    wdram = T(None, Buf("wdram"))

    with ExitStack() as es:
        es.enter_context(nc.allow_low_precision("bf16 matmul operands, fp32 accumulation"))
        k = K(nc, es)
        k.init_psum(6, 2)

        cst = k.sb("cst", [128, ncst], F32)
        k.dma(cst[:], cst_d, r=(), w=(cst,))

        def cs(name, a=None, b=None):
            o, n = coff[name]
            if a is None:
                return cst[:, o:o + n]
            return cst[:, o + a:o + b]

        flg = k.sb("flg", [128, 8], F32)
        k.dma(flg[:], flg_d, r=(), w=(flg,))
        ident = k.sb("ident", [128, 128], BF16)
        k.cp("dve", ident[:], cs("ident"), (cst,), (ident,))
        blk = k.sb("blk", [128, 128], BF16)
        k.cp("dve", blk[:], cs("blk"), (cst,), (blk,))

        SW = 1412
        stg = []
        stg_i = [0]
        cast_engs = ("dve", "act", "dve")

        def load_cast(dst_t, dst_ap, src_ap, ncols, nrows=128):
            i = stg_i[0]
            stg_i[0] += 1
            s = stg[i % len(stg)]
            k.dma(s[0:nrows, 0:ncols], src_ap, r=(wdram,), w=(s,))
            k.cp(cast_engs[i % 3], dst_ap, s[0:nrows, 0:ncols], (s,), (dst_t,))

        def rstd_from(ss, n, ncol, tmp, out, f=1.0):
            k.ts("pool", tmp[:, 0:ncol], ss[:, 0:ncol], 1.0 / (n * f * f), EPS / (f * f), ALU.mult, ALU.add, (ss,), (tmp,))
            k.tt("pool", out[:, 0:ncol], tmp[:, 0:ncol], cs("nhalf", 0, ncol), ALU.pow, (tmp, cst), (out,))

        def rstd_act(ss, n, ncol, tmp, out, f=1.0):
            k.act(tmp[:, 0:ncol], ss[:, 0:ncol], AF.Ln, (ss,), (tmp,), scale=1.0 / (n * f * f), bias=EPS / (f * f))
            k.act(out[:, 0:ncol], tmp[:, 0:ncol], AF.Exp, (tmp,), (out,), scale=-0.5)

        for l in range(L):
            src = dx_in if l == 0 else dx_s
            dst = dx_o if l == L - 1 else dx_s
            k.barrier()
            with ExitStack() as pa:
                k.work = k.ps_f[3:6]

                def sb(name, shape, dt=F32):
                    return k.sb("a_" + name, shape, dt, es=pa)

                stg[:] = [sb("stg%d" % i_, [128, 706], F32) for i_ in range(4)]
                w_in = sb("w_in", [128, 8, PIN], BF16)
                wo_pool = sb("wo_pool", [128, 4, D], BF16)
                wo = sb("wo", [128, 6, D], BF16)
                lp = sb("lp", [128, nA], F32)
                k.dma(lp[:], lpa_d[l], r=(), w=(lp,))
                ropet = sb("ropet", [128, nrope], F32)
                k.dma(ropet[:], rope_d, r=(), w=(ropet,))

                def rp(name, a, b):
                    o, n = roff[name]
                    return ropet[:, o + a:o + b]

                ptf = sb("ptf", [128, 512], BF16)
                ptr = sb("ptr", [128, 512], BF16)
                ptp = sb("ptp", [128, 512], BF16)
                for nm_, t_ in (("pt0", ptf), ("ptr", ptr), ("ptp", ptp)):
                    o_, n_ = PM_DEV[nm_]
                    load_cast(t_, t_[:, :], pm_d[:, o_:o_ + n_], n_)

                def P(name, a=None, b=None):
                    o, n = offA[name]
                    if a is None:
                        return lp[:, o:o + n]
                    return lp[:, o + a:o + b]


                poolw = sb("poolw", [128, 4, 128], BF16)
                k.cp("dve", poolw[:].rearrange("p g d -> p (g d)"), P("poolw"), (lp,), (poolw,))
                wmT = sb("wmT", [128, 4, 128], BF16)
                k.tt("dve", wmT[:], P("wsT").rearrange("p (g t) -> p g t", g=4),
                     bc(cs("triu").unsqueeze(1), [128, 4, 128]), ALU.mult, (lp, cst), (wmT,))
                nexpA = sb("nexpA", [128, 4])
                k.act(nexpA[:], P("alog"), AF.Exp, (lp,), (nexpA,))
                k.ts("dve", nexpA[:], nexpA[:], -1.0, None, ALU.mult, None, (nexpA,), (nexpA,))

                xt = [sb("xt%d" % i, [128, D]) for i in range(2)]
                xmo0 = sb("xmo0", [128, D])
                xmo = [xmo0, xmo0]
                junk = sb("junk", [128, D], BF16)
                xn = sb("xn", [128, D], BF16)
                hT = sb("hT", [128, 8, 128], BF16)
                ss = sb("ss", [128, 4])
                tmp4 = sb("tmp4", [128, 4])
                rstd = sb("rstd", [128, 4])
                a_bf = [sb("a_bf%d" % i, [128, 256], BF16) for i in range(2)]
                dT_bf = sb("dT_bf", [128, 4, 128], BF16)
                yaT = sb("yaT", [128, 4, 128], BF16)
                k.memset("pool", dT_bf[:], 0.0, (dT_bf,))
                k.memset("pool", yaT[:], 0.0, (yaT,))
                yT = sb("yT", [128, 6, 128], BF16)
                k.memset("pool", yT[:], 0.0, (yT,))
                gu_2 = [sb("gu%d" % i_, [128, 256]) for i_ in range(2)]
                gv_2 = [sb("gv%d" % i_, [128, 256]) for i_ in range(2)]
                vln = sb("vln", [128, 256])
                vln2 = vln
                vln_bf = sb("vln_bf", [128, 256], BF16)
                s_sum = sb("s_sum", [128, 4])
                s_nm = sb("s_nm", [128, 4])
                s_ss = sb("s_ss", [128, 4])
                s_tmp = sb("s_tmp", [128, 4])
                s_rs = sb("s_rs", [128, 4])
                yc_t = sb("yc_t", [128, 256])
                yc_bf = sb("yc_bf", [128, 256], BF16)
                r_qd_2 = [sb("r_qd%d" % i_, [128, 256]) for i_ in range(2)]
                r_t1 = sb("r_t1", [128, 256])
                r_t2 = sb("r_t2", [128, 256])
                r_kd_2 = [sb("r_kd%d" % i_, [128, 256]) for i_ in range(2)]
                r_u1 = r_t1
                r_u2 = r_t2
                r_q_bf = sb("r_q_bf", [128, 256], BF16)
                r_kz = sb("r_kz", [128, 4, 128], BF16)
                r_v_bf_2 = [sb("r_v_bf%d" % i_, [128, 256], BF16) for i_ in range(2)]
                r_sg_2 = [sb("r_sg%d" % i_, [128, 256]) for i_ in range(2)]
                r_qT = sb("r_qT", [128, 2, 128], BF16)
                r_kTz = sb("r_kTz", [128, 4, 128], BF16)
                r_scm = sb("r_scm", [128, 4, 128], BF16)
                r_S = sb("r_S", [128, 2, 128])
                r_S_bf = sb("r_S_bf", [128, 2, 128], BF16)
                r_St = sb("r_St", [128, 2, 128])
                r_sso = sb("r_sso", [128, 4])
                r_tm = sb("r_tm", [128, 4])
                r_rs = sb("r_rs", [128, 4])
                r_y = sb("r_y", [128, 256])
                r_y_bf = sb("r_y_bf", [128, 256], BF16)
                k.memset("pool", r_kz[:], 0.0, (r_kz,))
                k.memset("pool", r_kTz[:], 0.0, (r_kTz,))
                k.memset("pool", r_S[:], 0.0, (r_S,))
                k.memset("pool", r_S_bf[:], 0.0, (r_S_bf,))
                for t_ in a_bf:
                    k.memset("pool", t_[:], 0.0, (t_,))
                d_xc = sb("d_xc", [128, 6, 131])
                d_acc = sb("d_acc", [128, 6, 128])
                d_tmp = sb("d_tmp", [128, 6, 128])
                d_qkv = d_acc
                d_sq = sb("d_sq", [128, 4, 128], BF16)
                d_qkn = sb("d_qkn", [128, 4, 128], BF16)
                d_qz = sb("d_qz", [128, 4, 128], BF16)
                d_kz = sb("d_kz", [128, 4, 128], BF16)
                d_vT = sb("d_vT", [128, 2, 128], BF16)
                d_sc = sb("d_sc", [128, 32])
                d_ab_2 = [sb("d_ab%d" % i_, [128, 8]) for i_ in range(2)]
                d_g = sb("d_g", [128, 4])
                d_hb = sb("d_hb", [128, 4])
                d_Ug = sb("d_Ug", [128, 4, 128])
                d_rn = d_Ug
                d_dec = sb("d_dec", [128, 8])
                d_arg = sb("d_arg", [128, 4, 128])
                d_E = d_arg
                d_Eb = sb("d_Eb", [128, 4, 128])
                d_eR = sb("d_eR", [128, 4, 128])
                d_qd_2 = [sb("d_qd%d" % i_, [128, 2, 128], BF16) for i_ in range(2)]
                d_L0_2 = [sb("d_L0%d" % i_, [128, 4, 128], BF16) for i_ in range(2)]
                d_Z0_2 = [sb("d_Z0%d" % i_, [128, 4, 128], BF16) for i_ in range(2)]
                d_edl_2 = [sb("d_edl%d" % i_, [128, 4]) for i_ in range(2)]
                d_Pt = sb("d_Pt", [128, 4, 128], BF16)
                d_Rp = sb("d_Rp", [128, 4, 128], BF16)
                d_Dl = sb("d_Dl", [128, 4, 128], BF16)
                d_L = [sb("d_L%d" % i, [128, 4, 128], BF16) for i in range(2)]
                d_Z = [sb("d_Z%d" % i, [128, 4, 128], BF16) for i in range(2)]
                d_P0_2 = [sb("d_P0%d" % i_, [128, 4, 128], BF16) for i_ in range(2)]
                d_Pb = [sb("d_Pb%d" % i, [128, 4, 128], BF16) for i in range(2)]
                d_at = sb("d_at", [128, 4, 128], BF16)
                d_atT_2 = [sb("d_atT%d" % i_, [128, 4, 128], BF16) for i_ in range(2)]
                d_kbz_2 = [sb("d_kbz%d" % i_, [128, 4, 128], BF16) for i_ in range(2)]
                d_ktz_2 = [sb("d_ktz%d" % i_, [128, 4, 128], BF16) for i_ in range(2)]
                d_vb_2 = [sb("d_vb%d" % i_, [128, 256], BF16) for i_ in range(2)]
                d_nWT = sb("d_nWT", [128, 2, 128], BF16)
                d_vn = sb("d_vn", [128, 256], BF16)
                d_S = sb("d_S", [128, 2, 128])
                d_St = sb("d_St", [128, 2, 128])
                d_S_bf = sb("d_S_bf", [128, 2, 128], BF16)
                d_sz_2 = [sb("d_sz%d" % i_, [128, 256]) for i_ in range(2)]
                d_szg_2 = [sb("d_szg%d" % i_, [128, 256]) for i_ in range(2)]
                d_sso = sb("d_sso", [128, 4])
                d_tm = sb("d_tm", [128, 4])
                d_rs = sb("d_rs", [128, 4])
                d_y = sb("d_y", [128, 256])
                d_y_bf = sb("d_y_bf", [128, 256], BF16)
                for t_ in (d_qz, d_kz, d_kbz_2[0], d_kbz_2[1], d_ktz_2[0], d_ktz_2[1]):
                    k.memset("pool", t_[:], 0.0, (t_,))
                k.memset("pool", d_xc[:], 0.0, (d_xc,))
                k.memset("pool", d_S[:], 0.0, (d_S,))
                k.memset("pool", d_S_bf[:], 0.0, (d_S_bf,))
                if skew and l > 0:
                    hfl = flg[:, 0:1]
                    k.dma(r_St[:].rearrange("p j x -> p (j x)"), st_all[0:128, 0:256], r=(stb_all,), w=(r_St,))
                    k.ts("dve", r_S[:], r_St[:], hfl, None, ALU.mult, None, (r_St, flg), (r_S,))
                    k.cp("dve", r_S_bf[:], r_S[:], (r_S,), (r_S_bf,))
                    k.dma(d_St[:].rearrange("p j x -> p (j x)"), st_all[0:128, 256:512], r=(stb_all,), w=(d_St,))
                    k.ts("dve", d_S[:], d_St[:], hfl, None, ALU.mult, None, (d_St, flg), (d_S,))
                    k.cp("dve", d_S_bf[:], d_S[:], (d_S,), (d_S_bf,))
                    k.dma(r_y[:], st_all[0:128, 512:768], r=(stb_all,), w=(r_y,))
                    k.ts("dve", a_bf[1][:], r_y[:], hfl, None, ALU.mult, None, (r_y, flg), (a_bf[1],))
                    k.dma(d_y[:, 0:18], st_all[0:128, 768:786], r=(stb_all,), w=(d_y,))
                    k.ts("dve", d_xc[:, :, 0:3], d_y[:, 0:18].rearrange("p (c j) -> p c j", c=6), hfl, None, ALU.mult, None,
                         (d_y, flg), (d_xc,))
                o_ss = sb("o_ss", [128, 4])
                o_tm = sb("o_tm", [128, 4])
                o_rs = sb("o_rs", [128, 4])
                o_t = [sb("o_t%d" % i, [128, 512]) for i in range(2)]

                v4 = lambda ap: ap.rearrange("p (h x) -> p h x", h=4)
                if l == 0:
                    print("[build] phase A sbuf bytes remaining/partition:", nc.sbuf_bytes_remaining)

                def front(i):
                    par = i % 2
                    x_t = xt[par]
                    gu, gv, r_qd, r_kd, r_sg, d_sz = gu_2[par], gv_2[par], r_qd_2[par], r_kd_2[par], r_sg_2[par], d_sz_2[par]
                    r_v_bf, d_ab = r_v_bf_2[par], d_ab_2[par]
                    k.act(junk[:], x_t[:], AF.Square, (x_t,), (junk, ss), accum_out=ss[:, 0:1])
                    rstd_act(ss, D, 1, tmp4, rstd)
                    k.ts("dve", xn[:], x_t[:], rstd[:, 0:1], None, ALU.mult, None, (x_t, rstd), (xn,))
                    pT = k.psb()
                    for kc in range(8):
                        k.tr(pT[:, kc * 128:(kc + 1) * 128], xn[:, kc * 128:(kc + 1) * 128], ident[:],
                             (xn, ident), (pT,), inc=(kc == 7))
                    k.tt("dve", hT[:], pT[:, :].rearrange("p (c t) -> p c t", c=8),
                         bc(P("gpre").unsqueeze(2), [128, 8, 128]), ALU.mult, (pT, lp), (hT,))
                    yield

                    def proj(c0, c1):
                        ps = k.psj()
                        for kc in range(8):
                            k.mm(ps[:, 0:c1 - c0], hT[:, kc, :], w_in[:, kc, c0:c1], kc == 0, kc == 7,
                                 (hT, w_in), (ps,), inc=(kc == 7))
                        return ps

                    ab = a_bf[par]
                    G0 = proj(0, 512)
                    k.cp("act", ab[:], G0[:, 0:256], (G0,), (ab,))
                    k.tt("dve", v4(r_qd[:]), v4(G0[:, 256:512]), bc(cs("dq").unsqueeze(2), [128, 4, 64]),
                         ALU.mult, (G0, cst), (r_qd,))
                    yield
                    G1 = proj(512, 1024)
                    k.tt("dve", v4(r_kd[:]), v4(G1[:, 0:256]), bc(cs("dk").unsqueeze(2), [128, 4, 64]),
                         ALU.mult, (G1, cst), (r_kd,))
                    k.cp("act", r_v_bf[:], G1[:, 256:512], (G1,), (r_v_bf,))
                    yield
                    G3 = proj(1536, 1792)
                    k.act(gv[:], G3[:, 0:256], AF.Gelu_apprx_tanh, (G3,), (gv, s_sum), accum_out=s_sum[:, par:par + 1])
                    G2 = proj(1024, 1536)
                    k.act(gu[:], G2[:, 256:512], AF.Gelu_apprx_tanh, (G2,), (gu,))
                    k.act(r_sg[:], G2[:, 0:256], AF.Tanh, (G2,), (r_sg,), scale=0.5)
                    k.stt("dve", r_sg[:], r_sg[:], 1.0, G2[:, 0:256], ALU.add, ALU.mult, (r_sg, G2), (r_sg,))
                    yield
                    G4 = proj(2560, 2824)
                    k.act(d_sz[:], G4[:, 0:256], AF.Tanh, (G4,), (d_sz,), scale=0.5)
                    k.stt("dve", d_sz[:], d_sz[:], 1.0, G4[:, 0:256], ALU.add, ALU.mult, (d_sz, G4), (d_sz,))
                    k.tt("dve", d_ab[:, 0:4], G4[:, 260:264], P("dtb"), ALU.add, (G4, lp), (d_ab,))
                    k.cp("dve", d_ab[:, 4:8], G4[:, 256:260], (G4,), (d_ab,))
                    yield
                    if "dn" in mixers:
                        psQ = [k.ps(), k.ps()]
                        for fc in range(6):
                            pq = psQ[fc // 4]
                            for kc in range(8):
                                k.mm(pq[:, (fc % 4) * 128:(fc % 4 + 1) * 128],
                                     w_in[:, kc, 1792 + fc * 128:1792 + (fc + 1) * 128], hT[:, kc, :], kc == 0, kc == 7,
                                     (w_in, hT), (pq,), inc=(kc == 7 and fc in (3, 5)))
                        k.cp("act", d_xc[:, 0:4, 3:131], psQ[0][:, :].rearrange("p (c t) -> p c t", c=4), (psQ[0],), (d_xc,))
                        k.cp("act", d_xc[:, 4:6, 3:131], psQ[1][:, 0:256].rearrange("p (c t) -> p c t", c=2), (psQ[1],), (d_xc,))

                def run_all(gens):
                    gens = list(gens)
                    while gens:
                        for g_ in list(gens):
                            try:
                                next(g_)
                            except StopIteration:
                                gens.remove(g_)

                k.dma(xt[0][:], src.ap[0:128, :], r=(src.tiles[0],), w=(xt[0],))
                if NT > 1:
                    k.dma(xt[1][:], src.ap[128:256, :], r=(src.tiles[1],), w=(xt[1],))
                def chain(*gs):
                    for g_ in gs:
                        yield from g_

                g0 = front(0)
                next(g0)
                for kc in range(8):
                    for hf in range(4):
                        load_cast(w_in, w_in[:, kc, hf * 706:(hf + 1) * 706],
                                  w_in_d[l, kc * 128:(kc + 1) * 128, hf * 706:(hf + 1) * 706], 706)
                k.memset("pool", wo_pool[:], 0.0, (wo_pool,))
                for g in range(4):
                    for hf in range(2):
                        load_cast(wo_pool, wo_pool[0:64, g, hf * 512:(hf + 1) * 512],
                                  w_out_d[l, g * 64:(g + 1) * 64, hf * 512:(hf + 1) * 512], 512, nrows=64)
                for c in range(6):
                    for hf in range(2):
                        load_cast(wo, wo[:, c, hf * 512:(hf + 1) * 512],
                                  w_out_d[l, 256 + c * 128:256 + (c + 1) * 128, hf * 512:(hf + 1) * 512], 512)
                run_all([g0])
                for i in range(NT):
                    par = i % 2
                    x_t = xt[par]
                    gu, gv, r_qd, r_kd, r_sg, d_sz = gu_2[par], gv_2[par], r_qd_2[par], r_kd_2[par], r_sg_2[par], d_sz_2[par]
                    r_v_bf, d_ab = r_v_bf_2[par], d_ab_2[par]
                    ab = a_bf[par]
                    ap_ = a_bf[(i + 1) % 2]

                    def m_pool():
                        psD = k.ps()
                        ptm = ptf if i == 0 else ptr
                        for g in range(4):
                            k.mm(psD[0:64, g * 128:(g + 1) * 128], ab[:, g * 64:(g + 1) * 64],
                                 ptm[:, g * 128:(g + 1) * 128], True, False, (ab, ptm), (psD,),
                                 inc=False)
                            if True:
                                k.mm(psD[0:64, g * 128:(g + 1) * 128], ap_[:, g * 64:(g + 1) * 64],
                                     ptp[:, g * 128:(g + 1) * 128], False, True, (ap_, ptp), (psD,), inc=(g == 3))
                        k.cp("act", dT_bf[0:64, :, :].rearrange("p g t -> p (g t)"), psD[0:64, :], (psD,), (dT_bf,))
                        yield
                        psYa = k.ps()
                        for g in range(4):
                            k.mm(psYa[0:64, g * 128:(g + 1) * 128], poolw[:, g, 0:64], dT_bf[:, g, :], True, True,
                                 (poolw, dT_bf), (psYa,), inc=(g == 3))
                        k.tt("dve", yaT[0:64, :, :], psYa[0:64, :].rearrange("p (g t) -> p g t", g=4),
                             bc(P("poolsc")[0:64, :].unsqueeze(2), [64, 4, 128]), ALU.mult, (psYa, lp), (yaT,))

                    def m_sgu():
                        k.ts("dve", s_nm[:, 0:1], s_sum[:, par:par + 1], -1.0 / 256, None, ALU.mult, None, (s_sum,), (s_nm,))
                        k.act(junk[:, 0:256], gv[:], AF.Square, (gv, s_nm), (junk, s_ss), bias=s_nm[:, 0:1],
                              accum_out=s_ss[:, 0:1])
                        rstd_from(s_ss, 256, 1, s_tmp, s_rs)
                        k.ts("dve", vln[:], gv[:], s_nm[:, 0:1], s_rs[:, 0:1], ALU.add, ALU.mult, (gv, s_nm, s_rs), (vln,))
                        k.tt("dve", vln2[:], vln[:], P("lng"), ALU.mult, (vln, lp), (vln2,))
                        k.tt("dve", vln_bf[:], vln2[:], P("lnb"), ALU.add, (vln2, lp), (vln_bf,))
                        yield
                        psS = k.ps()
                        for g in range(4):
                            k.mm(psS[:, g * 64:(g + 1) * 64], wmT[:, g, :], vln_bf[:, g * 64:(g + 1) * 64], True, True,
                                 (wmT, vln_bf), (psS,), inc=(g == 3))
                        k.tt("dve", v4(yc_t[:]), v4(psS[:, 0:256]), bc(P("bs").unsqueeze(2), [128, 4, 64]), ALU.add,
                             (psS, lp), (yc_t,))
                        k.tt("dve", yc_bf[:], yc_t[:], gu[:], ALU.mult, (yc_t, gu), (yc_bf,))
                        yield
                        pT2 = k.psb()
                        for c in range(2):
                            k.tr(pT2[:, c * 128:(c + 1) * 128], yc_bf[:, c * 128:(c + 1) * 128], ident[:],
                                 (yc_bf, ident), (pT2,), inc=(c == 1))
                        k.cp("act", yT[:, 2:4, :].rearrange("p c t -> p (c t)"), pT2[:, 0:256], (pT2,), (yT,))

                    def m_ret():
                        cosb = bc(rp("cos", i * 32, (i + 1) * 32).unsqueeze(1).unsqueeze(1), [128, 4, 2, 32])
                        sinb = bc(rp("sin", i * 32, (i + 1) * 32).unsqueeze(1), [128, 4, 32])
                        nsinb = bc(rp("nsin", i * 32, (i + 1) * 32).unsqueeze(1), [128, 4, 32])
                        v42 = lambda ap: ap.rearrange("p (h two x) -> p h two x", h=4, two=2)

                        def rope(xd, t1, t2, eng2):
                            k.tt(eng2, v42(t1[:]), v42(xd[:]), cosb, ALU.mult, (xd, ropet), (t1,))
                            k.tt(eng2, v42(t2[:])[:, :, 0, :], v42(xd[:])[:, :, 1, :], nsinb, ALU.mult, (xd, ropet), (t2,))
                            k.tt(eng2, v42(t2[:])[:, :, 1, :], v42(xd[:])[:, :, 0, :], sinb, ALU.mult, (xd, ropet), (t2,))

                        rope(r_qd, r_t1, r_t2, "dve")
                        k.tt("dve", r_q_bf[:], r_t1[:], r_t2[:], ALU.add, (r_t1, r_t2), (r_q_bf,))
                        yield
                        rope(r_kd, r_u1, r_u2, "dve")
                        for hh in range(2):
                            k.tt("dve", r_kz[:, hh::2, hh * 64:(hh + 1) * 64],
                                 v4(r_u1[:])[:, hh::2, :], v4(r_u2[:])[:, hh::2, :], ALU.add, (r_u1, r_u2), (r_kz,))
                        pT3 = k.psb()
                        for c in range(2):
                            k.tr(pT3[:, c * 128:(c + 1) * 128], r_q_bf[:, c * 128:(c + 1) * 128], ident[:],
                                 (r_q_bf, ident), (pT3,), inc=False)
                        for h in range(4):
                            k.tr(pT3[:, (2 + h) * 128:(3 + h) * 128], r_kz[:, h, :], ident[:],
                                 (r_kz, ident), (pT3,), inc=(h == 3))
                        k.cp("act", r_qT[:].rearrange("p c t -> p (c t)"), pT3[:, 0:256], (pT3,), (r_qT,))
                        k.cp("dve", r_kTz[:].rearrange("p c t -> p (c t)"), pT3[:, 256:768], (pT3,), (r_kTz,))
                        yield
                        psSc = k.ps()
                        for h in range(4):
                            k.mm(psSc[:, h * 128:(h + 1) * 128], r_kTz[:, h, :], r_qT[:, h // 2, :], True, True,
                                 (r_kTz, r_qT), (psSc,), inc=(h == 3))
                        k.tt("dve", r_scm[:], psSc[:, :].rearrange("p (h s) -> p h s", h=4),
                             bc(cs("triu").unsqueeze(1), [128, 4, 128]), ALU.mult, (psSc, cst), (r_scm,))
                        yield
                        psO = k.ps()
                        for j in range(2):
                            k.mm(psO[:, j * 128:(j + 1) * 128], r_qT[:, j, :], r_S_bf[:, j, :], True, False,
                                 (r_qT, r_S_bf), (psO,), inc=False)
                            for hh in range(2):
                                h = 2 * j + hh
                                k.mm(psO[:, h * 64:(h + 1) * 64], r_scm[:, h, :], r_v_bf[:, h * 64:(h + 1) * 64],
                                     False, hh == 1, (r_scm, r_v_bf), (psO,), inc=(h == 3))
                        psKV = k.ps()
                        for h in range(4):
                            k.mm(psKV[:, h * 64:(h + 1) * 64], r_kz[:, h, :], r_v_bf[:, h * 64:(h + 1) * 64], True, True,
                                 (r_kz, r_v_bf), (psKV,), inc=(h == 3))
                        k.tt("dve", r_St[:].rearrange("p j x -> p (j x)"), psKV[:, 0:256],
                             r_S[:].rearrange("p j x -> p (j x)"), ALU.add, (psKV, r_S), (r_St,))
                        k.tt("dve", r_S[:].rearrange("p j (hh e) -> p (j hh) e", hh=2),
                             r_St[:].rearrange("p j (hh e) -> p (j hh) e", hh=2),
                             bc(cs("g128").unsqueeze(2), [128, 4, 64]), ALU.mult, (r_St, cst), (r_S,))
                        k.cp("act", r_S_bf[:], r_S[:], (r_S,), (r_S_bf,))
                        for h in range(4):
                            k.act(junk[:, h * 64:(h + 1) * 64], psO[:, h * 64:(h + 1) * 64], AF.Square, (psO,),
                                  (junk, r_sso), accum_out=r_sso[:, h:h + 1])
                        rstd_from(r_sso, 64, 4, r_tm, r_rs, f=0.5)
                        k.tt("dve", v4(r_y[:]), v4(psO[:, 0:256]), bc(r_rs[:, 0:4].unsqueeze(2), [128, 4, 64]), ALU.mult,
                             (psO, r_rs), (r_y,))
                        yield
                        k.tt("dve", r_y_bf[:], r_y[:], r_sg[:], ALU.mult, (r_y, r_sg), (r_y_bf,))
                        pT4 = k.psb()
                        for c in range(2):
                            k.tr(pT4[:, c * 128:(c + 1) * 128], r_y_bf[:, c * 128:(c + 1) * 128], ident[:],
                                 (r_y_bf, ident), (pT4,), inc=(c == 1))
                        k.cp("act", yT[:, 0:2, :].rearrange("p c t -> p (c t)"), pT4[:, 0:256], (pT4,), (yT,))

                    def dn_head(pp):
                        d_sz, d_ab = d_sz_2[pp], d_ab_2[pp]
                        L0, Z0, d_P0, d_atT = d_L0_2[pp], d_Z0_2[pp], d_P0_2[pp], d_atT_2[pp]
                        d_kbz, d_ktz, d_vb, d_qd, d_szg, d_edl = d_kbz_2[pp], d_ktz_2[pp], d_vb_2[pp], d_qd_2[pp], d_szg_2[pp], d_edl_2[pp]
                        cw = P("convw").rearrange("p (c j) -> p c j", c=6)
                        k.tt("dve", d_acc[:], d_xc[:, :, 0:128], bc(cw[:, :, 0:1], [128, 6, 128]), ALU.mult, (d_xc, lp), (d_acc,))
                        for j in range(1, 4):
                            k.tt("dve", d_tmp[:], d_xc[:, :, j:j + 128], bc(cw[:, :, j:j + 1], [128, 6, 128]), ALU.mult,
                                 (d_xc, lp), (d_tmp,))
                            k.tt("dve", d_acc[:], d_acc[:], d_tmp[:], ALU.add, (d_acc, d_tmp), (d_acc,))
                        k.cp("pool", d_xc[:, :, 0:3], d_xc[:, :, 128:131], (d_xc,), (d_xc,))
                        yield
                        k.act(d_tmp[:], d_acc[:], AF.Tanh, (d_acc,), (d_tmp,), scale=0.5)
                        k.stt("dve", d_qkv[:], d_tmp[:], 1.0, d_acc[:], ALU.add, ALU.mult, (d_tmp, d_acc), (d_qkv,))
                        k.tt("dve", v4(d_szg[:]), v4(d_sz[:]), bc(P("normg").unsqueeze(1), [128, 4, 64]),
                             ALU.mult, (d_sz, lp), (d_szg,))
                        k.act(d_sc[:, 12:16], d_ab[:, 4:8], AF.Exp, (d_ab,), (d_sc,), scale=-1.0)
                        k.act(d_sc[:, 4:8], d_ab[:, 0:4], AF.Exp, (d_ab,), (d_sc,))
                        k.act(d_sc[:, 8:12], d_sc[:, 4:8], AF.Ln, (d_sc,), (d_sc,), bias=1.0)
                        k.tt("dve", d_g[:], d_sc[:, 8:12], nexpA[:], ALU.mult, (d_sc, nexpA), (d_g,))
                        k.ts("dve", d_sc[:, 12:16], d_sc[:, 12:16], 1.0, None, ALU.add, None, (d_sc,), (d_sc,))
                        k.op("dve", lambda e: e.reciprocal(out=d_sc[:, 16:20], in_=d_sc[:, 12:16]), (d_sc,), (d_sc,))
                        k.ts("dve", d_hb[:, 0:4], d_sc[:, 16:20], 0.5, None, ALU.mult, None, (d_sc,), (d_hb,))
                        yield
                        for h in range(4):
                            k.ts("dve", d_Ug[:, h, :], cs("triu"), d_g[:, h:h + 1], None, ALU.mult, None, (cst, d_g), (d_Ug,))
                        psC = k.ps()
                        k.mm(psC[:, 0:4], cs("triu"), d_g[:, 0:4], True, True, (cst, d_g), (psC,), inc=False)
                        k.mm(psC[:, 4:8], cs("ones"), d_g[:, 0:4], True, True, (cst, d_g), (psC,), inc=True)
                        psR = k.ps()
                        k.mm(psR[:, :], cs("ones"), d_Ug[:].rearrange("p h x -> p (h x)"), True, True, (cst, d_Ug), (psR,), inc=True)
                        k.cp("dve", d_dec[:, 0:8], psC[:, 0:8], (psC,), (d_dec,))
                        k.act(d_sc[:, 24:28], d_dec[:, 0:4], AF.Exp, (d_dec,), (d_sc,))
                        k.tt("dve", d_sc[:, 0:4], d_dec[:, 4:8], d_dec[:, 0:4], ALU.subtract, (d_dec,), (d_sc,))
                        k.act(d_sc[:, 28:32], d_sc[:, 0:4], AF.Exp, (d_sc,), (d_sc,))
                        k.act(d_edl[:, 0:4], d_dec[:, 4:8], AF.Exp, (d_dec,), (d_edl,))
                        k.stt("dve", d_sc[:, 20:24], d_sc[:, 16:20], -1.0, d_sc[:, 24:28], ALU.mult, ALU.mult, (d_sc,), (d_sc,))
                        k.tt("dve", d_arg[:], psR[:, :].rearrange("p (h x) -> p h x", h=4),
                             bc(d_dec[:, 0:4].unsqueeze(2), [128, 4, 128]), ALU.subtract, (psR, d_dec), (d_arg,))
                        k.tt("dve", d_arg[:], d_arg[:], bc(cs("negmask").unsqueeze(1), [128, 4, 128]), ALU.add, (d_arg, cst), (d_arg,))
                        k.act(d_E[:], d_arg[:], AF.Exp, (d_arg,), (d_E,), scale=-1.0)
                        k.act(d_eR[:].rearrange("p h x -> p (h x)"), psR[:, :], AF.Exp, (psR,), (d_eR,))
                        yield
                        k.tt("dve", d_Eb[:], d_E[:], bc(cs("strict").unsqueeze(1), [128, 4, 128]), ALU.mult, (d_E, cst), (d_Eb,))
                        k.tt("dve", d_Eb[:], d_Eb[:], bc(d_sc[:, 16:20].unsqueeze(2), [128, 4, 128]), ALU.mult, (d_Eb, d_sc), (d_Eb,))
                        k.act(d_sq[:], d_qkv[:, 0:4, :], AF.Square, (d_qkv,), (d_sq,))
                        psN = k.ps()
                        k.mm(psN[:, :], blk[:], d_sq[:].rearrange("p c t -> p (c t)"), True, True, (blk, d_sq), (psN,), inc=True)
                        k.act(d_rn[:, 0:2, :].rearrange("p c t -> p (c t)"), psN[:, 0:256], AF.Ln, (psN,), (d_rn,),
                              scale=64.0, bias=256.0 * EPS)
                        k.act(d_rn[:, 2:4, :].rearrange("p c t -> p (c t)"), psN[:, 256:512], AF.Ln, (psN,), (d_rn,),
                              scale=1.0, bias=4.0 * EPS)
                        k.act(d_rn[:], d_rn[:], AF.Exp, (d_rn,), (d_rn,), scale=-0.5)
                        k.tt("dve", d_qkn[:], d_qkv[:, 0:4, :], d_rn[:], ALU.mult, (d_qkv, d_rn), (d_qkn,))
                        k.cp("act", d_vT[:], d_qkv[:, 4:6, :], (d_qkv,), (d_vT,))
                        yield
                        for hh in range(2):
                            sl = slice(hh * 64, (hh + 1) * 64)
                            k.cp("act", d_qz[sl, hh::2, :], d_qkn[sl, 0:2, :], (d_qkn,), (d_qz,))
                            k.cp("dve", d_kz[sl, hh::2, :], d_qkn[sl, 2:4, :], (d_qkn,), (d_kz,))
                            k.tt("dve", d_qd[sl, :, :], d_qkn[sl, 0:2, :], d_eR[sl, hh::2, :], ALU.mult, (d_qkn, d_eR), (d_qd,))
                        psKK = k.ps()
                        psQK = k.ps()
                        for h in range(4):
                            k.mm(psKK[:, h * 128:(h + 1) * 128], d_kz[:, h, :], d_qkn[:, 2 + h // 2, :], True, True,
                                 (d_kz, d_qkn), (psKK,), inc=(h == 3))
                        for h in range(4):
                            k.mm(psQK[:, h * 128:(h + 1) * 128], d_qz[:, h, :], d_qkn[:, 2 + h // 2, :], True, True,
                                 (d_qz, d_qkn), (psQK,), inc=(h == 3))
                        k.tt("dve", L0[:].rearrange("p h x -> p (h x)"), psKK[:, :], d_Eb[:].rearrange("p h x -> p (h x)"),
                             ALU.mult, (psKK, d_Eb), (L0,))
                        k.tt("dve", d_at[:].rearrange("p h x -> p (h x)"), psQK[:, :], d_E[:].rearrange("p h x -> p (h x)"),
                             ALU.mult, (psQK, d_E), (d_at,))
                        yield
                        pT5 = k.psb()
                        for h in range(4):
                            k.tr(pT5[:, h * 128:(h + 1) * 128], L0[:, h, :], ident[:], (L0, ident), (pT5,), inc=False)
                        for h in range(4):
                            k.tr(pT5[:, (4 + h) * 128:(5 + h) * 128], d_at[:, h, :], ident[:], (d_at, ident), (pT5,), inc=(h == 3))
                        k.cp("act", Z0[:].rearrange("p h x -> p (h x)"), pT5[:, 0:512], (pT5,), (Z0,))
                        k.cp("dve", d_atT[:].rearrange("p h x -> p (h x)"), pT5[:, 512:1024], (pT5,), (d_atT,))
                        yield
                        pT6 = k.psb()
                        for c in range(2):
                            k.tr(pT6[:, c * 128:(c + 1) * 128], d_qkn[:, 2 + c, :], ident[:], (d_qkn, ident), (pT6,), inc=False)
                        for c in range(2):
                            k.tr(pT6[:, (2 + c) * 128:(3 + c) * 128], d_vT[:, c, :], ident[:], (d_vT, ident), (pT6,), inc=(c == 1))
                        ktok = pT6[:, 0:256].rearrange("p (h x) -> p h x", h=4)
                        for hh in range(2):
                            k.tt("dve", d_kbz[:, hh::2, hh * 64:(hh + 1) * 64], ktok[:, hh::2, :],
                                 bc(d_sc[:, 20 + hh:24:2].unsqueeze(2), [128, 2, 64]), ALU.mult, (pT6, d_sc), (d_kbz,))
                            k.tt("dve", d_ktz[:, hh::2, hh * 64:(hh + 1) * 64], ktok[:, hh::2, :],
                                 bc(d_sc[:, 28 + hh:32:2].unsqueeze(2), [128, 2, 64]), ALU.mult, (pT6, d_sc), (d_ktz,))
                        k.tt("dve", v4(d_vb[:]), pT6[:, 256:512].rearrange("p (h x) -> p h x", h=4),
                             bc(d_hb[:, 0:4].unsqueeze(2), [128, 4, 64]), ALU.mult, (pT6, d_hb), (d_vb,))
                        yield
                        k.stt("dve", d_P0[:], Z0[:], -1.0, bc(ident[:, :].unsqueeze(1), [128, 4, 128]), ALU.mult, ALU.add,
                              (ident, Z0), (d_P0,))

                    def dn_tail(pp):
                        L0, Z0, d_P0, d_atT = d_L0_2[pp], d_Z0_2[pp], d_P0_2[pp], d_atT_2[pp]
                        d_kbz, d_ktz, d_vb, d_qd, d_szg, d_edl = d_kbz_2[pp], d_ktz_2[pp], d_vb_2[pp], d_qd_2[pp], d_szg_2[pp], d_edl_2[pp]
                        psP = k.dedicated
                        Lc, Zc, Pc = L0, Z0, d_P0
                        NST = 5
                        for st in range(1, NST + 1):
                            Ln_, Zn_ = d_L[st % 2], d_Z[st % 2]
                            psA = k.ps()
                            for h in range(4):
                                k.mm(psA[:, h * 128:(h + 1) * 128], Zc[:, h, :], Lc[:, h, :], True, True, (Zc, Lc), (psA,), inc=(h == 3))
                            if st < NST:
                                psB = k.ps()
                                for h in range(4):
                                    k.mm(psB[:, h * 128:(h + 1) * 128], Lc[:, h, :], Zc[:, h, :], True, True, (Zc, Lc), (psB,), inc=(h == 3))
                            k.cp("act", Ln_[:].rearrange("p h x -> p (h x)"), psA[:, :], (psA,), (Ln_,))
                            if st < NST:
                                k.cp("dve", Zn_[:].rearrange("p h x -> p (h x)"), psB[:, :], (psB,), (Zn_,))
                            for h in range(4):
                                k.mm(psP[:, h * 128:(h + 1) * 128], Ln_[:, h, :], Pc[:, h, :], st == 1 and h == 0, st == NST,
                                     (Ln_, Pc), (psP,), inc=(h == 3))
                            Pn = d_Pb[st % 2]
                            k.tt("dve", Pn[:].rearrange("p h x -> p (h x)"), psP[:, :], d_P0[:].rearrange("p h x -> p (h x)"),
                                 ALU.add, (psP, d_P0), (Pn,))
                            Lc, Zc, Pc = Ln_, Zn_, Pn
                            yield
                        psZP = k.ps()
                        for h in range(4):
                            k.mm(psZP[:, h * 128:(h + 1) * 128], L0[:, h, :], Pc[:, h, :], True, False, (L0, Pc), (psZP,), inc=False)
                            k.mm(psZP[:, h * 128:(h + 1) * 128], ident[:], Pc[:, h, :], False, True, (ident, Pc), (psZP,), inc=(h == 3))
                        k.stt("dve", d_Rp[:], psZP[:, :].rearrange("p (h x) -> p h x", h=4), -1.0,
                              bc(ident[:, :].unsqueeze(1), [128, 4, 128]), ALU.mult, ALU.add, (psZP, ident), (d_Rp,))
                        pT8 = k.psb()
                        for h in range(4):
                            k.tr(pT8[:, h * 128:(h + 1) * 128], Pc[:, h, :], ident[:], (Pc, ident), (pT8,), inc=(h == 3))
                        k.cp("act", d_Pt[:].rearrange("p h x -> p (h x)"), pT8[:, 0:512], (pT8,), (d_Pt,))
                        yield
                        psDl = k.ps()
                        for h in range(4):
                            k.mm(psDl[:, h * 128:(h + 1) * 128], d_Pt[:, h, :], d_Rp[:, h, :], True, True, (d_Pt, d_Rp), (psDl,), inc=(h == 3))
                        k.cp("act", d_Dl[:].rearrange("p h x -> p (h x)"), psDl[:, :], (psDl,), (d_Dl,))
                        yield
                        psW = k.ps()
                        for j in range(2):
                            for hh in range(2):
                                h = 2 * j + hh
                                k.mm(psW[:, j * 128:(j + 1) * 128], d_kbz[:, h, :], Pc[:, h, :], hh == 0, False,
                                     (d_kbz, Pc), (psW,), inc=False)
                                k.mm(psW[:, j * 128:(j + 1) * 128], d_kbz[:, h, :], d_Dl[:, h, :], False, hh == 1,
                                     (d_kbz, d_Dl), (psW,), inc=(h == 3))
                        k.cp("act", d_nWT[:].rearrange("p j x -> p (j x)"), psW[:, 0:256], (psW,), (d_nWT,))
                        yield
                        psU = k.ps()
                        for j in range(2):
                            k.mm(psU[:, j * 128:(j + 1) * 128], d_nWT[:, j, :], d_S_bf[:, j, :], True, False,
                                 (d_nWT, d_S_bf), (psU,), inc=False)
                            for hh in range(2):
                                h = 2 * j + hh
                                k.mm(psU[:, h * 64:(h + 1) * 64], Pc[:, h, :], d_vb[:, h * 64:(h + 1) * 64], False, False,
                                     (Pc, d_vb), (psU,), inc=False)
                                k.mm(psU[:, h * 64:(h + 1) * 64], d_Dl[:, h, :], d_vb[:, h * 64:(h + 1) * 64], False, hh == 1,
                                     (d_Dl, d_vb), (psU,), inc=(h == 3))
                        k.cp("act", d_vn[:], psU[:, 0:256], (psU,), (d_vn,))
                        yield
                        psO2 = k.ps()
                        for j in range(2):
                            k.mm(psO2[:, j * 128:(j + 1) * 128], d_qd[:, j, :], d_S_bf[:, j, :], True, False,
                                 (d_qd, d_S_bf), (psO2,), inc=False)
                            for hh in range(2):
                                h = 2 * j + hh
                                k.mm(psO2[:, h * 64:(h + 1) * 64], d_atT[:, h, :], d_vn[:, h * 64:(h + 1) * 64], False, hh == 1,
                                     (d_atT, d_vn), (psO2,), inc=(h == 3))
                        psK2 = k.ps()
                        for h in range(4):
                            k.mm(psK2[:, h * 64:(h + 1) * 64], d_ktz[:, h, :], d_vn[:, h * 64:(h + 1) * 64], True, True,
                                 (d_ktz, d_vn), (psK2,), inc=(h == 3))
                        k.tt("dve", d_St[:].rearrange("p j (hh e) -> p (j hh) e", hh=2),
                             d_S[:].rearrange("p j (hh e) -> p (j hh) e", hh=2),
                             bc(d_edl[:, 0:4].unsqueeze(2), [128, 4, 64]), ALU.mult, (d_S, d_edl), (d_St,))
                        k.tt("dve", d_S[:].rearrange("p j x -> p (j x)"), d_St[:].rearrange("p j x -> p (j x)"), psK2[:, 0:256],
                             ALU.add, (d_St, psK2), (d_S,))
                        k.cp("act", d_S_bf[:], d_S[:], (d_S,), (d_S_bf,))
                        for h in range(4):
                            k.act(junk[:, h * 64:(h + 1) * 64], psO2[:, h * 64:(h + 1) * 64], AF.Square, (psO2,),
                                  (junk, d_sso), accum_out=d_sso[:, h:h + 1])
                        rstd_act(d_sso, 64, 4, d_tm, d_rs, f=0.5)
                        k.tt("dve", v4(d_y[:]), v4(psO2[:, 0:256]), bc(d_rs[:, 0:4].unsqueeze(2), [128, 4, 64]), ALU.mult,
                             (psO2, d_rs), (d_y,))
                        yield
                        k.tt("dve", d_y_bf[:], d_y[:], d_szg[:], ALU.mult, (d_y, d_szg), (d_y_bf,))
                        pT7 = k.psb()
                        for c in range(2):
                            k.tr(pT7[:, c * 128:(c + 1) * 128], d_y_bf[:, c * 128:(c + 1) * 128], ident[:],
                                 (d_y_bf, ident), (pT7,), inc=(c == 1))
                        k.cp("act", yT[:, 4:6, :].rearrange("p c t -> p (c t)"), pT7[:, 0:256], (pT7,), (yT,))

                    if i == 0 and "dn" in mixers:
                        run_all([dn_head(0)])
                    gens = []
                    if "dn" in mixers:
                        gens.append(dn_tail(par))
                    for nm_, fn_ in (("ret", m_ret), ("sgu", m_sgu), ("pool", m_pool)):
                        if nm_ in mixers:
                            gens.append(fn_())
                    if i + 1 < NT:
                        if "dn" in mixers:
                            gens.append(chain(front(i + 1), dn_head((i + 1) % 2)))
                        else:
                            gens.append(front(i + 1))
                    run_all(gens)

                    psY = [k.ps(), k.ps()]
                    for n in range(2):
                        for g in range(4):
                            k.mm(psY[n][:, :], yaT[:, g, :], wo_pool[:, g, n * 512:(n + 1) * 512], g == 0, False,
                                 (yaT, wo_pool), (psY[n],), inc=False)
                        for c in range(6):
                            k.mm(psY[n][:, :], yT[:, c, :], wo[:, c, n * 512:(n + 1) * 512], False, c == 5,
                                 (yT, wo), (psY[n],), inc=(c == 5))
                    for n in range(2):
                        k.act(junk[:, n * 512:(n + 1) * 512], psY[n][:, :], AF.Square, (psY[n],), (junk, o_ss),
                              accum_out=o_ss[:, n:n + 1])
                    k.tt("dve", o_ss[:, 2:3], o_ss[:, 0:1], o_ss[:, 1:2], ALU.add, (o_ss,), (o_ss,))
                    k.act(o_tm[:, 0:1], o_ss[:, 2:3], AF.Ln, (o_ss,), (o_tm,), scale=1.0 / D, bias=EPS)
                    k.act(o_rs[:, 0:1], o_tm[:, 0:1], AF.Exp, (o_tm,), (o_rs,), scale=-0.5)
                    if skew:
                        k.tt("dve", o_rs[:, 0:1], o_rs[:, 0:1], flg[:, 1 + l:2 + l], ALU.mult, (o_rs, flg), (o_rs,))
                    xo = xmo[i % 2]
                    for n in range(2):
                        k.stt("dve", o_t[n][:], psY[n][:, :], o_rs[:, 0:1], P("gpost", n * 512, (n + 1) * 512), ALU.mult, ALU.mult,
                              (psY[n], o_rs, lp), (o_t[n],))
                        k.tt("dve", xo[:, n * 512:(n + 1) * 512], o_t[n][:], x_t[:, n * 512:(n + 1) * 512], ALU.add,
                             (o_t[n], x_t), (xo,))
                    tgt = dx_m if do_ffn else dst
                    k.dma(tgt.ap[i * 128:(i + 1) * 128, :], xo[:], r=(xo,), w=(tgt.tiles[i],))
                    if i + 2 < NT:
                        k.dma(xt[par][:], src.ap[(i + 2) * 128:(i + 3) * 128, :], r=(src.tiles[i + 2],), w=(xt[par],))
                if skew and l < L - 1:
                    k.dma(st_out[:, 0:256], r_S[:].rearrange("p j x -> p (j x)"), r=(r_S,), w=(stb_out,))
                    k.dma(st_out[:, 256:512], d_S[:].rearrange("p j x -> p (j x)"), r=(d_S,), w=(stb_out,))
                    k.cp("dve", r_y[:], a_bf[(NT - 1) % 2][:], (a_bf[(NT - 1) % 2],), (r_y,))
                    k.dma(st_out[:, 512:768], r_y[:], r=(r_y,), w=(stb_out,))
                    k.cp("dve", d_y[:, 0:18].rearrange("p (c j) -> p c j", c=6), d_xc[:, :, 0:3], (d_xc,), (d_y,))
                    k.dma(st_out[:, 768:786], d_y[:, 0:18], r=(d_y,), w=(stb_out,))
                k.barrier()

            if not do_ffn:
                continue
            with ExitStack() as pb:
                k.barrier()

                def sb(name, shape, dt=F32):
                    return k.sb("b_" + name, shape, dt, es=pb)

                stg[:] = [sb("stg%d" % i_, [128, 704], F32) for i_ in range(5)]
                w_up = sb("w_up", [128, 8, 2 * DFF], BF16)
                w_dn = sb("w_dn", [128, NFF, D], BF16)
                lp = sb("lp", [128, nB], F32)
                k.dma(lp[:], lpb_d[l], r=(), w=(lp,))

                def P(name, a=None, b=None):
                    o, n = offB[name]
                    if a is None:
                        return lp[:, o:o + n]
                    return lp[:, o + a:o + b]


                NTB = NT // 2
                k.work = k.ps_f[0:6]
                xb = [sb("xb%d" % i, [128, 2, D]) for i in range(2)]
                junk = sb("junk", [128, D], BF16)
                xn = sb("xn", [128, 2, D], BF16)
                hT = sb("hT", [128, 8, 256], BF16)
                gT = sb("gT", [128, NFF, 256], BF16)
                ss = sb("ss", [128, 4])
                tmp4 = sb("tmp4", [128, 4])
                rstd = sb("rstd", [128, 4])
                ca = [sb("ca%d" % i, [128, 258]) for i in range(2)]
                acc = [sb("acc%d" % i, [128, 256]) for i in range(2)]
                ge = [sb("ge%d" % i, [128, 256]) for i in range(2)]
                halo = sb("halo", [128, NFF, 2])
                k.memset("pool", halo[:], 0.0, (halo,))
                o_ss = sb("o_ss", [128, 4])
                o_tm = sb("o_tm", [128, 4])
                o_rs = sb("o_rs", [128, 4])
                o_t = [sb("o_t%d" % i, [128, 512]) for i in range(2)]
                fcw = P("fcw").rearrange("p (c j) -> p c j", c=NFF)
                if l == 0:
                    print("[build] phase B sbuf bytes remaining/partition:", nc.sbuf_bytes_remaining)
                if skew and l > 0:
                    k.dma(o_t[0][:, 0:44], st_all[0:128, 786:830], r=(stb_all,), w=(o_t[0],))
                    k.ts("dve", halo[:].rearrange("p c j -> p (c j)"), o_t[0][:, 0:44], flg[:, 0:1], None, ALU.mult, None,
                         (o_t[0], flg), (halo,))

                def ldx(i):
                    for s in range(2):
                        ti = 2 * i + s
                        k.dma(xb[i % 2][:, s, :], dx_m.ap[ti * 128:(ti + 1) * 128, :], r=(dx_m.tiles[ti],), w=(xb[i % 2],))

                def prenorm1(xx):
                    for s in range(2):
                        k.act(junk[:], xx[:, s, :], AF.Square, (xx,), (junk, ss), accum_out=ss[:, s:s + 1])
                    rstd_from(ss, D, 2, tmp4, rstd)
                    for s in range(2):
                        k.ts("dve", xn[:, s, :], xx[:, s, :], rstd[:, s:s + 1], None, ALU.mult, None, (xx, rstd), (xn,))

                def prenorm2():
                    for s in range(2):
                        pT = k.psb()
                        for kc in range(8):
                            k.tr(pT[:, kc * 128:(kc + 1) * 128], xn[:, s, kc * 128:(kc + 1) * 128], ident[:],
                                 (xn, ident), (pT,), inc=(kc == 7))
                        k.tt("dve", hT[:, :, s * 128:(s + 1) * 128], pT[:, :].rearrange("p (c t) -> p c t", c=8),
                             bc(P("gpre").unsqueeze(2), [128, 8, 128]), ALU.mult, (pT, lp), (hT,))

                ldx(0)
                if NTB > 1:
                    ldx(1)
                prenorm1(xb[0])
                prenorm2()
                for kc in range(8):
                    for q8 in range(8):
                        load_cast(w_up, w_up[:, kc, q8 * 704:(q8 + 1) * 704],
                                  w_up_d[l, kc * 128:(kc + 1) * 128, q8 * 704:(q8 + 1) * 704], 704)
                for c in range(NFF):
                    for hf in range(2):
                        load_cast(w_dn, w_dn[:, c, hf * 512:(hf + 1) * 512],
                                  w_dn_d[l, c * 128:(c + 1) * 128, hf * 512:(hf + 1) * 512], 512)
                for i in range(NTB):
                    x_t = xb[i % 2]
                    pend = [None]
                    for c in range(NFF):
                        ps = k.ps()
                        for kc in range(8):
                            k.mm(ps[:, 0:256], w_up[:, kc, c * 128:(c + 1) * 128], hT[:, kc, :], kc == 0, kc == 7,
                                 (w_up, hT), (ps,), inc=False)
                        for kc in range(8):
                            k.mm(ps[:, 256:512], w_up[:, kc, DFF + c * 128:DFF + (c + 1) * 128], hT[:, kc, :], kc == 0, kc == 7,
                                 (w_up, hT), (ps,), inc=(kc == 7))
                        ca_, acc_, ge_ = ca[c % 2], acc[c % 2], ge[c % 2]
                        k.cp("pool", ca_[:, 0:2], halo[:, c, :], (halo,), (ca_,))
                        k.cp("act", ca_[:, 2:258], ps[:, 0:256], (ps,), (ca_,))
                        k.cp("pool", halo[:, c, :], ca_[:, 256:258], (ca_,), (halo,))
                        k.ts("dve", acc_[:], ca_[:, 0:256], fcw[:, c, 0:1], P("fcb", c, c + 1), ALU.mult, ALU.add, (ca_, lp), (acc_,))
                        k.stt("dve", acc_[:], ca_[:, 1:257], fcw[:, c, 1:2], acc_[:], ALU.mult, ALU.add, (ca_, lp, acc_), (acc_,))
                        k.stt("dve", acc_[:], ca_[:, 2:258], fcw[:, c, 2:3], acc_[:], ALU.mult, ALU.add, (ca_, lp, acc_), (acc_,))
                        if pend[0] is not None:
                            pend[0]()

                        def fin(c=c, ps=ps, acc_=acc_, ge_=ge_):
                            k.act(ge_[:], acc_[:], AF.Gelu_apprx_tanh, (acc_,), (ge_,))
                            k.tt("dve", gT[:, c, :], ps[:, 256:512], ge_[:], ALU.mult, (ps, ge_), (gT,))
                        pend[0] = fin
                    pend[0]()
                    pend[0] = None
                    if i + 1 < NTB:
                        prenorm1(xb[(i + 1) % 2])
                    for s in range(2):
                        if s == 1 and i + 1 < NTB:
                            prenorm2()
                        psY = [k.ps(), k.ps()]
                        for n in range(2):
                            for c in range(NFF):
                                k.mm(psY[n][:, :], gT[:, c, s * 128:(s + 1) * 128], w_dn[:, c, n * 512:(n + 1) * 512], c == 0,
                                     c == NFF - 1, (gT, w_dn), (psY[n],), inc=(c == NFF - 1))
                        for n in range(2):
                            k.act(junk[:, n * 512:(n + 1) * 512], psY[n][:, :], AF.Square, (psY[n],), (junk, o_ss),
                                  accum_out=o_ss[:, n:n + 1])
                        k.tt("pool", o_ss[:, 2:3], o_ss[:, 0:1], o_ss[:, 1:2], ALU.add, (o_ss,), (o_ss,))
                        k.ts("pool", o_tm[:, 0:1], o_ss[:, 2:3], 1.0 / D, EPS, ALU.mult, ALU.add, (o_ss,), (o_tm,))
                        k.tt("pool", o_rs[:, 0:1], o_tm[:, 0:1], cs("nhalf", 0, 1), ALU.pow, (o_tm, cst), (o_rs,))
                        if skew:
                            k.tt("pool", o_rs[:, 0:1], o_rs[:, 0:1], flg[:, 1 + l:2 + l], ALU.mult, (o_rs, flg), (o_rs,))
                        for n in range(2):
                            k.stt("dve", o_t[n][:], psY[n][:, :], o_rs[:, 0:1], P("gpost", n * 512, (n + 1) * 512), ALU.mult, ALU.mult,
                                  (psY[n], o_rs, lp), (o_t[n],))
                            k.tt("pool", x_t[:, s, n * 512:(n + 1) * 512], o_t[n][:], x_t[:, s, n * 512:(n + 1) * 512], ALU.add,
                                 (o_t[n], x_t), (x_t,))
                        ti = 2 * i + s
                        k.dma(dst.ap[ti * 128:(ti + 1) * 128, :], x_t[:, s, :], r=(x_t,), w=(dst.tiles[ti],))
                    if i + 2 < NTB:
                        ldx(i + 2)
                if skew and l < L - 1:
                    k.dma(st_out[:, 786:830], halo[:].rearrange("p c j -> p (c j)"), r=(halo,), w=(stb_out,))
                    k.collective(GROUPS, st_out_t.ap().opt(), st_all_t.ap().opt(), r=(stb_out,), w=(stb_all,))

        for q in ("sp", "act"):
            i = k.dma_i[q]
            n = k.ndma[q]
            for j in range(n):
                cntj = (i - j + n - 1) // n if i > j else 0
                if cntj > 0:
                    k._wait("sp", ("dma", q, j), 16 * cntj)
        print("[build] instructions=%d waits=%d" % (k.nins, k.nwait))
    return nc


def make_inmaps(inputs, T_, L, batches):
    cst, ropet, pmat = host_consts(T_)
    lpa = np.stack([host_params(inputs, l)[0] for l in range(L)])
    lpb = np.stack([host_params(inputs, l)[1] for l in range(L)])
    f = lambda a: np.ascontiguousarray(np.asarray(a, np.float32))
    flags = np.ones((128, 8), np.float32)
    flags[:, 0] = 0.0
    maps = []
    for b in batches:
        maps.append({
            "x": f(inputs["x"][b, :T_]),
            "w_in": f(inputs["w_in"][:L]), "w_out": f(inputs["w_out"][:L]),
            "w_up": f(inputs["ffn_w_up"][:L]), "w_down": f(inputs["ffn_w_down"][:L]),
            "cst": cst, "ropet": ropet, "pmat": pmat, "lpa": lpa, "lpb": lpb, "flags": flags,
        })
    return maps


def make_inmaps_skew(inputs, S, L):
    Th = S // 2
    f = lambda a: np.ascontiguousarray(np.asarray(a, np.float32))
    lp = [host_params(inputs, l) for l in range(L)]
    maps = []
    B = inputs["x"].shape[0]
    slots_h = ([min(s_, L - 1) for s_ in range(L + 1)], [max(s_ - 1, 0) for s_ in range(L + 1)])
    cache = {}
    for h in range(2):
        cst, ropet, pm = host_consts(Th, pos0=h * Th)
        pmat = pm.copy()
        if h == 1:
            pmat[:, 0:512] = pm[:, 512:1024]
        sl = slots_h[h]
        flags = np.zeros((128, 8), np.float32)
        flags[:, 0] = float(h)
        for s_ in range(L + 1):
            flags[:, 1 + s_] = 1.0 if (s_ < L if h == 0 else s_ >= 1) else 0.0
        cache[h] = dict(cst=cst, ropet=ropet, pmat=pmat, flags=flags,
                        w_in=f(inputs["w_in"][sl]), w_out=f(inputs["w_out"][sl]),
                        w_up=f(inputs["ffn_w_up"][sl]), w_down=f(inputs["ffn_w_down"][sl]),
                        lpa=np.stack([lp[l][0] for l in sl]), lpb=np.stack([lp[l][1] for l in sl]))
    for c in range(2 * B):
        b, h = c // 2, c % 2
        m = dict(cache[h])
        m["x"] = f(inputs["x"][b, h * Th:(h + 1) * Th])
        maps.append(m)
    return maps


def kernel(**inputs):
    inputs = {k_: np.asarray(v) for k_, v in inputs.items()}
    B, S, _ = inputs["x"].shape
    L = inputs["w_in"].shape[0]
    Th = S // 2
    nc = build(Th, L + 1, skew=True)
    maps = make_inmaps_skew(inputs, S, L)
    res = run_bass_kernel_spmd(nc, maps, core_ids=list(range(2 * B)))
    out = np.zeros((B, S, D), np.float32)
    for c in range(2 * B):
        b, h = c // 2, c % 2
        out[b, h * Th:(h + 1) * Th] = np.asarray(res.results[c]["out"]).reshape(Th, D)
    return out
```

```python
import numpy as np
from contextlib import ExitStack
import concourse.bass as bass
import concourse.mybir as mybir
from concourse.bass_utils import run_bass_kernel_spmd

F32 = mybir.dt.float32
BF16 = mybir.dt.bfloat16
AF = mybir.ActivationFunctionType
ALU = mybir.AluOpType

D = 1024
PIN = 2824
DFF = 2816
EPS = 1e-6
NFF = DFF // 128


class Buf:
    __slots__ = ("w", "r", "psum", "name")

    def __init__(self, name, psum=False):
        self.w = None
        self.r = {}
        self.psum = psum
        self.name = name


class T:
    __slots__ = ("t", "b")

    def __init__(self, t, b):
        self.t = t
        self.b = b

    def __getitem__(self, key):
        return self.t[key]


class K:
    def __init__(self, nc, es):
        self.nc = nc
        self.es = es
        self.E = {"pe": nc.tensor, "dve": nc.vector, "act": nc.scalar, "pool": nc.gpsimd, "sp": nc.sync}
        self.semh = {}
        self.cnt = {}
        for e in ("pe", "dve", "act", "pool"):
            self.semh[e] = es.enter_context(nc.semaphore("s_" + e))
            self.cnt[e] = 0
        self.seen = {e: {} for e in self.E}
        self.ndma = {"sp": 12, "act": 4}
        self.dma_i = {"sp": 0, "act": 0}
        for q, n in self.ndma.items():
            for j in range(n):
                self.semh[("dma", q, j)] = es.enter_context(nc.semaphore("d_%s%d" % (q, j)))
        self.ps_f = []
        self.ps_b = []
        self.ps_fi = 0
        self.ps_bi = 0
        self.nwait = 0
        self.nins = 0

    def sb(self, name, shape, dt, es=None):
        self.nsb = getattr(self, "nsb", 0) + 1
        name = "sb%d_%s" % (self.nsb, name)
        t = (es or self.es).enter_context(self.nc.sbuf_tensor(name, list(shape), dt))
        return T(t, Buf(name))

    def init_psum(self, nf=6, nb=2):
        for i in range(nf):
            t = self.es.enter_context(self.nc.psum_tensor("psf%d" % i, [128, 512], F32))
            self.ps_f.append(T(t, Buf("psf%d" % i, psum=True)))
        for i in range(nb):
            t = self.es.enter_context(self.nc.psum_tensor("psb%d" % i, [128, 1024], BF16))
            self.ps_b.append(T(t, Buf("psb%d" % i, psum=True)))
        self.work = self.ps_f[3:6]
        self.proj = self.ps_f[0:2]
        self.pj_i = 0
        self.dedicated = self.ps_f[2]

    def ps(self):
        p = self.work[self.ps_fi % len(self.work)]
        self.ps_fi += 1
        return p

    def psj(self):
        p = self.proj[self.pj_i % 2]
        self.pj_i += 1
        return p

    def psb(self):
        p = self.ps_b[self.ps_bi % len(self.ps_b)]
        self.ps_bi += 1
        return p

    def _wait(self, eng, key, val):
        if self.seen[eng].get(key, 0) >= val:
            return
        self.E[eng].wait_ge(self.semh[key], val)
        self.seen[eng][key] = val
        self.nwait += 1

    def _deps(self, eng, reads, writes):
        deps = {}

        def add(tok):
            if tok is None:
                return
            k_, v = tok
            if deps.get(k_, 0) < v:
                deps[k_] = v

        for t in reads:
            b = t.b
            add(b.w)
            if b.psum:
                for k_, v in b.r.items():
                    add((k_, v))
        for t in writes:
            b = t.b
            add(b.w)
            for k_, v in b.r.items():
                add((k_, v))
        for k_, v in deps.items():
            if eng == "pe" and k_ == "pe":
                continue
            self._wait(eng, k_, v)

    def _mark(self, tok, reads, writes):
        k_, v = tok
        for t in reads:
            b = t.b
            if b.psum:
                b.w = tok
                b.r = {}
            else:
                if b.r.get(k_, 0) < v:
                    b.r[k_] = v
        for t in writes:
            b = t.b
            b.w = tok
            b.r = {}

    def op(self, eng, fn, r=(), w=(), inc=True):
        self._deps(eng, r, w)
        ins = fn(self.E[eng])
        self.nins += 1
        if inc:
            self.cnt[eng] += 1
            ins.then_inc(self.semh[eng], 1)
            tok = (eng, self.cnt[eng])
        else:
            tok = (eng, self.cnt[eng] + 1)
        self._mark(tok, r, w)
        return ins

    def dma(self, out, in_, r=(), w=(), q="sp"):
        i = self.dma_i[q]
        n = self.ndma[q]
        j = i % n
        key = ("dma", q, j)
        if i >= n:
            self._wait(q, key, 16 * (i // n))
        self._deps(q, r, w)
        ins = self.E[q].dma_start(out=out, in_=in_)
        ins.then_inc(self.semh[key], 16)
        self.dma_i[q] = i + 1
        self.nins += 1
        tok = (key, 16 * (i // n + 1))
        self._mark(tok, r, w)
        return tok

    def collective(self, groups, ins_ap, outs_ap, r, w):
        if "cc" not in self.semh:
            self.semh["cc"] = self.es.enter_context(self.nc.semaphore("s_cc"))
            self.ncc = 0
        self._deps("pool", r, w)
        ins = self.nc.gpsimd.collective_compute("AllGather", ALU.bypass, replica_groups=groups,
                                                ins=[ins_ap], outs=[outs_ap])
        self.ncc += 1
        ins.then_inc(self.semh["cc"], 1)
        self.nins += 1
        self._mark(("cc", self.ncc), r, w)

    def barrier(self):
        toks = [(e, self.cnt[e]) for e in ("pe", "dve", "act", "pool") if self.cnt[e] > 0]
        for q, n in self.ndma.items():
            i = self.dma_i[q]
            for j in range(n):
                cj = (i - j + n - 1) // n if i > j else 0
                if cj > 0:
                    toks.append((("dma", q, j), 16 * cj))
        for eng in ("pe", "dve", "act", "pool", "sp"):
            for key, val in toks:
                if key != eng:
                    self._wait(eng, key, val)

    def mm(self, out, lhsT, rhs, start, stop, r, w, inc):
        return self.op("pe", lambda e: e.matmul(out, lhsT=lhsT, rhs=rhs, start=start, stop=stop,
                                                skip_group_check=True), r, w, inc=inc)

    def tr(self, out, in_, ident, r, w, inc):
        return self.op("pe", lambda e: e.transpose(out, in_, ident), r, w, inc=inc)

    def tt(self, eng, out, in0, in1, op, r, w):
        return self.op(eng, lambda e: e.tensor_tensor(out=out, in0=in0, in1=in1, op=op), r, w)

    def ts(self, eng, out, in0, s1, s2, op0, op1, r, w):
        if s2 is None:
            return self.op(eng, lambda e: e.tensor_scalar(out=out, in0=in0, scalar1=s1, scalar2=None, op0=op0), r, w)
        return self.op(eng, lambda e: e.tensor_scalar(out=out, in0=in0, scalar1=s1, scalar2=s2, op0=op0, op1=op1), r, w)

    def stt(self, eng, out, in0, scalar, in1, op0, op1, r, w):
        return self.op(eng, lambda e: e.scalar_tensor_tensor(out=out, in0=in0, scalar=scalar, in1=in1,
                                                             op0=op0, op1=op1), r, w)

    def act(self, out, in_, func, r, w, **kw):
        return self.op("act", lambda e: e.activation(out=out, in_=in_, func=func, **kw), r, w)

    def cp(self, eng, out, in_, r, w):
        if eng == "act":
            return self.op("act", lambda e: e.activation(out=out, in_=in_, func=AF.Copy), r, w)
        return self.op(eng, lambda e: e.tensor_copy(out=out, in_=in_), r, w)

    def memset(self, eng, ap, val, w):
        return self.op(eng, lambda e: e.memset(ap, val), (), w)


def bc(ap, shape):
    return ap.to_broadcast(list(shape))


def const_layout(T_):
    NT = T_ // 128
    off = {}
    c = 0
    for name, n in (("ident", 128), ("triu", 128), ("negmask", 128), ("strict", 128), ("ones", 128),
                    ("blk", 128), ("dq", 4), ("dk", 4), ("g128", 4), ("nhalf", 8)):
        off[name] = (c, n)
        c += n
    return off, c


def rope_layout(T_):
    NT = T_ // 128
    return {"cos": (0, NT * 32), "sin": (NT * 32, NT * 32), "nsin": (2 * NT * 32, NT * 32)}, 3 * NT * 32


PM_OFF = {"ptf": (0, 512), "ptr": (512, 512), "ptp": (1024, 512)}
PM_DEV = {"pt0": (0, 512), "ptr": (512, 512), "ptp": (1024, 512)}
NST = 830
GROUPS = [[0, 1], [2, 3], [4, 5], [6, 7]]


def host_consts(T_, pos0=0):
    off, n = const_layout(T_)
    roff, rn = rope_layout(T_)
    NT = T_ // 128
    C = np.zeros((128, n), np.float32)
    R = np.zeros((128, rn), np.float32)
    PM = np.zeros((128, 1536), np.float32)

    def put(name, arr):
        if name in off:
            o, m = off[name]
            C[:, o:o + m] = np.asarray(arr, np.float32).reshape(128, m)
        elif name in roff:
            o, m = roff[name]
            R[:, o:o + m] = np.asarray(arr, np.float32).reshape(128, m)
        else:
            o, m = PM_OFF[name]
            PM[:, o:o + m] = np.asarray(arr, np.float32).reshape(128, m)

    p = np.arange(128)
    put("ident", np.eye(128))
    put("triu", (p[:, None] <= p[None, :]))
    put("negmask", np.where(p[:, None] >= p[None, :], 0.0, 1e30))
    put("strict", (p[:, None] > p[None, :]))
    put("ones", np.ones((128, 128)))
    put("blk", (p[:, None] // 64 == p[None, :] // 64))
    inv = (1.0 / (np.float32(10000.0) ** (np.arange(0, 64, 2, dtype=np.float32) / np.float32(64)))).astype(np.float32)
    pos = np.arange(pos0, pos0 + T_).astype(np.float32)
    ang = (pos[:, None] * inv[None, :]).astype(np.float32)
    cos = np.cos(ang).astype(np.float32).reshape(NT, 128, 32).transpose(1, 0, 2)
    sin = np.sin(ang).astype(np.float32).reshape(NT, 128, 32).transpose(1, 0, 2)
    put("cos", cos)
    put("sin", sin)
    put("nsin", -sin)
    lg = np.log(1.0 - 2.0 ** (-5.0 - np.arange(4, dtype=np.float64)))
    put("dq", np.exp(lg[None, :] * (p[:, None] + 1.0)))
    put("dk", np.exp(-lg[None, :] * (p[:, None] + 1.0)) * 0.125)
    put("g128", np.broadcast_to(np.exp(lg * 128.0)[None, :], (128, 4)))
    put("nhalf", np.full((128, 8), -0.5))
    wins = (2, 4, 8, 16)
    ptf = np.zeros((128, 4, 128))
    ptr = np.zeros((128, 4, 128))
    ptp = np.zeros((128, 4, 128))
    for g, w in enumerate(wins):
        for t in range(128):
            for s in range(max(0, t - w + 1), t + 1):
                ptf[s, g, t] += 1.0 / min(t + 1, w)
                ptr[s, g, t] += 1.0 / w
            ptf[t, g, t] -= 1.0
            ptr[t, g, t] -= 1.0
            for srel in range(t - w + 1, 0):
                ptp[128 + srel, g, t] += 1.0 / w
    put("ptf", ptf)
    put("ptr", ptr)
    put("ptp", ptp)
    return C, R, PM


LP_A = (("gpre", 8), ("gpost", 1024), ("poolw", 512), ("poolsc", 4), ("lng", 256), ("lnb", 256),
        ("wsT", 512), ("bs", 4), ("convw", 24), ("alog", 4), ("dtb", 4), ("normg", 64))
LP_B = (("gpre", 8), ("gpost", 1024), ("fcw", 66), ("fcb", 22))


def lay(spec):
    off = {}
    c = 0
    for name, n in spec:
        off[name] = (c, n)
        c += n
    return off, c


def host_params(inp, l):
    offA, nA = lay(LP_A)
    offB, nB = lay(LP_B)
    A = np.zeros((128, nA), np.float32)
    B = np.zeros((128, nB), np.float32)

    def put(M, off, name, arr):
        o, m = off[name]
        M[:, o:o + m] = np.asarray(arr, np.float32).reshape(128, m)

    def rep(v):
        return np.broadcast_to(np.asarray(v, np.float32).reshape(1, -1), (128, np.asarray(v).size))

    put(A, offA, "gpre", inp["norm_pre_mix"][l].reshape(8, 128).T)
    put(A, offA, "gpost", rep(inp["norm_post_mix"][l]))
    pw = np.zeros((128, 4, 128), np.float32)
    for g in range(4):
        pw[:64, g, :64] = inp["pool_w"][l][g]
    put(A, offA, "poolw", pw)
    psc = np.zeros((128, 4), np.float32)
    psc[:64, :] = inp["pool_scale"][l].reshape(4, 64).T
    put(A, offA, "poolsc", psc)
    put(A, offA, "lng", rep(inp["sgu_ln_g"][l]))
    put(A, offA, "lnb", rep(inp["sgu_ln_b"][l]))
    put(A, offA, "wsT", inp["sgu_ws"][l].transpose(2, 0, 1))
    put(A, offA, "bs", inp["sgu_bs"][l].T)
    put(A, offA, "convw", inp["dn_conv_w"][l].reshape(4, 6, 128).transpose(2, 1, 0))
    put(A, offA, "alog", rep(inp["dn_a_log"][l]))
    put(A, offA, "dtb", rep(inp["dn_dt_bias"][l]))
    put(A, offA, "normg", rep(inp["dn_norm_g"][l]))
    put(B, offB, "gpre", inp["norm_pre_ffn"][l].reshape(8, 128).T)
    put(B, offB, "gpost", rep(inp["norm_post_ffn"][l]))
    put(B, offB, "fcw", inp["ffn_conv_w"][l].reshape(3, NFF, 128).transpose(2, 1, 0))
    put(B, offB, "fcb", inp["ffn_conv_b"][l].reshape(NFF, 128).T)
    return A, B


def build(T_, L, mixers=("pool", "ret", "sgu", "dn"), do_ffn=True, skew=False):
    NT = T_ // 128
    nc = bass.Bass("TRN2", target_bir_lowering=False)
    coff, ncst = const_layout(T_)
    offA, nA = lay(LP_A)
    offB, nB = lay(LP_B)

    x_in = nc.dram_tensor("x", [T_, D], F32, kind="ExternalInput").ap()
    w_in_d = nc.dram_tensor("w_in", [L, D, PIN], F32, kind="ExternalInput").ap()
    w_out_d = nc.dram_tensor("w_out", [L, D, D], F32, kind="ExternalInput").ap()
    w_up_d = nc.dram_tensor("w_up", [L, D, 2 * DFF], F32, kind="ExternalInput").ap()
    w_dn_d = nc.dram_tensor("w_down", [L, DFF, D], F32, kind="ExternalInput").ap()
    cst_d = nc.dram_tensor("cst", [128, ncst], F32, kind="ExternalInput").ap()
    roff, nrope = rope_layout(T_)
    rope_d = nc.dram_tensor("ropet", [128, nrope], F32, kind="ExternalInput").ap()
    pm_d = nc.dram_tensor("pmat", [128, 1536], F32, kind="ExternalInput").ap()
    lpa_d = nc.dram_tensor("lpa", [L, 128, nA], F32, kind="ExternalInput").ap()
    lpb_d = nc.dram_tensor("lpb", [L, 128, nB], F32, kind="ExternalInput").ap()
    flg_d = nc.dram_tensor("flags", [128, 8], F32, kind="ExternalInput").ap()
    st_out_t = nc.dram_tensor("st_out", [128, NST], F32)
    st_all_t = nc.dram_tensor("st_all", [256, NST], F32)
    st_out, st_all = st_out_t.ap(), st_all_t.ap()
    stb_out = T(None, Buf("st_out"))
    stb_all = T(None, Buf("st_all"))
    out_d = nc.dram_tensor("out", [T_, D], F32, kind="ExternalOutput").ap()
    xm_d = nc.dram_tensor("xm_scr", [T_, D], F32, kind="Internal").ap()
    xs_d = nc.dram_tensor("xs_scr", [T_, D], F32, kind="Internal").ap()

    class DT:
        def __init__(self, name, ap):
            self.ap = ap
            self.tiles = [T(None, Buf("%s%d" % (name, i))) for i in range(NT)]

    dx_in, dx_m, dx_s, dx_o = DT("xin", x_in), DT("xm", xm_d), DT("xs", xs_d), DT("xo", out_d)
    wdram = T(None, Buf("wdram"))

    with ExitStack() as es:
        es.enter_context(nc.allow_low_precision("bf16 matmul operands, fp32 accumulation"))
        k = K(nc, es)
        k.init_psum(6, 2)

        cst = k.sb("cst", [128, ncst], F32)
        k.dma(cst[:], cst_d, r=(), w=(cst,))

        def cs(name, a=None, b=None):
            o, n = coff[name]
            if a is None:
                return cst[:, o:o + n]
            return cst[:, o + a:o + b]

        flg = k.sb("flg", [128, 8], F32)
        k.dma(flg[:], flg_d, r=(), w=(flg,))
        ident = k.sb("ident", [128, 128], BF16)
        k.cp("dve", ident[:], cs("ident"), (cst,), (ident,))
        blk = k.sb("blk", [128, 128], BF16)
        k.cp("dve", blk[:], cs("blk"), (cst,), (blk,))

        SW = 1412
        stg = []
        stg_i = [0]
        cast_engs = ("dve", "act", "dve")

        def load_cast(dst_t, dst_ap, src_ap, ncols, nrows=128):
            i = stg_i[0]
            stg_i[0] += 1
            s = stg[i % len(stg)]
            k.dma(s[0:nrows, 0:ncols], src_ap, r=(wdram,), w=(s,))
            k.cp(cast_engs[i % 3], dst_ap, s[0:nrows, 0:ncols], (s,), (dst_t,))

        def rstd_from(ss, n, ncol, tmp, out, f=1.0):
            k.ts("pool", tmp[:, 0:ncol], ss[:, 0:ncol], 1.0 / (n * f * f), EPS / (f * f), ALU.mult, ALU.add, (ss,), (tmp,))
            k.tt("pool", out[:, 0:ncol], tmp[:, 0:ncol], cs("nhalf", 0, ncol), ALU.pow, (tmp, cst), (out,))

        def rstd_act(ss, n, ncol, tmp, out, f=1.0):
            k.act(tmp[:, 0:ncol], ss[:, 0:ncol], AF.Ln, (ss,), (tmp,), scale=1.0 / (n * f * f), bias=EPS / (f * f))
            k.act(out[:, 0:ncol], tmp[:, 0:ncol], AF.Exp, (tmp,), (out,), scale=-0.5)

        for l in range(L):
            src = dx_in if l == 0 else dx_s
            dst = dx_o if l == L - 1 else dx_s
            k.barrier()
            with ExitStack() as pa:
                k.work = k.ps_f[3:6]

                def sb(name, shape, dt=F32):
                    return k.sb("a_" + name, shape, dt, es=pa)

                stg[:] = [sb("stg%d" % i_, [128, 706], F32) for i_ in range(4)]
                w_in = sb("w_in", [128, 8, PIN], BF16)
                wo_pool = sb("wo_pool", [128, 4, D], BF16)
                wo = sb("wo", [128, 6, D], BF16)
                lp = sb("lp", [128, nA], F32)
                k.dma(lp[:], lpa_d[l], r=(), w=(lp,))
                ropet = sb("ropet", [128, nrope], F32)
                k.dma(ropet[:], rope_d, r=(), w=(ropet,))

                def rp(name, a, b):
                    o, n = roff[name]
                    return ropet[:, o + a:o + b]

                ptf = sb("ptf", [128, 512], BF16)
                ptr = sb("ptr", [128, 512], BF16)
                ptp = sb("ptp", [128, 512], BF16)
                for nm_, t_ in (("pt0", ptf), ("ptr", ptr), ("ptp", ptp)):
                    o_, n_ = PM_DEV[nm_]
                    load_cast(t_, t_[:, :], pm_d[:, o_:o_ + n_], n_)

                def P(name, a=None, b=None):
                    o, n = offA[name]
                    if a is None:
                        return lp[:, o:o + n]
                    return lp[:, o + a:o + b]


                poolw = sb("poolw", [128, 4, 128], BF16)
                k.cp("dve", poolw[:].rearrange("p g d -> p (g d)"), P("poolw"), (lp,), (poolw,))
                wmT = sb("wmT", [128, 4, 128], BF16)
                k.tt("dve", wmT[:], P("wsT").rearrange("p (g t) -> p g t", g=4),
                     bc(cs("triu").unsqueeze(1), [128, 4, 128]), ALU.mult, (lp, cst), (wmT,))
                nexpA = sb("nexpA", [128, 4])
                k.act(nexpA[:], P("alog"), AF.Exp, (lp,), (nexpA,))
                k.ts("dve", nexpA[:], nexpA[:], -1.0, None, ALU.mult, None, (nexpA,), (nexpA,))

                xt = [sb("xt%d" % i, [128, D]) for i in range(2)]
                xmo0 = sb("xmo0", [128, D])
                xmo = [xmo0, xmo0]
                junk = sb("junk", [128, D], BF16)
                xn = sb("xn", [128, D], BF16)
                hT = sb("hT", [128, 8, 128], BF16)
                ss = sb("ss", [128, 4])
                tmp4 = sb("tmp4", [128, 4])
                rstd = sb("rstd", [128, 4])
                a_bf = [sb("a_bf%d" % i, [128, 256], BF16) for i in range(2)]
                dT_bf = sb("dT_bf", [128, 4, 128], BF16)
                yaT = sb("yaT", [128, 4, 128], BF16)
                k.memset("pool", dT_bf[:], 0.0, (dT_bf,))
                k.memset("pool", yaT[:], 0.0, (yaT,))
                yT = sb("yT", [128, 6, 128], BF16)
                k.memset("pool", yT[:], 0.0, (yT,))
                gu_2 = [sb("gu%d" % i_, [128, 256]) for i_ in range(2)]
                gv_2 = [sb("gv%d" % i_, [128, 256]) for i_ in range(2)]
                vln = sb("vln", [128, 256])
                vln2 = vln
                vln_bf = sb("vln_bf", [128, 256], BF16)
                s_sum = sb("s_sum", [128, 4])
                s_nm = sb("s_nm", [128, 4])
                s_ss = sb("s_ss", [128, 4])
                s_tmp = sb("s_tmp", [128, 4])
                s_rs = sb("s_rs", [128, 4])
                yc_t = sb("yc_t", [128, 256])
                yc_bf = sb("yc_bf", [128, 256], BF16)
                r_qd_2 = [sb("r_qd%d" % i_, [128, 256]) for i_ in range(2)]
                r_t1 = sb("r_t1", [128, 256])
                r_t2 = sb("r_t2", [128, 256])
                r_kd_2 = [sb("r_kd%d" % i_, [128, 256]) for i_ in range(2)]
                r_u1 = r_t1
                r_u2 = r_t2
                r_q_bf = sb("r_q_bf", [128, 256], BF16)
                r_kz = sb("r_kz", [128, 4, 128], BF16)
                r_v_bf_2 = [sb("r_v_bf%d" % i_, [128, 256], BF16) for i_ in range(2)]
                r_sg_2 = [sb("r_sg%d" % i_, [128, 256]) for i_ in range(2)]
                r_qT = sb("r_qT", [128, 2, 128], BF16)
                r_kTz = sb("r_kTz", [128, 4, 128], BF16)
                r_scm = sb("r_scm", [128, 4, 128], BF16)
                r_S = sb("r_S", [128, 2, 128])
                r_S_bf = sb("r_S_bf", [128, 2, 128], BF16)
                r_St = sb("r_St", [128, 2, 128])
                r_sso = sb("r_sso", [128, 4])
                r_tm = sb("r_tm", [128, 4])
                r_rs = sb("r_rs", [128, 4])
                r_y = sb("r_y", [128, 256])
                r_y_bf = sb("r_y_bf", [128, 256], BF16)
                k.memset("pool", r_kz[:], 0.0, (r_kz,))
                k.memset("pool", r_kTz[:], 0.0, (r_kTz,))
                k.memset("pool", r_S[:], 0.0, (r_S,))
                k.memset("pool", r_S_bf[:], 0.0, (r_S_bf,))
                for t_ in a_bf:
                    k.memset("pool", t_[:], 0.0, (t_,))
                d_xc = sb("d_xc", [128, 6, 131])
                d_acc = sb("d_acc", [128, 6, 128])
                d_tmp = sb("d_tmp", [128, 6, 128])
                d_qkv = d_acc
                d_sq = sb("d_sq", [128, 4, 128], BF16)
                d_qkn = sb("d_qkn", [128, 4, 128], BF16)
                d_qz = sb("d_qz", [128, 4, 128], BF16)
                d_kz = sb("d_kz", [128, 4, 128], BF16)
                d_vT = sb("d_vT", [128, 2, 128], BF16)
                d_sc = sb("d_sc", [128, 32])
                d_ab_2 = [sb("d_ab%d" % i_, [128, 8]) for i_ in range(2)]
                d_g = sb("d_g", [128, 4])
                d_hb = sb("d_hb", [128, 4])
                d_Ug = sb("d_Ug", [128, 4, 128])
                d_rn = d_Ug
                d_dec = sb("d_dec", [128, 8])
                d_arg = sb("d_arg", [128, 4, 128])
                d_E = d_arg
                d_Eb = sb("d_Eb", [128, 4, 128])
                d_eR = sb("d_eR", [128, 4, 128])
                d_qd_2 = [sb("d_qd%d" % i_, [128, 2, 128], BF16) for i_ in range(2)]
                d_L0_2 = [sb("d_L0%d" % i_, [128, 4, 128], BF16) for i_ in range(2)]
                d_Z0_2 = [sb("d_Z0%d" % i_, [128, 4, 128], BF16) for i_ in range(2)]
                d_edl_2 = [sb("d_edl%d" % i_, [128, 4]) for i_ in range(2)]
                d_Pt = sb("d_Pt", [128, 4, 128], BF16)
                d_Rp = sb("d_Rp", [128, 4, 128], BF16)
                d_Dl = sb("d_Dl", [128, 4, 128], BF16)
                d_L = [sb("d_L%d" % i, [128, 4, 128], BF16) for i in range(2)]
                d_Z = [sb("d_Z%d" % i, [128, 4, 128], BF16) for i in range(2)]
                d_P0_2 = [sb("d_P0%d" % i_, [128, 4, 128], BF16) for i_ in range(2)]
                d_Pb = [sb("d_Pb%d" % i, [128, 4, 128], BF16) for i in range(2)]
                d_at = sb("d_at", [128, 4, 128], BF16)
                d_atT_2 = [sb("d_atT%d" % i_, [128, 4, 128], BF16) for i_ in range(2)]
                d_kbz_2 = [sb("d_kbz%d" % i_, [128, 4, 128], BF16) for i_ in range(2)]
                d_ktz_2 = [sb("d_ktz%d" % i_, [128, 4, 128], BF16) for i_ in range(2)]
                d_vb_2 = [sb("d_vb%d" % i_, [128, 256], BF16) for i_ in range(2)]
                d_nWT = sb("d_nWT", [128, 2, 128], BF16)
                d_vn = sb("d_vn", [128, 256], BF16)
                d_S = sb("d_S", [128, 2, 128])
                d_St = sb("d_St", [128, 2, 128])
                d_S_bf = sb("d_S_bf", [128, 2, 128], BF16)
                d_sz_2 = [sb("d_sz%d" % i_, [128, 256]) for i_ in range(2)]
                d_szg_2 = [sb("d_szg%d" % i_, [128, 256]) for i_ in range(2)]
                d_sso = sb("d_sso", [128, 4])
                d_tm = sb("d_tm", [128, 4])
                d_rs = sb("d_rs", [128, 4])
                d_y = sb("d_y", [128, 256])
                d_y_bf = sb("d_y_bf", [128, 256], BF16)
                for t_ in (d_qz, d_kz, d_kbz_2[0], d_kbz_2[1], d_ktz_2[0], d_ktz_2[1]):
                    k.memset("pool", t_[:], 0.0, (t_,))
                k.memset("pool", d_xc[:], 0.0, (d_xc,))
                k.memset("pool", d_S[:], 0.0, (d_S,))
                k.memset("pool", d_S_bf[:], 0.0, (d_S_bf,))
                if skew and l > 0:
                    hfl = flg[:, 0:1]
                    k.dma(r_St[:].rearrange("p j x -> p (j x)"), st_all[0:128, 0:256], r=(stb_all,), w=(r_St,))
                    k.ts("dve", r_S[:], r_St[:], hfl, None, ALU.mult, None, (r_St, flg), (r_S,))
                    k.cp("dve", r_S_bf[:], r_S[:], (r_S,), (r_S_bf,))
                    k.dma(d_St[:].rearrange("p j x -> p (j x)"), st_all[0:128, 256:512], r=(stb_all,), w=(d_St,))
                    k.ts("dve", d_S[:], d_St[:], hfl, None, ALU.mult, None, (d_St, flg), (d_S,))
                    k.cp("dve", d_S_bf[:], d_S[:], (d_S,), (d_S_bf,))
                    k.dma(r_y[:], st_all[0:128, 512:768], r=(stb_all,), w=(r_y,))
                    k.ts("dve", a_bf[1][:], r_y[:], hfl, None, ALU.mult, None, (r_y, flg), (a_bf[1],))
                    k.dma(d_y[:, 0:18], st_all[0:128, 768:786], r=(stb_all,), w=(d_y,))
                    k.ts("dve", d_xc[:, :, 0:3], d_y[:, 0:18].rearrange("p (c j) -> p c j", c=6), hfl, None, ALU.mult, None,
                         (d_y, flg), (d_xc,))
                o_ss = sb("o_ss", [128, 4])
                o_tm = sb("o_tm", [128, 4])
                o_rs = sb("o_rs", [128, 4])
                o_t = [sb("o_t%d" % i, [128, 512]) for i in range(2)]

                v4 = lambda ap: ap.rearrange("p (h x) -> p h x", h=4)
                if l == 0:
                    print("[build] phase A sbuf bytes remaining/partition:", nc.sbuf_bytes_remaining)

                def front(i):
                    par = i % 2
                    x_t = xt[par]
                    gu, gv, r_qd, r_kd, r_sg, d_sz = gu_2[par], gv_2[par], r_qd_2[par], r_kd_2[par], r_sg_2[par], d_sz_2[par]
                    r_v_bf, d_ab = r_v_bf_2[par], d_ab_2[par]
                    k.act(junk[:], x_t[:], AF.Square, (x_t,), (junk, ss), accum_out=ss[:, 0:1])
                    rstd_act(ss, D, 1, tmp4, rstd)
                    k.ts("dve", xn[:], x_t[:], rstd[:, 0:1], None, ALU.mult, None, (x_t, rstd), (xn,))
                    pT = k.psb()
                    for kc in range(8):
                        k.tr(pT[:, kc * 128:(kc + 1) * 128], xn[:, kc * 128:(kc + 1) * 128], ident[:],
                             (xn, ident), (pT,), inc=(kc == 7))
                    k.tt("dve", hT[:], pT[:, :].rearrange("p (c t) -> p c t", c=8),
                         bc(P("gpre").unsqueeze(2), [128, 8, 128]), ALU.mult, (pT, lp), (hT,))
                    yield

                    def proj(c0, c1):
                        ps = k.psj()
                        for kc in range(8):
                            k.mm(ps[:, 0:c1 - c0], hT[:, kc, :], w_in[:, kc, c0:c1], kc == 0, kc == 7,
                                 (hT, w_in), (ps,), inc=(kc == 7))
                        return ps

                    ab = a_bf[par]
                    G0 = proj(0, 512)
                    k.cp("act", ab[:], G0[:, 0:256], (G0,), (ab,))
                    k.tt("dve", v4(r_qd[:]), v4(G0[:, 256:512]), bc(cs("dq").unsqueeze(2), [128, 4, 64]),
                         ALU.mult, (G0, cst), (r_qd,))
                    yield
                    G1 = proj(512, 1024)
                    k.tt("dve", v4(r_kd[:]), v4(G1[:, 0:256]), bc(cs("dk").unsqueeze(2), [128, 4, 64]),
                         ALU.mult, (G1, cst), (r_kd,))
                    k.cp("act", r_v_bf[:], G1[:, 256:512], (G1,), (r_v_bf,))
                    yield
                    G3 = proj(1536, 1792)
                    k.act(gv[:], G3[:, 0:256], AF.Gelu_apprx_tanh, (G3,), (gv, s_sum), accum_out=s_sum[:, par:par + 1])
                    G2 = proj(1024, 1536)
                    k.act(gu[:], G2[:, 256:512], AF.Gelu_apprx_tanh, (G2,), (gu,))
                    k.act(r_sg[:], G2[:, 0:256], AF.Tanh, (G2,), (r_sg,), scale=0.5)
                    k.stt("dve", r_sg[:], r_sg[:], 1.0, G2[:, 0:256], ALU.add, ALU.mult, (r_sg, G2), (r_sg,))
                    yield
                    G4 = proj(2560, 2824)
                    k.act(d_sz[:], G4[:, 0:256], AF.Tanh, (G4,), (d_sz,), scale=0.5)
                    k.stt("dve", d_sz[:], d_sz[:], 1.0, G4[:, 0:256], ALU.add, ALU.mult, (d_sz, G4), (d_sz,))
                    k.tt("dve", d_ab[:, 0:4], G4[:, 260:264], P("dtb"), ALU.add, (G4, lp), (d_ab,))
                    k.cp("dve", d_ab[:, 4:8], G4[:, 256:260], (G4,), (d_ab,))
                    yield
                    if "dn" in mixers:
                        psQ = [k.ps(), k.ps()]
                        for fc in range(6):
                            pq = psQ[fc // 4]
                            for kc in range(8):
                                k.mm(pq[:, (fc % 4) * 128:(fc % 4 + 1) * 128],
                                     w_in[:, kc, 1792 + fc * 128:1792 + (fc + 1) * 128], hT[:, kc, :], kc == 0, kc == 7,
                                     (w_in, hT), (pq,), inc=(kc == 7 and fc in (3, 5)))
                        k.cp("act", d_xc[:, 0:4, 3:131], psQ[0][:, :].rearrange("p (c t) -> p c t", c=4), (psQ[0],), (d_xc,))
                        k.cp("act", d_xc[:, 4:6, 3:131], psQ[1][:, 0:256].rearrange("p (c t) -> p c t", c=2), (psQ[1],), (d_xc,))

                def run_all(gens):
                    gens = list(gens)
                    while gens:
                        for g_ in list(gens):
                            try:
                                next(g_)
                            except StopIteration:
                                gens.remove(g_)

                k.dma(xt[0][:], src.ap[0:128, :], r=(src.tiles[0],), w=(xt[0],))
                if NT > 1:
                    k.dma(xt[1][:], src.ap[128:256, :], r=(src.tiles[1],), w=(xt[1],))
                def chain(*gs):
                    for g_ in gs:
                        yield from g_

                g0 = front(0)
                next(g0)
                for kc in range(8):
                    for hf in range(4):
                        load_cast(w_in, w_in[:, kc, hf * 706:(hf + 1) * 706],
                                  w_in_d[l, kc * 128:(kc + 1) * 128, hf * 706:(hf + 1) * 706], 706)
                k.memset("pool", wo_pool[:], 0.0, (wo_pool,))
                for g in range(4):
                    for hf in range(2):
                        load_cast(wo_pool, wo_pool[0:64, g, hf * 512:(hf + 1) * 512],
                                  w_out_d[l, g * 64:(g + 1) * 64, hf * 512:(hf + 1) * 512], 512, nrows=64)
                for c in range(6):
                    for hf in range(2):
                        load_cast(wo, wo[:, c, hf * 512:(hf + 1) * 512],
                                  w_out_d[l, 256 + c * 128:256 + (c + 1) * 128, hf * 512:(hf + 1) * 512], 512)
                run_all([g0])
                for i in range(NT):
                    par = i % 2
                    x_t = xt[par]
                    gu, gv, r_qd, r_kd, r_sg, d_sz = gu_2[par], gv_2[par], r_qd_2[par], r_kd_2[par], r_sg_2[par], d_sz_2[par]
                    r_v_bf, d_ab = r_v_bf_2[par], d_ab_2[par]
                    ab = a_bf[par]
                    ap_ = a_bf[(i + 1) % 2]

                    def m_pool():
                        psD = k.ps()
                        ptm = ptf if i == 0 else ptr
                        for g in range(4):
                            k.mm(psD[0:64, g * 128:(g + 1) * 128], ab[:, g * 64:(g + 1) * 64],
                                 ptm[:, g * 128:(g + 1) * 128], True, False, (ab, ptm), (psD,),
                                 inc=False)
                            if True:
                                k.mm(psD[0:64, g * 128:(g + 1) * 128], ap_[:, g * 64:(g + 1) * 64],
                                     ptp[:, g * 128:(g + 1) * 128], False, True, (ap_, ptp), (psD,), inc=(g == 3))
                        k.cp("act", dT_bf[0:64, :, :].rearrange("p g t -> p (g t)"), psD[0:64, :], (psD,), (dT_bf,))
                        yield
                        psYa = k.ps()
                        for g in range(4):
                            k.mm(psYa[0:64, g * 128:(g + 1) * 128], poolw[:, g, 0:64], dT_bf[:, g, :], True, True,
                                 (poolw, dT_bf), (psYa,), inc=(g == 3))
                        k.tt("dve", yaT[0:64, :, :], psYa[0:64, :].rearrange("p (g t) -> p g t", g=4),
                             bc(P("poolsc")[0:64, :].unsqueeze(2), [64, 4, 128]), ALU.mult, (psYa, lp), (yaT,))

                    def m_sgu():
                        k.ts("dve", s_nm[:, 0:1], s_sum[:, par:par + 1], -1.0 / 256, None, ALU.mult, None, (s_sum,), (s_nm,))
                        k.act(junk[:, 0:256], gv[:], AF.Square, (gv, s_nm), (junk, s_ss), bias=s_nm[:, 0:1],
                              accum_out=s_ss[:, 0:1])
                        rstd_from(s_ss, 256, 1, s_tmp, s_rs)
                        k.ts("dve", vln[:], gv[:], s_nm[:, 0:1], s_rs[:, 0:1], ALU.add, ALU.mult, (gv, s_nm, s_rs), (vln,))
                        k.tt("dve", vln2[:], vln[:], P("lng"), ALU.mult, (vln, lp), (vln2,))
                        k.tt("dve", vln_bf[:], vln2[:], P("lnb"), ALU.add, (vln2, lp), (vln_bf,))
                        yield
                        psS = k.ps()
                        for g in range(4):
                            k.mm(psS[:, g * 64:(g + 1) * 64], wmT[:, g, :], vln_bf[:, g * 64:(g + 1) * 64], True, True,
                                 (wmT, vln_bf), (psS,), inc=(g == 3))
                        k.tt("dve", v4(yc_t[:]), v4(psS[:, 0:256]), bc(P("bs").unsqueeze(2), [128, 4, 64]), ALU.add,
                             (psS, lp), (yc_t,))
                        k.tt("dve", yc_bf[:], yc_t[:], gu[:], ALU.mult, (yc_t, gu), (yc_bf,))
                        yield
                        pT2 = k.psb()
                        for c in range(2):
                            k.tr(pT2[:, c * 128:(c + 1) * 128], yc_bf[:, c * 128:(c + 1) * 128], ident[:],
                                 (yc_bf, ident), (pT2,), inc=(c == 1))
                        k.cp("act", yT[:, 2:4, :].rearrange("p c t -> p (c t)"), pT2[:, 0:256], (pT2,), (yT,))

                    def m_ret():
                        cosb = bc(rp("cos", i * 32, (i + 1) * 32).unsqueeze(1).unsqueeze(1), [128, 4, 2, 32])
                        sinb = bc(rp("sin", i * 32, (i + 1) * 32).unsqueeze(1), [128, 4, 32])
                        nsinb = bc(rp("nsin", i * 32, (i + 1) * 32).unsqueeze(1), [128, 4, 32])
                        v42 = lambda ap: ap.rearrange("p (h two x) -> p h two x", h=4, two=2)

                        def rope(xd, t1, t2, eng2):
                            k.tt(eng2, v42(t1[:]), v42(xd[:]), cosb, ALU.mult, (xd, ropet), (t1,))
                            k.tt(eng2, v42(t2[:])[:, :, 0, :], v42(xd[:])[:, :, 1, :], nsinb, ALU.mult, (xd, ropet), (t2,))
                            k.tt(eng2, v42(t2[:])[:, :, 1, :], v42(xd[:])[:, :, 0, :], sinb, ALU.mult, (xd, ropet), (t2,))

                        rope(r_qd, r_t1, r_t2, "dve")
                        k.tt("dve", r_q_bf[:], r_t1[:], r_t2[:], ALU.add, (r_t1, r_t2), (r_q_bf,))
                        yield
                        rope(r_kd, r_u1, r_u2, "dve")
                        for hh in range(2):
                            k.tt("dve", r_kz[:, hh::2, hh * 64:(hh + 1) * 64],
                                 v4(r_u1[:])[:, hh::2, :], v4(r_u2[:])[:, hh::2, :], ALU.add, (r_u1, r_u2), (r_kz,))
                        pT3 = k.psb()
                        for c in range(2):
                            k.tr(pT3[:, c * 128:(c + 1) * 128], r_q_bf[:, c * 128:(c + 1) * 128], ident[:],
                                 (r_q_bf, ident), (pT3,), inc=False)
                        for h in range(4):
                            k.tr(pT3[:, (2 + h) * 128:(3 + h) * 128], r_kz[:, h, :], ident[:],
                                 (r_kz, ident), (pT3,), inc=(h == 3))
                        k.cp("act", r_qT[:].rearrange("p c t -> p (c t)"), pT3[:, 0:256], (pT3,), (r_qT,))
                        k.cp("dve", r_kTz[:].rearrange("p c t -> p (c t)"), pT3[:, 256:768], (pT3,), (r_kTz,))
                        yield
                        psSc = k.ps()
                        for h in range(4):
                            k.mm(psSc[:, h * 128:(h + 1) * 128], r_kTz[:, h, :], r_qT[:, h // 2, :], True, True,
                                 (r_kTz, r_qT), (psSc,), inc=(h == 3))
                        k.tt("dve", r_scm[:], psSc[:, :].rearrange("p (h s) -> p h s", h=4),
                             bc(cs("triu").unsqueeze(1), [128, 4, 128]), ALU.mult, (psSc, cst), (r_scm,))
                        yield
                        psO = k.ps()
                        for j in range(2):
                            k.mm(psO[:, j * 128:(j + 1) * 128], r_qT[:, j, :], r_S_bf[:, j, :], True, False,
                                 (r_qT, r_S_bf), (psO,), inc=False)
                            for hh in range(2):
                                h = 2 * j + hh
                                k.mm(psO[:, h * 64:(h + 1) * 64], r_scm[:, h, :], r_v_bf[:, h * 64:(h + 1) * 64],
                                     False, hh == 1, (r_scm, r_v_bf), (psO,), inc=(h == 3))
                        psKV = k.ps()
                        for h in range(4):
                            k.mm(psKV[:, h * 64:(h + 1) * 64], r_kz[:, h, :], r_v_bf[:, h * 64:(h + 1) * 64], True, True,
                                 (r_kz, r_v_bf), (psKV,), inc=(h == 3))
                        k.tt("dve", r_St[:].rearrange("p j x -> p (j x)"), psKV[:, 0:256],
                             r_S[:].rearrange("p j x -> p (j x)"), ALU.add, (psKV, r_S), (r_St,))
                        k.tt("dve", r_S[:].rearrange("p j (hh e) -> p (j hh) e", hh=2),
                             r_St[:].rearrange("p j (hh e) -> p (j hh) e", hh=2),
                             bc(cs("g128").unsqueeze(2), [128, 4, 64]), ALU.mult, (r_St, cst), (r_S,))
                        k.cp("act", r_S_bf[:], r_S[:], (r_S,), (r_S_bf,))
                        for h in range(4):
                            k.act(junk[:, h * 64:(h + 1) * 64], psO[:, h * 64:(h + 1) * 64], AF.Square, (psO,),
                                  (junk, r_sso), accum_out=r_sso[:, h:h + 1])
                        rstd_from(r_sso, 64, 4, r_tm, r_rs, f=0.5)
                        k.tt("dve", v4(r_y[:]), v4(psO[:, 0:256]), bc(r_rs[:, 0:4].unsqueeze(2), [128, 4, 64]), ALU.mult,
                             (psO, r_rs), (r_y,))
                        yield
                        k.tt("dve", r_y_bf[:], r_y[:], r_sg[:], ALU.mult, (r_y, r_sg), (r_y_bf,))
                        pT4 = k.psb()
                        for c in range(2):
                            k.tr(pT4[:, c * 128:(c + 1) * 128], r_y_bf[:, c * 128:(c + 1) * 128], ident[:],
                                 (r_y_bf, ident), (pT4,), inc=(c == 1))
                        k.cp("act", yT[:, 0:2, :].rearrange("p c t -> p (c t)"), pT4[:, 0:256], (pT4,), (yT,))

                    def dn_head(pp):
                        d_sz, d_ab = d_sz_2[pp], d_ab_2[pp]
                        L0, Z0, d_P0, d_atT = d_L0_2[pp], d_Z0_2[pp], d_P0_2[pp], d_atT_2[pp]
                        d_kbz, d_ktz, d_vb, d_qd, d_szg, d_edl = d_kbz_2[pp], d_ktz_2[pp], d_vb_2[pp], d_qd_2[pp], d_szg_2[pp], d_edl_2[pp]
                        cw = P("convw").rearrange("p (c j) -> p c j", c=6)
                        k.tt("dve", d_acc[:], d_xc[:, :, 0:128], bc(cw[:, :, 0:1], [128, 6, 128]), ALU.mult, (d_xc, lp), (d_acc,))
                        for j in range(1, 4):
                            k.tt("dve", d_tmp[:], d_xc[:, :, j:j + 128], bc(cw[:, :, j:j + 1], [128, 6, 128]), ALU.mult,
                                 (d_xc, lp), (d_tmp,))
                            k.tt("dve", d_acc[:], d_acc[:], d_tmp[:], ALU.add, (d_acc, d_tmp), (d_acc,))
                        k.cp("pool", d_xc[:, :, 0:3], d_xc[:, :, 128:131], (d_xc,), (d_xc,))
                        yield
                        k.act(d_tmp[:], d_acc[:], AF.Tanh, (d_acc,), (d_tmp,), scale=0.5)
                        k.stt("dve", d_qkv[:], d_tmp[:], 1.0, d_acc[:], ALU.add, ALU.mult, (d_tmp, d_acc), (d_qkv,))
                        k.tt("dve", v4(d_szg[:]), v4(d_sz[:]), bc(P("normg").unsqueeze(1), [128, 4, 64]),
                             ALU.mult, (d_sz, lp), (d_szg,))
                        k.act(d_sc[:, 12:16], d_ab[:, 4:8], AF.Exp, (d_ab,), (d_sc,), scale=-1.0)
                        k.act(d_sc[:, 4:8], d_ab[:, 0:4], AF.Exp, (d_ab,), (d_sc,))
                        k.act(d_sc[:, 8:12], d_sc[:, 4:8], AF.Ln, (d_sc,), (d_sc,), bias=1.0)
                        k.tt("dve", d_g[:], d_sc[:, 8:12], nexpA[:], ALU.mult, (d_sc, nexpA), (d_g,))
                        k.ts("dve", d_sc[:, 12:16], d_sc[:, 12:16], 1.0, None, ALU.add, None, (d_sc,), (d_sc,))
                        k.op("dve", lambda e: e.reciprocal(out=d_sc[:, 16:20], in_=d_sc[:, 12:16]), (d_sc,), (d_sc,))
                        k.ts("dve", d_hb[:, 0:4], d_sc[:, 16:20], 0.5, None, ALU.mult, None, (d_sc,), (d_hb,))
                        yield
                        for h in range(4):
                            k.ts("dve", d_Ug[:, h, :], cs("triu"), d_g[:, h:h + 1], None, ALU.mult, None, (cst, d_g), (d_Ug,))
                        psC = k.ps()
                        k.mm(psC[:, 0:4], cs("triu"), d_g[:, 0:4], True, True, (cst, d_g), (psC,), inc=False)
                        k.mm(psC[:, 4:8], cs("ones"), d_g[:, 0:4], True, True, (cst, d_g), (psC,), inc=True)
                        psR = k.ps()
                        k.mm(psR[:, :], cs("ones"), d_Ug[:].rearrange("p h x -> p (h x)"), True, True, (cst, d_Ug), (psR,), inc=True)
                        k.cp("dve", d_dec[:, 0:8], psC[:, 0:8], (psC,), (d_dec,))
                        k.act(d_sc[:, 24:28], d_dec[:, 0:4], AF.Exp, (d_dec,), (d_sc,))
                        k.tt("dve", d_sc[:, 0:4], d_dec[:, 4:8], d_dec[:, 0:4], ALU.subtract, (d_dec,), (d_sc,))
                        k.act(d_sc[:, 28:32], d_sc[:, 0:4], AF.Exp, (d_sc,), (d_sc,))
                        k.act(d_edl[:, 0:4], d_dec[:, 4:8], AF.Exp, (d_dec,), (d_edl,))
                        k.stt("dve", d_sc[:, 20:24], d_sc[:, 16:20], -1.0, d_sc[:, 24:28], ALU.mult, ALU.mult, (d_sc,), (d_sc,))
                        k.tt("dve", d_arg[:], psR[:, :].rearrange("p (h x) -> p h x", h=4),
                             bc(d_dec[:, 0:4].unsqueeze(2), [128, 4, 128]), ALU.subtract, (psR, d_dec), (d_arg,))
                        k.tt("dve", d_arg[:], d_arg[:], bc(cs("negmask").unsqueeze(1), [128, 4, 128]), ALU.add, (d_arg, cst), (d_arg,))
                        k.act(d_E[:], d_arg[:], AF.Exp, (d_arg,), (d_E,), scale=-1.0)
                        k.act(d_eR[:].rearrange("p h x -> p (h x)"), psR[:, :], AF.Exp, (psR,), (d_eR,))
                        yield
                        k.tt("dve", d_Eb[:], d_E[:], bc(cs("strict").unsqueeze(1), [128, 4, 128]), ALU.mult, (d_E, cst), (d_Eb,))
                        k.tt("dve", d_Eb[:], d_Eb[:], bc(d_sc[:, 16:20].unsqueeze(2), [128, 4, 128]), ALU.mult, (d_Eb, d_sc), (d_Eb,))
                        k.act(d_sq[:], d_qkv[:, 0:4, :], AF.Square, (d_qkv,), (d_sq,))
                        psN = k.ps()
                        k.mm(psN[:, :], blk[:], d_sq[:].rearrange("p c t -> p (c t)"), True, True, (blk, d_sq), (psN,), inc=True)
                        k.act(d_rn[:, 0:2, :].rearrange("p c t -> p (c t)"), psN[:, 0:256], AF.Ln, (psN,), (d_rn,),
                              scale=64.0, bias=256.0 * EPS)
                        k.act(d_rn[:, 2:4, :].rearrange("p c t -> p (c t)"), psN[:, 256:512], AF.Ln, (psN,), (d_rn,),
                              scale=1.0, bias=4.0 * EPS)
                        k.act(d_rn[:], d_rn[:], AF.Exp, (d_rn,), (d_rn,), scale=-0.5)
                        k.tt("dve", d_qkn[:], d_qkv[:, 0:4, :], d_rn[:], ALU.mult, (d_qkv, d_rn), (d_qkn,))
                        k.cp("act", d_vT[:], d_qkv[:, 4:6, :], (d_qkv,), (d_vT,))
                        yield
                        for hh in range(2):
                            sl = slice(hh * 64, (hh + 1) * 64)
                            k.cp("act", d_qz[sl, hh::2, :], d_qkn[sl, 0:2, :], (d_qkn,), (d_qz,))
                            k.cp("dve", d_kz[sl, hh::2, :], d_qkn[sl, 2:4, :], (d_qkn,), (d_kz,))
                            k.tt("dve", d_qd[sl, :, :], d_qkn[sl, 0:2, :], d_eR[sl, hh::2, :], ALU.mult, (d_qkn, d_eR), (d_qd,))
                        psKK = k.ps()
                        psQK = k.ps()
                        for h in range(4):
                            k.mm(psKK[:, h * 128:(h + 1) * 128], d_kz[:, h, :], d_qkn[:, 2 + h // 2, :], True, True,
                                 (d_kz, d_qkn), (psKK,), inc=(h == 3))
                        for h in range(4):
                            k.mm(psQK[:, h * 128:(h + 1) * 128], d_qz[:, h, :], d_qkn[:, 2 + h // 2, :], True, True,
                                 (d_qz, d_qkn), (psQK,), inc=(h == 3))
                        k.tt("dve", L0[:].rearrange("p h x -> p (h x)"), psKK[:, :], d_Eb[:].rearrange("p h x -> p (h x)"),
                             ALU.mult, (psKK, d_Eb), (L0,))
                        k.tt("dve", d_at[:].rearrange("p h x -> p (h x)"), psQK[:, :], d_E[:].rearrange("p h x -> p (h x)"),
                             ALU.mult, (psQK, d_E), (d_at,))
                        yield
                        pT5 = k.psb()
                        for h in range(4):
                            k.tr(pT5[:, h * 128:(h + 1) * 128], L0[:, h, :], ident[:], (L0, ident), (pT5,), inc=False)
                        for h in range(4):
                            k.tr(pT5[:, (4 + h) * 128:(5 + h) * 128], d_at[:, h, :], ident[:], (d_at, ident), (pT5,), inc=(h == 3))
                        k.cp("act", Z0[:].rearrange("p h x -> p (h x)"), pT5[:, 0:512], (pT5,), (Z0,))
                        k.cp("dve", d_atT[:].rearrange("p h x -> p (h x)"), pT5[:, 512:1024], (pT5,), (d_atT,))
                        yield
                        pT6 = k.psb()
                        for c in range(2):
                            k.tr(pT6[:, c * 128:(c + 1) * 128], d_qkn[:, 2 + c, :], ident[:], (d_qkn, ident), (pT6,), inc=False)
                        for c in range(2):
                            k.tr(pT6[:, (2 + c) * 128:(3 + c) * 128], d_vT[:, c, :], ident[:], (d_vT, ident), (pT6,), inc=(c == 1))
                        ktok = pT6[:, 0:256].rearrange("p (h x) -> p h x", h=4)
                        for hh in range(2):
                            k.tt("dve", d_kbz[:, hh::2, hh * 64:(hh + 1) * 64], ktok[:, hh::2, :],
                                 bc(d_sc[:, 20 + hh:24:2].unsqueeze(2), [128, 2, 64]), ALU.mult, (pT6, d_sc), (d_kbz,))
                            k.tt("dve", d_ktz[:, hh::2, hh * 64:(hh + 1) * 64], ktok[:, hh::2, :],
                                 bc(d_sc[:, 28 + hh:32:2].unsqueeze(2), [128, 2, 64]), ALU.mult, (pT6, d_sc), (d_ktz,))
                        k.tt("dve", v4(d_vb[:]), pT6[:, 256:512].rearrange("p (h x) -> p h x", h=4),
                             bc(d_hb[:, 0:4].unsqueeze(2), [128, 4, 64]), ALU.mult, (pT6, d_hb), (d_vb,))
                        yield
                        k.stt("dve", d_P0[:], Z0[:], -1.0, bc(ident[:, :].unsqueeze(1), [128, 4, 128]), ALU.mult, ALU.add,
                              (ident, Z0), (d_P0,))

                    def dn_tail(pp):
                        L0, Z0, d_P0, d_atT = d_L0_2[pp], d_Z0_2[pp], d_P0_2[pp], d_atT_2[pp]
                        d_kbz, d_ktz, d_vb, d_qd, d_szg, d_edl = d_kbz_2[pp], d_ktz_2[pp], d_vb_2[pp], d_qd_2[pp], d_szg_2[pp], d_edl_2[pp]
                        psP = k.dedicated
                        Lc, Zc, Pc = L0, Z0, d_P0
                        NST = 5
                        for st in range(1, NST + 1):
                            Ln_, Zn_ = d_L[st % 2], d_Z[st % 2]
                            psA = k.ps()
                            for h in range(4):
                                k.mm(psA[:, h * 128:(h + 1) * 128], Zc[:, h, :], Lc[:, h, :], True, True, (Zc, Lc), (psA,), inc=(h == 3))
                            if st < NST:
                                psB = k.ps()
                                for h in range(4):
                                    k.mm(psB[:, h * 128:(h + 1) * 128], Lc[:, h, :], Zc[:, h, :], True, True, (Zc, Lc), (psB,), inc=(h == 3))
                            k.cp("act", Ln_[:].rearrange("p h x -> p (h x)"), psA[:, :], (psA,), (Ln_,))
                            if st < NST:
                                k.cp("dve", Zn_[:].rearrange("p h x -> p (h x)"), psB[:, :], (psB,), (Zn_,))
                            for h in range(4):
                                k.mm(psP[:, h * 128:(h + 1) * 128], Ln_[:, h, :], Pc[:, h, :], st == 1 and h == 0, st == NST,
                                     (Ln_, Pc), (psP,), inc=(h == 3))
                            Pn = d_Pb[st % 2]
                            k.tt("dve", Pn[:].rearrange("p h x -> p (h x)"), psP[:, :], d_P0[:].rearrange("p h x -> p (h x)"),
                                 ALU.add, (psP, d_P0), (Pn,))
                            Lc, Zc, Pc = Ln_, Zn_, Pn
                            yield
                        psZP = k.ps()
                        for h in range(4):
                            k.mm(psZP[:, h * 128:(h + 1) * 128], L0[:, h, :], Pc[:, h, :], True, False, (L0, Pc), (psZP,), inc=False)
                            k.mm(psZP[:, h * 128:(h + 1) * 128], ident[:], Pc[:, h, :], False, True, (ident, Pc), (psZP,), inc=(h == 3))
                        k.stt("dve", d_Rp[:], psZP[:, :].rearrange("p (h x) -> p h x", h=4), -1.0,
                              bc(ident[:, :].unsqueeze(1), [128, 4, 128]), ALU.mult, ALU.add, (psZP, ident), (d_Rp,))
                        pT8 = k.psb()
                        for h in range(4):
                            k.tr(pT8[:, h * 128:(h + 1) * 128], Pc[:, h, :], ident[:], (Pc, ident), (pT8,), inc=(h == 3))
                        k.cp("act", d_Pt[:].rearrange("p h x -> p (h x)"), pT8[:, 0:512], (pT8,), (d_Pt,))
                        yield
                        psDl = k.ps()
                        for h in range(4):
                            k.mm(psDl[:, h * 128:(h + 1) * 128], d_Pt[:, h, :], d_Rp[:, h, :], True, True, (d_Pt, d_Rp), (psDl,), inc=(h == 3))
                        k.cp("act", d_Dl[:].rearrange("p h x -> p (h x)"), psDl[:, :], (psDl,), (d_Dl,))
                        yield
                        psW = k.ps()
                        for j in range(2):
                            for hh in range(2):
                                h = 2 * j + hh
                                k.mm(psW[:, j * 128:(j + 1) * 128], d_kbz[:, h, :], Pc[:, h, :], hh == 0, False,
                                     (d_kbz, Pc), (psW,), inc=False)
                                k.mm(psW[:, j * 128:(j + 1) * 128], d_kbz[:, h, :], d_Dl[:, h, :], False, hh == 1,
                                     (d_kbz, d_Dl), (psW,), inc=(h == 3))
                        k.cp("act", d_nWT[:].rearrange("p j x -> p (j x)"), psW[:, 0:256], (psW,), (d_nWT,))
                        yield
                        psU = k.ps()
                        for j in range(2):
                            k.mm(psU[:, j * 128:(j + 1) * 128], d_nWT[:, j, :], d_S_bf[:, j, :], True, False,
                                 (d_nWT, d_S_bf), (psU,), inc=False)
                            for hh in range(2):
                                h = 2 * j + hh
                                k.mm(psU[:, h * 64:(h + 1) * 64], Pc[:, h, :], d_vb[:, h * 64:(h + 1) * 64], False, False,
                                     (Pc, d_vb), (psU,), inc=False)
                                k.mm(psU[:, h * 64:(h + 1) * 64], d_Dl[:, h, :], d_vb[:, h * 64:(h + 1) * 64], False, hh == 1,
                                     (d_Dl, d_vb), (psU,), inc=(h == 3))
                        k.cp("act", d_vn[:], psU[:, 0:256], (psU,), (d_vn,))
                        yield
                        psO2 = k.ps()
                        for j in range(2):
                            k.mm(psO2[:, j * 128:(j + 1) * 128], d_qd[:, j, :], d_S_bf[:, j, :], True, False,
                                 (d_qd, d_S_bf), (psO2,), inc=False)
                            for hh in range(2):
                                h = 2 * j + hh
                                k.mm(psO2[:, h * 64:(h + 1) * 64], d_atT[:, h, :], d_vn[:, h * 64:(h + 1) * 64], False, hh == 1,
                                     (d_atT, d_vn), (psO2,), inc=(h == 3))
                        psK2 = k.ps()
                        for h in range(4):
                            k.mm(psK2[:, h * 64:(h + 1) * 64], d_ktz[:, h, :], d_vn[:, h * 64:(h + 1) * 64], True, True,
                                 (d_ktz, d_vn), (psK2,), inc=(h == 3))
                        k.tt("dve", d_St[:].rearrange("p j (hh e) -> p (j hh) e", hh=2),
                             d_S[:].rearrange("p j (hh e) -> p (j hh) e", hh=2),
                             bc(d_edl[:, 0:4].unsqueeze(2), [128, 4, 64]), ALU.mult, (d_S, d_edl), (d_St,))
                        k.tt("dve", d_S[:].rearrange("p j x -> p (j x)"), d_St[:].rearrange("p j x -> p (j x)"), psK2[:, 0:256],
                             ALU.add, (d_St, psK2), (d_S,))
                        k.cp("act", d_S_bf[:], d_S[:], (d_S,), (d_S_bf,))
                        for h in range(4):
                            k.act(junk[:, h * 64:(h + 1) * 64], psO2[:, h * 64:(h + 1) * 64], AF.Square, (psO2,),
                                  (junk, d_sso), accum_out=d_sso[:, h:h + 1])
                        rstd_act(d_sso, 64, 4, d_tm, d_rs, f=0.5)
                        k.tt("dve", v4(d_y[:]), v4(psO2[:, 0:256]), bc(d_rs[:, 0:4].unsqueeze(2), [128, 4, 64]), ALU.mult,
                             (psO2, d_rs), (d_y,))
                        yield
                        k.tt("dve", d_y_bf[:], d_y[:], d_szg[:], ALU.mult, (d_y, d_szg), (d_y_bf,))
                        pT7 = k.psb()
                        for c in range(2):
                            k.tr(pT7[:, c * 128:(c + 1) * 128], d_y_bf[:, c * 128:(c + 1) * 128], ident[:],
                                 (d_y_bf, ident), (pT7,), inc=(c == 1))
                        k.cp("act", yT[:, 4:6, :].rearrange("p c t -> p (c t)"), pT7[:, 0:256], (pT7,), (yT,))

                    if i == 0 and "dn" in mixers:
                        run_all([dn_head(0)])
                    gens = []
                    if "dn" in mixers:
                        gens.append(dn_tail(par))
                    for nm_, fn_ in (("ret", m_ret), ("sgu", m_sgu), ("pool", m_pool)):
                        if nm_ in mixers:
                            gens.append(fn_())
                    if i + 1 < NT:
                        if "dn" in mixers:
                            gens.append(chain(front(i + 1), dn_head((i + 1) % 2)))
                        else:
                            gens.append(front(i + 1))
                    run_all(gens)

                    psY = [k.ps(), k.ps()]
                    for n in range(2):
                        for g in range(4):
                            k.mm(psY[n][:, :], yaT[:, g, :], wo_pool[:, g, n * 512:(n + 1) * 512], g == 0, False,
                                 (yaT, wo_pool), (psY[n],), inc=False)
                        for c in range(6):
                            k.mm(psY[n][:, :], yT[:, c, :], wo[:, c, n * 512:(n + 1) * 512], False, c == 5,
                                 (yT, wo), (psY[n],), inc=(c == 5))
                    for n in range(2):
                        k.act(junk[:, n * 512:(n + 1) * 512], psY[n][:, :], AF.Square, (psY[n],), (junk, o_ss),
                              accum_out=o_ss[:, n:n + 1])
                    k.tt("dve", o_ss[:, 2:3], o_ss[:, 0:1], o_ss[:, 1:2], ALU.add, (o_ss,), (o_ss,))
                    k.act(o_tm[:, 0:1], o_ss[:, 2:3], AF.Ln, (o_ss,), (o_tm,), scale=1.0 / D, bias=EPS)
                    k.act(o_rs[:, 0:1], o_tm[:, 0:1], AF.Exp, (o_tm,), (o_rs,), scale=-0.5)
                    if skew:
                        k.tt("dve", o_rs[:, 0:1], o_rs[:, 0:1], flg[:, 1 + l:2 + l], ALU.mult, (o_rs, flg), (o_rs,))
                    xo = xmo[i % 2]
                    for n in range(2):
                        k.stt("dve", o_t[n][:], psY[n][:, :], o_rs[:, 0:1], P("gpost", n * 512, (n + 1) * 512), ALU.mult, ALU.mult,
                              (psY[n], o_rs, lp), (o_t[n],))
                        k.tt("dve", xo[:, n * 512:(n + 1) * 512], o_t[n][:], x_t[:, n * 512:(n + 1) * 512], ALU.add,
                             (o_t[n], x_t), (xo,))
                    tgt = dx_m if do_ffn else dst
                    k.dma(tgt.ap[i * 128:(i + 1) * 128, :], xo[:], r=(xo,), w=(tgt.tiles[i],))
                    if i + 2 < NT:
                        k.dma(xt[par][:], src.ap[(i + 2) * 128:(i + 3) * 128, :], r=(src.tiles[i + 2],), w=(xt[par],))
                if skew and l < L - 1:
                    k.dma(st_out[:, 0:256], r_S[:].rearrange("p j x -> p (j x)"), r=(r_S,), w=(stb_out,))
                    k.dma(st_out[:, 256:512], d_S[:].rearrange("p j x -> p (j x)"), r=(d_S,), w=(stb_out,))
                    k.cp("dve", r_y[:], a_bf[(NT - 1) % 2][:], (a_bf[(NT - 1) % 2],), (r_y,))
                    k.dma(st_out[:, 512:768], r_y[:], r=(r_y,), w=(stb_out,))
                    k.cp("dve", d_y[:, 0:18].rearrange("p (c j) -> p c j", c=6), d_xc[:, :, 0:3], (d_xc,), (d_y,))
                    k.dma(st_out[:, 768:786], d_y[:, 0:18], r=(d_y,), w=(stb_out,))
                k.barrier()

            if not do_ffn:
                continue
            with ExitStack() as pb:
                k.barrier()

                def sb(name, shape, dt=F32):
                    return k.sb("b_" + name, shape, dt, es=pb)

                stg[:] = [sb("stg%d" % i_, [128, 704], F32) for i_ in range(7)]
                w_up = sb("w_up", [128, 8, 2 * DFF], BF16)
                w_dn = sb("w_dn", [128, NFF, D], BF16)
                lp = sb("lp", [128, nB], F32)
                k.dma(lp[:], lpb_d[l], r=(), w=(lp,))

                def P(name, a=None, b=None):
                    o, n = offB[name]
                    if a is None:
                        return lp[:, o:o + n]
                    return lp[:, o + a:o + b]


                NTB = NT // 2
                k.work = k.ps_f[0:6]
                xb = [sb("xb%d" % i, [128, 2, D]) for i in range(2)]
                junk = sb("junk", [128, D], BF16)
                xn = sb("xn", [128, 2, D], BF16)
                hT = sb("hT", [128, 8, 256], BF16)
                gT = sb("gT", [128, NFF, 256], BF16)
                ss = sb("ss", [128, 4])
                tmp4 = sb("tmp4", [128, 4])
                rstd = sb("rstd", [128, 4])
                ca = [sb("ca%d" % i, [128, 258]) for i in range(2)]
                acc = [sb("acc%d" % i, [128, 256]) for i in range(2)]
                ge = [sb("ge%d" % i, [128, 256]) for i in range(2)]
                halo = sb("halo", [128, NFF, 2])
                k.memset("pool", halo[:], 0.0, (halo,))
                o_ss = sb("o_ss", [128, 4])
                o_tm = sb("o_tm", [128, 4])
                o_rs = sb("o_rs", [128, 4])
                o_t = [sb("o_t%d" % i, [128, 512]) for i in range(2)]
                fcw = P("fcw").rearrange("p (c j) -> p c j", c=NFF)
                if l == 0:
                    print("[build] phase B sbuf bytes remaining/partition:", nc.sbuf_bytes_remaining)
                if skew and l > 0:
                    k.dma(o_t[0][:, 0:44], st_all[0:128, 786:830], r=(stb_all,), w=(o_t[0],))
                    k.ts("dve", halo[:].rearrange("p c j -> p (c j)"), o_t[0][:, 0:44], flg[:, 0:1], None, ALU.mult, None,
                         (o_t[0], flg), (halo,))

                def ldx(i):
                    for s in range(2):
                        ti = 2 * i + s
                        k.dma(xb[i % 2][:, s, :], dx_m.ap[ti * 128:(ti + 1) * 128, :], r=(dx_m.tiles[ti],), w=(xb[i % 2],))

                def prenorm1(xx):
                    for s in range(2):
                        k.act(junk[:], xx[:, s, :], AF.Square, (xx,), (junk, ss), accum_out=ss[:, s:s + 1])
                    rstd_from(ss, D, 2, tmp4, rstd)
                    for s in range(2):
                        k.ts("dve", xn[:, s, :], xx[:, s, :], rstd[:, s:s + 1], None, ALU.mult, None, (xx, rstd), (xn,))

                def prenorm2():
                    for s in range(2):
                        pT = k.psb()
                        for kc in range(8):
                            k.tr(pT[:, kc * 128:(kc + 1) * 128], xn[:, s, kc * 128:(kc + 1) * 128], ident[:],
                                 (xn, ident), (pT,), inc=(kc == 7))
                        k.tt("dve", hT[:, :, s * 128:(s + 1) * 128], pT[:, :].rearrange("p (c t) -> p c t", c=8),
                             bc(P("gpre").unsqueeze(2), [128, 8, 128]), ALU.mult, (pT, lp), (hT,))

                ldx(0)
                if NTB > 1:
                    ldx(1)
                prenorm1(xb[0])
                prenorm2()
                for kc in range(8):
                    for q8 in range(8):
                        load_cast(w_up, w_up[:, kc, q8 * 704:(q8 + 1) * 704],
                                  w_up_d[l, kc * 128:(kc + 1) * 128, q8 * 704:(q8 + 1) * 704], 704)
                for c in range(NFF):
                    for hf in range(2):
                        load_cast(w_dn, w_dn[:, c, hf * 512:(hf + 1) * 512],
                                  w_dn_d[l, c * 128:(c + 1) * 128, hf * 512:(hf + 1) * 512], 512)
                for i in range(NTB):
                    x_t = xb[i % 2]
                    pend = [None]
                    for c in range(NFF):
                        ps = k.ps()
                        for kc in range(8):
                            k.mm(ps[:, 0:256], w_up[:, kc, c * 128:(c + 1) * 128], hT[:, kc, :], kc == 0, kc == 7,
                                 (w_up, hT), (ps,), inc=False)
                        for kc in range(8):
                            k.mm(ps[:, 256:512], w_up[:, kc, DFF + c * 128:DFF + (c + 1) * 128], hT[:, kc, :], kc == 0, kc == 7,
                                 (w_up, hT), (ps,), inc=(kc == 7))
                        ca_, acc_, ge_ = ca[c % 2], acc[c % 2], ge[c % 2]
                        k.cp("pool", ca_[:, 0:2], halo[:, c, :], (halo,), (ca_,))
                        k.cp("act", ca_[:, 2:258], ps[:, 0:256], (ps,), (ca_,))
                        k.cp("pool", halo[:, c, :], ca_[:, 256:258], (ca_,), (halo,))
                        k.ts("dve", acc_[:], ca_[:, 0:256], fcw[:, c, 0:1], P("fcb", c, c + 1), ALU.mult, ALU.add, (ca_, lp), (acc_,))
                        k.stt("dve", acc_[:], ca_[:, 1:257], fcw[:, c, 1:2], acc_[:], ALU.mult, ALU.add, (ca_, lp, acc_), (acc_,))
                        k.stt("dve", acc_[:], ca_[:, 2:258], fcw[:, c, 2:3], acc_[:], ALU.mult, ALU.add, (ca_, lp, acc_), (acc_,))
                        if pend[0] is not None:
                            pend[0]()

                        def fin(c=c, ps=ps, acc_=acc_, ge_=ge_):
                            k.act(ge_[:], acc_[:], AF.Gelu_apprx_tanh, (acc_,), (ge_,))
                            k.tt("dve", gT[:, c, :], ps[:, 256:512], ge_[:], ALU.mult, (ps, ge_), (gT,))
                        pend[0] = fin
                    pend[0]()
                    pend[0] = None
                    if i + 1 < NTB:
                        prenorm1(xb[(i + 1) % 2])
                    for s in range(2):
                        if s == 1 and i + 1 < NTB:
                            prenorm2()
                        psY = [k.ps(), k.ps()]
                        for n in range(2):
                            for c in range(NFF):
                                k.mm(psY[n][:, :], gT[:, c, s * 128:(s + 1) * 128], w_dn[:, c, n * 512:(n + 1) * 512], c == 0,
                                     c == NFF - 1, (gT, w_dn), (psY[n],), inc=(c == NFF - 1))
                        for n in range(2):
                            k.act(junk[:, n * 512:(n + 1) * 512], psY[n][:, :], AF.Square, (psY[n],), (junk, o_ss),
                                  accum_out=o_ss[:, n:n + 1])
                        k.tt("pool", o_ss[:, 2:3], o_ss[:, 0:1], o_ss[:, 1:2], ALU.add, (o_ss,), (o_ss,))
                        k.ts("pool", o_tm[:, 0:1], o_ss[:, 2:3], 1.0 / D, EPS, ALU.mult, ALU.add, (o_ss,), (o_tm,))
                        k.tt("pool", o_rs[:, 0:1], o_tm[:, 0:1], cs("nhalf", 0, 1), ALU.pow, (o_tm, cst), (o_rs,))
                        if skew:
                            k.tt("pool", o_rs[:, 0:1], o_rs[:, 0:1], flg[:, 1 + l:2 + l], ALU.mult, (o_rs, flg), (o_rs,))
                        for n in range(2):
                            k.stt("dve", o_t[n][:], psY[n][:, :], o_rs[:, 0:1], P("gpost", n * 512, (n + 1) * 512), ALU.mult, ALU.mult,
                                  (psY[n], o_rs, lp), (o_t[n],))
                            k.tt("pool", x_t[:, s, n * 512:(n + 1) * 512], o_t[n][:], x_t[:, s, n * 512:(n + 1) * 512], ALU.add,
                                 (o_t[n], x_t), (x_t,))
                        ti = 2 * i + s
                        k.dma(dst.ap[ti * 128:(ti + 1) * 128, :], x_t[:, s, :], r=(x_t,), w=(dst.tiles[ti],))
                    if i + 2 < NTB:
                        ldx(i + 2)
                if skew and l < L - 1:
                    k.dma(st_out[:, 786:830], halo[:].rearrange("p c j -> p (c j)"), r=(halo,), w=(stb_out,))
                    k.collective(GROUPS, st_out_t.ap().opt(), st_all_t.ap().opt(), r=(stb_out,), w=(stb_all,))

        for q in ("sp", "act"):
            i = k.dma_i[q]
            n = k.ndma[q]
            for j in range(n):
                cntj = (i - j + n - 1) // n if i > j else 0
                if cntj > 0:
                    k._wait("sp", ("dma", q, j), 16 * cntj)
        print("[build] instructions=%d waits=%d" % (k.nins, k.nwait))
    return nc


def make_inmaps(inputs, T_, L, batches):
    cst, ropet, pmat = host_consts(T_)
    lpa = np.stack([host_params(inputs, l)[0] for l in range(L)])
    lpb = np.stack([host_params(inputs, l)[1] for l in range(L)])
    f = lambda a: np.ascontiguousarray(np.asarray(a, np.float32))
    flags = np.ones((128, 8), np.float32)
    flags[:, 0] = 0.0
    maps = []
    for b in batches:
        maps.append({
            "x": f(inputs["x"][b, :T_]),
            "w_in": f(inputs["w_in"][:L]), "w_out": f(inputs["w_out"][:L]),
            "w_up": f(inputs["ffn_w_up"][:L]), "w_down": f(inputs["ffn_w_down"][:L]),
            "cst": cst, "ropet": ropet, "pmat": pmat, "lpa": lpa, "lpb": lpb, "flags": flags,
        })
    return maps


def make_inmaps_skew(inputs, S, L):
    Th = S // 2
    f = lambda a: np.ascontiguousarray(np.asarray(a, np.float32))
    lp = [host_params(inputs, l) for l in range(L)]
    maps = []
    B = inputs["x"].shape[0]
    slots_h = ([min(s_, L - 1) for s_ in range(L + 1)], [max(s_ - 1, 0) for s_ in range(L + 1)])
    cache = {}
    for h in range(2):
        cst, ropet, pm = host_consts(Th, pos0=h * Th)
        pmat = pm.copy()
        if h == 1:
            pmat[:, 0:512] = pm[:, 512:1024]
        sl = slots_h[h]
        flags = np.zeros((128, 8), np.float32)
        flags[:, 0] = float(h)
        for s_ in range(L + 1):
            flags[:, 1 + s_] = 1.0 if (s_ < L if h == 0 else s_ >= 1) else 0.0
        cache[h] = dict(cst=cst, ropet=ropet, pmat=pmat, flags=flags,
                        w_in=f(inputs["w_in"][sl]), w_out=f(inputs["w_out"][sl]),
                        w_up=f(inputs["ffn_w_up"][sl]), w_down=f(inputs["ffn_w_down"][sl]),
                        lpa=np.stack([lp[l][0] for l in sl]), lpb=np.stack([lp[l][1] for l in sl]))
    for c in range(2 * B):
        b, h = c // 2, c % 2
        m = dict(cache[h])
        m["x"] = f(inputs["x"][b, h * Th:(h + 1) * Th])
        maps.append(m)
    return maps


def kernel(**inputs):
    inputs = {k_: np.asarray(v) for k_, v in inputs.items()}
    B, S, _ = inputs["x"].shape
    L = inputs["w_in"].shape[0]
    Th = S // 2
    nc = build(Th, L + 1, skew=True)
    maps = make_inmaps_skew(inputs, S, L)
    res = run_bass_kernel_spmd(nc, maps, core_ids=list(range(2 * B)))
    out = np.zeros((B, S, D), np.float32)
    for c in range(2 * B):
        b, h = c // 2, c % 2
        out[b, h * Th:(h + 1) * Th] = np.asarray(res.results[c]["out"]).reshape(Th, D)
    return out
```

```python
import numpy as np
from contextlib import ExitStack
import concourse.bass as bass
import concourse.mybir as mybir
from concourse.bass_utils import run_bass_kernel_spmd

F32 = mybir.dt.float32
BF16 = mybir.dt.bfloat16
AF = mybir.ActivationFunctionType
ALU = mybir.AluOpType

D = 1024
PIN = 2824
DFF = 2816
EPS = 1e-6
NFF = DFF // 128


class Buf:
    __slots__ = ("w", "r", "psum", "name")

    def __init__(self, name, psum=False):
        self.w = None
        self.r = {}
        self.psum = psum
        self.name = name


class T:
    __slots__ = ("t", "b")

    def __init__(self, t, b):
        self.t = t
        self.b = b

    def __getitem__(self, key):
        return self.t[key]


class K:
    def __init__(self, nc, es):
        self.nc = nc
        self.es = es
        self.E = {"pe": nc.tensor, "dve": nc.vector, "act": nc.scalar, "pool": nc.gpsimd, "sp": nc.sync}
        self.semh = {}
        self.cnt = {}
        for e in ("pe", "dve", "act", "pool"):
            self.semh[e] = es.enter_context(nc.semaphore("s_" + e))
            self.cnt[e] = 0
        self.seen = {e: {} for e in self.E}
        self.ndma = {"sp": 8, "act": 4}
        self.dma_i = {"sp": 0, "act": 0}
        for q, n in self.ndma.items():
            for j in range(n):
                self.semh[("dma", q, j)] = es.enter_context(nc.semaphore("d_%s%d" % (q, j)))
        self.ps_f = []
        self.ps_b = []
        self.ps_fi = 0
        self.ps_bi = 0
        self.nwait = 0
        self.nins = 0

    def sb(self, name, shape, dt, es=None):
        self.nsb = getattr(self, "nsb", 0) + 1
        name = "sb%d_%s" % (self.nsb, name)
        t = (es or self.es).enter_context(self.nc.sbuf_tensor(name, list(shape), dt))
        return T(t, Buf(name))

    def init_psum(self, nf=6, nb=2):
        for i in range(nf):
            t = self.es.enter_context(self.nc.psum_tensor("psf%d" % i, [128, 512], F32))
            self.ps_f.append(T(t, Buf("psf%d" % i, psum=True)))
        for i in range(nb):
            t = self.es.enter_context(self.nc.psum_tensor("psb%d" % i, [128, 1024], BF16))
            self.ps_b.append(T(t, Buf("psb%d" % i, psum=True)))
        self.work = self.ps_f[3:6]
        self.proj = self.ps_f[0:2]
        self.pj_i = 0
        self.dedicated = self.ps_f[2]

    def ps(self):
        p = self.work[self.ps_fi % len(self.work)]
        self.ps_fi += 1
        return p

    def psj(self):
        p = self.proj[self.pj_i % 2]
        self.pj_i += 1
        return p

    def psb(self):
        p = self.ps_b[self.ps_bi % len(self.ps_b)]
        self.ps_bi += 1
        return p

    def _wait(self, eng, key, val):
        if self.seen[eng].get(key, 0) >= val:
            return
        self.E[eng].wait_ge(self.semh[key], val)
        self.seen[eng][key] = val
        self.nwait += 1

    def _deps(self, eng, reads, writes):
        deps = {}

        def add(tok):
            if tok is None:
                return
            k_, v = tok
            if deps.get(k_, 0) < v:
                deps[k_] = v

        for t in reads:
            b = t.b
            add(b.w)
            if b.psum:
                for k_, v in b.r.items():
                    add((k_, v))
        for t in writes:
            b = t.b
            add(b.w)
            for k_, v in b.r.items():
                add((k_, v))
        for k_, v in deps.items():
            if eng == "pe" and k_ == "pe":
                continue
            self._wait(eng, k_, v)

    def _mark(self, tok, reads, writes):
        k_, v = tok
        for t in reads:
            b = t.b
            if b.psum:
                b.w = tok
                b.r = {}
            else:
                if b.r.get(k_, 0) < v:
                    b.r[k_] = v
        for t in writes:
            b = t.b
            b.w = tok
            b.r = {}

    def op(self, eng, fn, r=(), w=(), inc=True):
        self._deps(eng, r, w)
        ins = fn(self.E[eng])
        self.nins += 1
        if inc:
            self.cnt[eng] += 1
            ins.then_inc(self.semh[eng], 1)
            tok = (eng, self.cnt[eng])
        else:
            tok = (eng, self.cnt[eng] + 1)
        self._mark(tok, r, w)
        return ins

    def dma(self, out, in_, r=(), w=(), q="sp"):
        i = self.dma_i[q]
        n = self.ndma[q]
        j = i % n
        key = ("dma", q, j)
        if i >= n:
            self._wait(q, key, 16 * (i // n))
        self._deps(q, r, w)
        ins = self.E[q].dma_start(out=out, in_=in_)
        ins.then_inc(self.semh[key], 16)
        self.dma_i[q] = i + 1
        self.nins += 1
        tok = (key, 16 * (i // n + 1))
        self._mark(tok, r, w)
        return tok

    def collective(self, groups, ins_ap, outs_ap, r, w):
        if "cc" not in self.semh:
            self.semh["cc"] = self.es.enter_context(self.nc.semaphore("s_cc"))
            self.ncc = 0
        self._deps("pool", r, w)
        ins = self.nc.gpsimd.collective_compute("AllGather", ALU.bypass, replica_groups=groups,
                                                ins=[ins_ap], outs=[outs_ap])
        self.ncc += 1
        ins.then_inc(self.semh["cc"], 1)
        self.nins += 1
        self._mark(("cc", self.ncc), r, w)

    def barrier(self):
        toks = [(e, self.cnt[e]) for e in ("pe", "dve", "act", "pool") if self.cnt[e] > 0]
        for q, n in self.ndma.items():
            i = self.dma_i[q]
            for j in range(n):
                cj = (i - j + n - 1) // n if i > j else 0
                if cj > 0:
                    toks.append((("dma", q, j), 16 * cj))
        for eng in ("pe", "dve", "act", "pool", "sp"):
            for key, val in toks:
                if key != eng:
                    self._wait(eng, key, val)

    def mm(self, out, lhsT, rhs, start, stop, r, w, inc):
        return self.op("pe", lambda e: e.matmul(out, lhsT=lhsT, rhs=rhs, start=start, stop=stop,
                                                skip_group_check=True), r, w, inc=inc)

    def tr(self, out, in_, ident, r, w, inc):
        return self.op("pe", lambda e: e.transpose(out, in_, ident), r, w, inc=inc)

    def tt(self, eng, out, in0, in1, op, r, w):
        return self.op(eng, lambda e: e.tensor_tensor(out=out, in0=in0, in1=in1, op=op), r, w)

    def ts(self, eng, out, in0, s1, s2, op0, op1, r, w):
        if s2 is None:
            return self.op(eng, lambda e: e.tensor_scalar(out=out, in0=in0, scalar1=s1, scalar2=None, op0=op0), r, w)
        return self.op(eng, lambda e: e.tensor_scalar(out=out, in0=in0, scalar1=s1, scalar2=s2, op0=op0, op1=op1), r, w)

    def stt(self, eng, out, in0, scalar, in1, op0, op1, r, w):
        return self.op(eng, lambda e: e.scalar_tensor_tensor(out=out, in0=in0, scalar=scalar, in1=in1,
                                                             op0=op0, op1=op1), r, w)

    def act(self, out, in_, func, r, w, **kw):
        return self.op("act", lambda e: e.activation(out=out, in_=in_, func=func, **kw), r, w)

    def cp(self, eng, out, in_, r, w):
        if eng == "act":
            return self.op("act", lambda e: e.activation(out=out, in_=in_, func=AF.Copy), r, w)
        return self.op(eng, lambda e: e.tensor_copy(out=out, in_=in_), r, w)

    def memset(self, eng, ap, val, w):
        return self.op(eng, lambda e: e.memset(ap, val), (), w)


def bc(ap, shape):
    return ap.to_broadcast(list(shape))


def const_layout(T_):
    NT = T_ // 128
    off = {}
    c = 0
    for name, n in (("ident", 128), ("triu", 128), ("negmask", 128), ("strict", 128), ("ones", 128),
                    ("blk", 128), ("dq", 4), ("dk", 4), ("g128", 4), ("nhalf", 8)):
        off[name] = (c, n)
        c += n
    return off, c


def rope_layout(T_):
    NT = T_ // 128
    return {"cos": (0, NT * 32), "sin": (NT * 32, NT * 32), "nsin": (2 * NT * 32, NT * 32)}, 3 * NT * 32


PM_OFF = {"ptf": (0, 512), "ptr": (512, 512), "ptp": (1024, 512)}
PM_DEV = {"pt0": (0, 512), "ptr": (512, 512), "ptp": (1024, 512)}
NST = 830
GROUPS = [[0, 1], [2, 3], [4, 5], [6, 7]]


def host_consts(T_, pos0=0):
    off, n = const_layout(T_)
    roff, rn = rope_layout(T_)
    NT = T_ // 128
    C = np.zeros((128, n), np.float32)
    R = np.zeros((128, rn), np.float32)
    PM = np.zeros((128, 1536), np.float32)

    def put(name, arr):
        if name in off:
            o, m = off[name]
            C[:, o:o + m] = np.asarray(arr, np.float32).reshape(128, m)
        elif name in roff:
            o, m = roff[name]
            R[:, o:o + m] = np.asarray(arr, np.float32).reshape(128, m)
        else:
            o, m = PM_OFF[name]
            PM[:, o:o + m] = np.asarray(arr, np.float32).reshape(128, m)

    p = np.arange(128)
    put("ident", np.eye(128))
    put("triu", (p[:, None] <= p[None, :]))
    put("negmask", np.where(p[:, None] >= p[None, :], 0.0, 1e30))
    put("strict", (p[:, None] > p[None, :]))
    put("ones", np.ones((128, 128)))
    put("blk", (p[:, None] // 64 == p[None, :] // 64))
    inv = (1.0 / (np.float32(10000.0) ** (np.arange(0, 64, 2, dtype=np.float32) / np.float32(64)))).astype(np.float32)
    pos = np.arange(pos0, pos0 + T_).astype(np.float32)
    ang = (pos[:, None] * inv[None, :]).astype(np.float32)
    cos = np.cos(ang).astype(np.float32).reshape(NT, 128, 32).transpose(1, 0, 2)
    sin = np.sin(ang).astype(np.float32).reshape(NT, 128, 32).transpose(1, 0, 2)
    put("cos", cos)
    put("sin", sin)
    put("nsin", -sin)
    lg = np.log(1.0 - 2.0 ** (-5.0 - np.arange(4, dtype=np.float64)))
    put("dq", np.exp(lg[None, :] * (p[:, None] + 1.0)))
    put("dk", np.exp(-lg[None, :] * (p[:, None] + 1.0)) * 0.125)
    put("g128", np.broadcast_to(np.exp(lg * 128.0)[None, :], (128, 4)))
    put("nhalf", np.full((128, 8), -0.5))
    wins = (2, 4, 8, 16)
    ptf = np.zeros((128, 4, 128))
    ptr = np.zeros((128, 4, 128))
    ptp = np.zeros((128, 4, 128))
    for g, w in enumerate(wins):
        for t in range(128):
            for s in range(max(0, t - w + 1), t + 1):
                ptf[s, g, t] += 1.0 / min(t + 1, w)
                ptr[s, g, t] += 1.0 / w
            ptf[t, g, t] -= 1.0
            ptr[t, g, t] -= 1.0
            for srel in range(t - w + 1, 0):
                ptp[128 + srel, g, t] += 1.0 / w
    put("ptf", ptf)
    put("ptr", ptr)
    put("ptp", ptp)
    return C, R, PM


LP_A = (("gpre", 8), ("gpost", 1024), ("poolw", 512), ("poolsc", 4), ("lng", 256), ("lnb", 256),
        ("wsT", 512), ("bs", 4), ("convw", 24), ("alog", 4), ("dtb", 4), ("normg", 64))
LP_B = (("gpre", 8), ("gpost", 1024), ("fcw", 66), ("fcb", 22))


def lay(spec):
    off = {}
    c = 0
    for name, n in spec:
        off[name] = (c, n)
        c += n
    return off, c


def host_params(inp, l):
    offA, nA = lay(LP_A)
    offB, nB = lay(LP_B)
    A = np.zeros((128, nA), np.float32)
    B = np.zeros((128, nB), np.float32)

    def put(M, off, name, arr):
        o, m = off[name]
        M[:, o:o + m] = np.asarray(arr, np.float32).reshape(128, m)

    def rep(v):
        return np.broadcast_to(np.asarray(v, np.float32).reshape(1, -1), (128, np.asarray(v).size))

    put(A, offA, "gpre", inp["norm_pre_mix"][l].reshape(8, 128).T)
    put(A, offA, "gpost", rep(inp["norm_post_mix"][l]))
    pw = np.zeros((128, 4, 128), np.float32)
    for g in range(4):
        pw[:64, g, :64] = inp["pool_w"][l][g]
    put(A, offA, "poolw", pw)
    psc = np.zeros((128, 4), np.float32)
    psc[:64, :] = inp["pool_scale"][l].reshape(4, 64).T
    put(A, offA, "poolsc", psc)
    put(A, offA, "lng", rep(inp["sgu_ln_g"][l]))
    put(A, offA, "lnb", rep(inp["sgu_ln_b"][l]))
    put(A, offA, "wsT", inp["sgu_ws"][l].transpose(2, 0, 1))
    put(A, offA, "bs", inp["sgu_bs"][l].T)
    put(A, offA, "convw", inp["dn_conv_w"][l].reshape(4, 6, 128).transpose(2, 1, 0))
    put(A, offA, "alog", rep(inp["dn_a_log"][l]))
    put(A, offA, "dtb", rep(inp["dn_dt_bias"][l]))
    put(A, offA, "normg", rep(inp["dn_norm_g"][l]))
    put(B, offB, "gpre", inp["norm_pre_ffn"][l].reshape(8, 128).T)
    put(B, offB, "gpost", rep(inp["norm_post_ffn"][l]))
    put(B, offB, "fcw", inp["ffn_conv_w"][l].reshape(3, NFF, 128).transpose(2, 1, 0))
    put(B, offB, "fcb", inp["ffn_conv_b"][l].reshape(NFF, 128).T)
    return A, B


def build(T_, L, mixers=("pool", "ret", "sgu", "dn"), do_ffn=True, skew=False):
    NT = T_ // 128
    nc = bass.Bass("TRN2", target_bir_lowering=False)
    coff, ncst = const_layout(T_)
    offA, nA = lay(LP_A)
    offB, nB = lay(LP_B)

    x_in = nc.dram_tensor("x", [T_, D], F32, kind="ExternalInput").ap()
    w_in_d = nc.dram_tensor("w_in", [L, D, PIN], F32, kind="ExternalInput").ap()
    w_out_d = nc.dram_tensor("w_out", [L, D, D], F32, kind="ExternalInput").ap()
    w_up_d = nc.dram_tensor("w_up", [L, D, 2 * DFF], F32, kind="ExternalInput").ap()
    w_dn_d = nc.dram_tensor("w_down", [L, DFF, D], F32, kind="ExternalInput").ap()
    cst_d = nc.dram_tensor("cst", [128, ncst], F32, kind="ExternalInput").ap()
    roff, nrope = rope_layout(T_)
    rope_d = nc.dram_tensor("ropet", [128, nrope], F32, kind="ExternalInput").ap()
    pm_d = nc.dram_tensor("pmat", [128, 1536], F32, kind="ExternalInput").ap()
    lpa_d = nc.dram_tensor("lpa", [L, 128, nA], F32, kind="ExternalInput").ap()
    lpb_d = nc.dram_tensor("lpb", [L, 128, nB], F32, kind="ExternalInput").ap()
    flg_d = nc.dram_tensor("flags", [128, 8], F32, kind="ExternalInput").ap()
    st_out_t = nc.dram_tensor("st_out", [128, NST], F32)
    st_all_t = nc.dram_tensor("st_all", [256, NST], F32)
    st_out, st_all = st_out_t.ap(), st_all_t.ap()
    stb_out = T(None, Buf("st_out"))
    stb_all = T(None, Buf("st_all"))
    out_d = nc.dram_tensor("out", [T_, D], F32, kind="ExternalOutput").ap()
    xm_d = nc.dram_tensor("xm_scr", [T_, D], F32, kind="Internal").ap()
    xs_d = nc.dram_tensor("xs_scr", [T_, D], F32, kind="Internal").ap()

    class DT:
        def __init__(self, name, ap):
            self.ap = ap
            self.tiles = [T(None, Buf("%s%d" % (name, i))) for i in range(NT)]

    dx_in, dx_m, dx_s, dx_o = DT("xin", x_in), DT("xm", xm_d), DT("xs", xs_d), DT("xo", out_d)
    wdram = T(None, Buf("wdram"))

    with ExitStack() as es:
        es.enter_context(nc.allow_low_precision("bf16 matmul operands, fp32 accumulation"))
        k = K(nc, es)
        k.init_psum(6, 2)

        cst = k.sb("cst", [128, ncst], F32)
        k.dma(cst[:], cst_d, r=(), w=(cst,))

        def cs(name, a=None, b=None):
            o, n = coff[name]
            if a is None:
                return cst[:, o:o + n]
            return cst[:, o + a:o + b]

        flg = k.sb("flg", [128, 8], F32)
        k.dma(flg[:], flg_d, r=(), w=(flg,))
        ident = k.sb("ident", [128, 128], BF16)
        k.cp("dve", ident[:], cs("ident"), (cst,), (ident,))
        blk = k.sb("blk", [128, 128], BF16)
        k.cp("dve", blk[:], cs("blk"), (cst,), (blk,))

        SW = 1412
        stg = []
        stg_i = [0]
        cast_engs = ("dve", "act", "dve")

        def load_cast(dst_t, dst_ap, src_ap, ncols, nrows=128):
            i = stg_i[0]
            stg_i[0] += 1
            s = stg[i % len(stg)]
            k.dma(s[0:nrows, 0:ncols], src_ap, r=(wdram,), w=(s,))
            k.cp(cast_engs[i % 3], dst_ap, s[0:nrows, 0:ncols], (s,), (dst_t,))

        def rstd_from(ss, n, ncol, tmp, out, f=1.0):
            k.ts("pool", tmp[:, 0:ncol], ss[:, 0:ncol], 1.0 / (n * f * f), EPS / (f * f), ALU.mult, ALU.add, (ss,), (tmp,))
            k.tt("pool", out[:, 0:ncol], tmp[:, 0:ncol], cs("nhalf", 0, ncol), ALU.pow, (tmp, cst), (out,))

        def rstd_act(ss, n, ncol, tmp, out, f=1.0):
            k.act(tmp[:, 0:ncol], ss[:, 0:ncol], AF.Ln, (ss,), (tmp,), scale=1.0 / (n * f * f), bias=EPS / (f * f))
            k.act(out[:, 0:ncol], tmp[:, 0:ncol], AF.Exp, (tmp,), (out,), scale=-0.5)

        for l in range(L):
            src = dx_in if l == 0 else dx_s
            dst = dx_o if l == L - 1 else dx_s
            k.barrier()
            with ExitStack() as pa:
                k.work = k.ps_f[3:6]

                def sb(name, shape, dt=F32):
                    return k.sb("a_" + name, shape, dt, es=pa)

                stg[:] = [sb("stg%d" % i_, [128, 706], F32) for i_ in range(4)]
                w_in = sb("w_in", [128, 8, PIN], BF16)
                wo_pool = sb("wo_pool", [128, 4, D], BF16)
                wo = sb("wo", [128, 6, D], BF16)
                lp = sb("lp", [128, nA], F32)
                k.dma(lp[:], lpa_d[l], r=(), w=(lp,))
                ropet = sb("ropet", [128, nrope], F32)
                k.dma(ropet[:], rope_d, r=(), w=(ropet,))

                def rp(name, a, b):
                    o, n = roff[name]
                    return ropet[:, o + a:o + b]

                ptf = sb("ptf", [128, 512], BF16)
                ptr = sb("ptr", [128, 512], BF16)
                ptp = sb("ptp", [128, 512], BF16)
                for nm_, t_ in (("pt0", ptf), ("ptr", ptr), ("ptp", ptp)):
                    o_, n_ = PM_DEV[nm_]
                    load_cast(t_, t_[:, :], pm_d[:, o_:o_ + n_], n_)

                def P(name, a=None, b=None):
                    o, n = offA[name]
                    if a is None:
                        return lp[:, o:o + n]
                    return lp[:, o + a:o + b]


                poolw = sb("poolw", [128, 4, 128], BF16)
                k.cp("dve", poolw[:].rearrange("p g d -> p (g d)"), P("poolw"), (lp,), (poolw,))
                wmT = sb("wmT", [128, 4, 128], BF16)
                k.tt("dve", wmT[:], P("wsT").rearrange("p (g t) -> p g t", g=4),
                     bc(cs("triu").unsqueeze(1), [128, 4, 128]), ALU.mult, (lp, cst), (wmT,))
                nexpA = sb("nexpA", [128, 4])
                k.act(nexpA[:], P("alog"), AF.Exp, (lp,), (nexpA,))
                k.ts("dve", nexpA[:], nexpA[:], -1.0, None, ALU.mult, None, (nexpA,), (nexpA,))

                xt = [sb("xt%d" % i, [128, D]) for i in range(2)]
                xmo0 = sb("xmo0", [128, D])
                xmo = [xmo0, xmo0]
                junk = sb("junk", [128, D], BF16)
                xn = sb("xn", [128, D], BF16)
                hT = sb("hT", [128, 8, 128], BF16)
                ss = sb("ss", [128, 4])
                tmp4 = sb("tmp4", [128, 4])
                rstd = sb("rstd", [128, 4])
                a_bf = [sb("a_bf%d" % i, [128, 256], BF16) for i in range(2)]
                dT_bf = sb("dT_bf", [128, 4, 128], BF16)
                yaT = sb("yaT", [128, 4, 128], BF16)
                k.memset("pool", dT_bf[:], 0.0, (dT_bf,))
                k.memset("pool", yaT[:], 0.0, (yaT,))
                yT = sb("yT", [128, 6, 128], BF16)
                k.memset("pool", yT[:], 0.0, (yT,))
                gu_2 = [sb("gu%d" % i_, [128, 256]) for i_ in range(2)]
                gv_2 = [sb("gv%d" % i_, [128, 256]) for i_ in range(2)]
                vln = sb("vln", [128, 256])
                vln2 = vln
                vln_bf = sb("vln_bf", [128, 256], BF16)
                s_sum = sb("s_sum", [128, 4])
                s_nm = sb("s_nm", [128, 4])
                s_ss = sb("s_ss", [128, 4])
                s_tmp = sb("s_tmp", [128, 4])
                s_rs = sb("s_rs", [128, 4])
                yc_t = sb("yc_t", [128, 256])
                yc_bf = sb("yc_bf", [128, 256], BF16)
                r_qd_2 = [sb("r_qd%d" % i_, [128, 256]) for i_ in range(2)]
                r_t1 = sb("r_t1", [128, 256])
                r_t2 = sb("r_t2", [128, 256])
                r_kd_2 = [sb("r_kd%d" % i_, [128, 256]) for i_ in range(2)]
                r_u1 = r_t1
                r_u2 = r_t2
                r_q_bf = sb("r_q_bf", [128, 256], BF16)
                r_kz = sb("r_kz", [128, 4, 128], BF16)
                r_v_bf_2 = [sb("r_v_bf%d" % i_, [128, 256], BF16) for i_ in range(2)]
                r_sg_2 = [sb("r_sg%d" % i_, [128, 256]) for i_ in range(2)]
                r_qT = sb("r_qT", [128, 2, 128], BF16)
                r_kTz = sb("r_kTz", [128, 4, 128], BF16)
                r_scm = sb("r_scm", [128, 4, 128], BF16)
                r_S = sb("r_S", [128, 2, 128])
                r_S_bf = sb("r_S_bf", [128, 2, 128], BF16)
                r_St = sb("r_St", [128, 2, 128])
                r_sso = sb("r_sso", [128, 4])
                r_tm = sb("r_tm", [128, 4])
                r_rs = sb("r_rs", [128, 4])
                r_y = sb("r_y", [128, 256])
                r_y_bf = sb("r_y_bf", [128, 256], BF16)
                k.memset("pool", r_kz[:], 0.0, (r_kz,))
                k.memset("pool", r_kTz[:], 0.0, (r_kTz,))
                k.memset("pool", r_S[:], 0.0, (r_S,))
                k.memset("pool", r_S_bf[:], 0.0, (r_S_bf,))
                for t_ in a_bf:
                    k.memset("pool", t_[:], 0.0, (t_,))
                d_xc = sb("d_xc", [128, 6, 131])
                d_acc = sb("d_acc", [128, 6, 128])
                d_tmp = sb("d_tmp", [128, 6, 128])
                d_qkv = d_acc
                d_sq = sb("d_sq", [128, 4, 128], BF16)
                d_qkn = sb("d_qkn", [128, 4, 128], BF16)
                d_qz = sb("d_qz", [128, 4, 128], BF16)
                d_kz = sb("d_kz", [128, 4, 128], BF16)
                d_vT = sb("d_vT", [128, 2, 128], BF16)
                d_sc = sb("d_sc", [128, 32])
                d_ab_2 = [sb("d_ab%d" % i_, [128, 8]) for i_ in range(2)]
                d_g = sb("d_g", [128, 4])
                d_hb = sb("d_hb", [128, 4])
                d_Ug = sb("d_Ug", [128, 4, 128])
                d_rn = d_Ug
                d_dec = sb("d_dec", [128, 8])
                d_arg = sb("d_arg", [128, 4, 128])
                d_E = d_arg
                d_Eb = sb("d_Eb", [128, 4, 128])
                d_eR = sb("d_eR", [128, 4, 128])
                d_qd_2 = [sb("d_qd%d" % i_, [128, 2, 128], BF16) for i_ in range(2)]
                d_L0_2 = [sb("d_L0%d" % i_, [128, 4, 128], BF16) for i_ in range(2)]
                d_Z0_2 = [sb("d_Z0%d" % i_, [128, 4, 128], BF16) for i_ in range(2)]
                d_edl_2 = [sb("d_edl%d" % i_, [128, 4]) for i_ in range(2)]
                d_Pt = sb("d_Pt", [128, 4, 128], BF16)
                d_Rp = sb("d_Rp", [128, 4, 128], BF16)
                d_Dl = sb("d_Dl", [128, 4, 128], BF16)
                d_L = [sb("d_L%d" % i, [128, 4, 128], BF16) for i in range(2)]
                d_Z = [sb("d_Z%d" % i, [128, 4, 128], BF16) for i in range(2)]
                d_P0_2 = [sb("d_P0%d" % i_, [128, 4, 128], BF16) for i_ in range(2)]
                d_Pb = [sb("d_Pb%d" % i, [128, 4, 128], BF16) for i in range(2)]
                d_at = sb("d_at", [128, 4, 128], BF16)
                d_atT_2 = [sb("d_atT%d" % i_, [128, 4, 128], BF16) for i_ in range(2)]
                d_kbz_2 = [sb("d_kbz%d" % i_, [128, 4, 128], BF16) for i_ in range(2)]
                d_ktz_2 = [sb("d_ktz%d" % i_, [128, 4, 128], BF16) for i_ in range(2)]
                d_vb_2 = [sb("d_vb%d" % i_, [128, 256], BF16) for i_ in range(2)]
                d_nWT = sb("d_nWT", [128, 2, 128], BF16)
                d_vn = sb("d_vn", [128, 256], BF16)
                d_S = sb("d_S", [128, 2, 128])
                d_St = sb("d_St", [128, 2, 128])
                d_S_bf = sb("d_S_bf", [128, 2, 128], BF16)
                d_sz_2 = [sb("d_sz%d" % i_, [128, 256]) for i_ in range(2)]
                d_szg_2 = [sb("d_szg%d" % i_, [128, 256]) for i_ in range(2)]
                d_sso = sb("d_sso", [128, 4])
                d_tm = sb("d_tm", [128, 4])
                d_rs = sb("d_rs", [128, 4])
                d_y = sb("d_y", [128, 256])
                d_y_bf = sb("d_y_bf", [128, 256], BF16)
                for t_ in (d_qz, d_kz, d_kbz_2[0], d_kbz_2[1], d_ktz_2[0], d_ktz_2[1]):
                    k.memset("pool", t_[:], 0.0, (t_,))
                k.memset("pool", d_xc[:], 0.0, (d_xc,))
                k.memset("pool", d_S[:], 0.0, (d_S,))
                k.memset("pool", d_S_bf[:], 0.0, (d_S_bf,))
                if skew and l > 0:
                    hfl = flg[:, 0:1]
                    k.dma(r_St[:].rearrange("p j x -> p (j x)"), st_all[0:128, 0:256], r=(stb_all,), w=(r_St,))
                    k.ts("dve", r_S[:], r_St[:], hfl, None, ALU.mult, None, (r_St, flg), (r_S,))
                    k.cp("dve", r_S_bf[:], r_S[:], (r_S,), (r_S_bf,))
                    k.dma(d_St[:].rearrange("p j x -> p (j x)"), st_all[0:128, 256:512], r=(stb_all,), w=(d_St,))
                    k.ts("dve", d_S[:], d_St[:], hfl, None, ALU.mult, None, (d_St, flg), (d_S,))
                    k.cp("dve", d_S_bf[:], d_S[:], (d_S,), (d_S_bf,))
                    k.dma(r_y[:], st_all[0:128, 512:768], r=(stb_all,), w=(r_y,))
                    k.ts("dve", a_bf[1][:], r_y[:], hfl, None, ALU.mult, None, (r_y, flg), (a_bf[1],))
                    k.dma(d_y[:, 0:18], st_all[0:128, 768:786], r=(stb_all,), w=(d_y,))
                    k.ts("dve", d_xc[:, :, 0:3], d_y[:, 0:18].rearrange("p (c j) -> p c j", c=6), hfl, None, ALU.mult, None,
                         (d_y, flg), (d_xc,))
                o_ss = sb("o_ss", [128, 4])
                o_tm = sb("o_tm", [128, 4])
                o_rs = sb("o_rs", [128, 4])
                o_t = [sb("o_t%d" % i, [128, 512]) for i in range(2)]

                v4 = lambda ap: ap.rearrange("p (h x) -> p h x", h=4)
                if l == 0:
                    print("[build] phase A sbuf bytes remaining/partition:", nc.sbuf_bytes_remaining)

                def front(i):
                    par = i % 2
                    x_t = xt[par]
                    gu, gv, r_qd, r_kd, r_sg, d_sz = gu_2[par], gv_2[par], r_qd_2[par], r_kd_2[par], r_sg_2[par], d_sz_2[par]
                    r_v_bf, d_ab = r_v_bf_2[par], d_ab_2[par]
                    k.act(junk[:], x_t[:], AF.Square, (x_t,), (junk, ss), accum_out=ss[:, 0:1])
                    rstd_act(ss, D, 1, tmp4, rstd)
                    k.ts("dve", xn[:], x_t[:], rstd[:, 0:1], None, ALU.mult, None, (x_t, rstd), (xn,))
                    pT = k.psb()
                    for kc in range(8):
                        k.tr(pT[:, kc * 128:(kc + 1) * 128], xn[:, kc * 128:(kc + 1) * 128], ident[:],
                             (xn, ident), (pT,), inc=(kc == 7))
                    k.tt("dve", hT[:], pT[:, :].rearrange("p (c t) -> p c t", c=8),
                         bc(P("gpre").unsqueeze(2), [128, 8, 128]), ALU.mult, (pT, lp), (hT,))
                    yield

                    def proj(c0, c1):
                        ps = k.psj()
                        for kc in range(8):
                            k.mm(ps[:, 0:c1 - c0], hT[:, kc, :], w_in[:, kc, c0:c1], kc == 0, kc == 7,
                                 (hT, w_in), (ps,), inc=(kc == 7))
                        return ps

                    ab = a_bf[par]
                    G0 = proj(0, 512)
                    k.cp("act", ab[:], G0[:, 0:256], (G0,), (ab,))
                    k.tt("dve", v4(r_qd[:]), v4(G0[:, 256:512]), bc(cs("dq").unsqueeze(2), [128, 4, 64]),
                         ALU.mult, (G0, cst), (r_qd,))
                    yield
                    G1 = proj(512, 1024)
                    k.tt("dve", v4(r_kd[:]), v4(G1[:, 0:256]), bc(cs("dk").unsqueeze(2), [128, 4, 64]),
                         ALU.mult, (G1, cst), (r_kd,))
                    k.cp("act", r_v_bf[:], G1[:, 256:512], (G1,), (r_v_bf,))
                    yield
                    G3 = proj(1536, 1792)
                    k.act(gv[:], G3[:, 0:256], AF.Gelu_apprx_tanh, (G3,), (gv, s_sum), accum_out=s_sum[:, par:par + 1])
                    G2 = proj(1024, 1536)
                    k.act(gu[:], G2[:, 256:512], AF.Gelu_apprx_tanh, (G2,), (gu,))
                    k.act(r_sg[:], G2[:, 0:256], AF.Tanh, (G2,), (r_sg,), scale=0.5)
                    k.stt("dve", r_sg[:], r_sg[:], 1.0, G2[:, 0:256], ALU.add, ALU.mult, (r_sg, G2), (r_sg,))
                    yield
                    G4 = proj(2560, 2824)
                    k.act(d_sz[:], G4[:, 0:256], AF.Tanh, (G4,), (d_sz,), scale=0.5)
                    k.stt("dve", d_sz[:], d_sz[:], 1.0, G4[:, 0:256], ALU.add, ALU.mult, (d_sz, G4), (d_sz,))
                    k.tt("dve", d_ab[:, 0:4], G4[:, 260:264], P("dtb"), ALU.add, (G4, lp), (d_ab,))
                    k.cp("dve", d_ab[:, 4:8], G4[:, 256:260], (G4,), (d_ab,))
                    yield
                    if "dn" in mixers:
                        psQ = [k.ps(), k.ps()]
                        for fc in range(6):
                            pq = psQ[fc // 4]
                            for kc in range(8):
                                k.mm(pq[:, (fc % 4) * 128:(fc % 4 + 1) * 128],
                                     w_in[:, kc, 1792 + fc * 128:1792 + (fc + 1) * 128], hT[:, kc, :], kc == 0, kc == 7,
                                     (w_in, hT), (pq,), inc=(kc == 7 and fc in (3, 5)))
                        k.cp("act", d_xc[:, 0:4, 3:131], psQ[0][:, :].rearrange("p (c t) -> p c t", c=4), (psQ[0],), (d_xc,))
                        k.cp("act", d_xc[:, 4:6, 3:131], psQ[1][:, 0:256].rearrange("p (c t) -> p c t", c=2), (psQ[1],), (d_xc,))

                def run_all(gens):
                    gens = list(gens)
                    while gens:
                        for g_ in list(gens):
                            try:
                                next(g_)
                            except StopIteration:
                                gens.remove(g_)

                k.dma(xt[0][:], src.ap[0:128, :], r=(src.tiles[0],), w=(xt[0],))
                if NT > 1:
                    k.dma(xt[1][:], src.ap[128:256, :], r=(src.tiles[1],), w=(xt[1],))
                def chain(*gs):
                    for g_ in gs:
                        yield from g_

                g0 = front(0)
                next(g0)
                for kc in range(8):
                    for hf in range(4):
                        load_cast(w_in, w_in[:, kc, hf * 706:(hf + 1) * 706],
                                  w_in_d[l, kc * 128:(kc + 1) * 128, hf * 706:(hf + 1) * 706], 706)
                k.memset("pool", wo_pool[:], 0.0, (wo_pool,))
                for g in range(4):
                    for hf in range(2):
                        load_cast(wo_pool, wo_pool[0:64, g, hf * 512:(hf + 1) * 512],
                                  w_out_d[l, g * 64:(g + 1) * 64, hf * 512:(hf + 1) * 512], 512, nrows=64)
                for c in range(6):
                    for hf in range(2):
                        load_cast(wo, wo[:, c, hf * 512:(hf + 1) * 512],
                                  w_out_d[l, 256 + c * 128:256 + (c + 1) * 128, hf * 512:(hf + 1) * 512], 512)
                run_all([g0])
                for i in range(NT):
                    par = i % 2
                    x_t = xt[par]
                    gu, gv, r_qd, r_kd, r_sg, d_sz = gu_2[par], gv_2[par], r_qd_2[par], r_kd_2[par], r_sg_2[par], d_sz_2[par]
                    r_v_bf, d_ab = r_v_bf_2[par], d_ab_2[par]
                    ab = a_bf[par]
                    ap_ = a_bf[(i + 1) % 2]

                    def m_pool():
                        psD = k.ps()
                        ptm = ptf if i == 0 else ptr
                        for g in range(4):
                            k.mm(psD[0:64, g * 128:(g + 1) * 128], ab[:, g * 64:(g + 1) * 64],
                                 ptm[:, g * 128:(g + 1) * 128], True, False, (ab, ptm), (psD,),
                                 inc=False)
                            if True:
                                k.mm(psD[0:64, g * 128:(g + 1) * 128], ap_[:, g * 64:(g + 1) * 64],
                                     ptp[:, g * 128:(g + 1) * 128], False, True, (ap_, ptp), (psD,), inc=(g == 3))
                        k.cp("act", dT_bf[0:64, :, :].rearrange("p g t -> p (g t)"), psD[0:64, :], (psD,), (dT_bf,))
                        yield
                        psYa = k.ps()
                        for g in range(4):
                            k.mm(psYa[0:64, g * 128:(g + 1) * 128], poolw[:, g, 0:64], dT_bf[:, g, :], True, True,
                                 (poolw, dT_bf), (psYa,), inc=(g == 3))
                        k.tt("dve", yaT[0:64, :, :], psYa[0:64, :].rearrange("p (g t) -> p g t", g=4),
                             bc(P("poolsc")[0:64, :].unsqueeze(2), [64, 4, 128]), ALU.mult, (psYa, lp), (yaT,))

                    def m_sgu():
                        k.ts("dve", s_nm[:, 0:1], s_sum[:, par:par + 1], -1.0 / 256, None, ALU.mult, None, (s_sum,), (s_nm,))
                        k.act(junk[:, 0:256], gv[:], AF.Square, (gv, s_nm), (junk, s_ss), bias=s_nm[:, 0:1],
                              accum_out=s_ss[:, 0:1])
                        rstd_from(s_ss, 256, 1, s_tmp, s_rs)
                        k.ts("dve", vln[:], gv[:], s_nm[:, 0:1], s_rs[:, 0:1], ALU.add, ALU.mult, (gv, s_nm, s_rs), (vln,))
                        k.tt("dve", vln2[:], vln[:], P("lng"), ALU.mult, (vln, lp), (vln2,))
                        k.tt("dve", vln_bf[:], vln2[:], P("lnb"), ALU.add, (vln2, lp), (vln_bf,))
                        yield
                        psS = k.ps()
                        for g in range(4):
                            k.mm(psS[:, g * 64:(g + 1) * 64], wmT[:, g, :], vln_bf[:, g * 64:(g + 1) * 64], True, True,
                                 (wmT, vln_bf), (psS,), inc=(g == 3))
                        k.tt("dve", v4(yc_t[:]), v4(psS[:, 0:256]), bc(P("bs").unsqueeze(2), [128, 4, 64]), ALU.add,
                             (psS, lp), (yc_t,))
                        k.tt("dve", yc_bf[:], yc_t[:], gu[:], ALU.mult, (yc_t, gu), (yc_bf,))
                        yield
                        pT2 = k.psb()
                        for c in range(2):
                            k.tr(pT2[:, c * 128:(c + 1) * 128], yc_bf[:, c * 128:(c + 1) * 128], ident[:],
                                 (yc_bf, ident), (pT2,), inc=(c == 1))
                        k.cp("act", yT[:, 2:4, :].rearrange("p c t -> p (c t)"), pT2[:, 0:256], (pT2,), (yT,))

                    def m_ret():
                        cosb = bc(rp("cos", i * 32, (i + 1) * 32).unsqueeze(1).unsqueeze(1), [128, 4, 2, 32])
                        sinb = bc(rp("sin", i * 32, (i + 1) * 32).unsqueeze(1), [128, 4, 32])
                        nsinb = bc(rp("nsin", i * 32, (i + 1) * 32).unsqueeze(1), [128, 4, 32])
                        v42 = lambda ap: ap.rearrange("p (h two x) -> p h two x", h=4, two=2)

                        def rope(xd, t1, t2, eng2):
                            k.tt(eng2, v42(t1[:]), v42(xd[:]), cosb, ALU.mult, (xd, ropet), (t1,))
                            k.tt(eng2, v42(t2[:])[:, :, 0, :], v42(xd[:])[:, :, 1, :], nsinb, ALU.mult, (xd, ropet), (t2,))
                            k.tt(eng2, v42(t2[:])[:, :, 1, :], v42(xd[:])[:, :, 0, :], sinb, ALU.mult, (xd, ropet), (t2,))

                        rope(r_qd, r_t1, r_t2, "dve")
                        k.tt("dve", r_q_bf[:], r_t1[:], r_t2[:], ALU.add, (r_t1, r_t2), (r_q_bf,))
                        yield
                        rope(r_kd, r_u1, r_u2, "pool")
                        for hh in range(2):
                            k.tt("dve", r_kz[:, hh::2, hh * 64:(hh + 1) * 64],
                                 v4(r_u1[:])[:, hh::2, :], v4(r_u2[:])[:, hh::2, :], ALU.add, (r_u1, r_u2), (r_kz,))
                        pT3 = k.psb()
                        for c in range(2):
                            k.tr(pT3[:, c * 128:(c + 1) * 128], r_q_bf[:, c * 128:(c + 1) * 128], ident[:],
                                 (r_q_bf, ident), (pT3,), inc=False)
                        for h in range(4):
                            k.tr(pT3[:, (2 + h) * 128:(3 + h) * 128], r_kz[:, h, :], ident[:],
                                 (r_kz, ident), (pT3,), inc=(h == 3))
                        k.cp("act", r_qT[:].rearrange("p c t -> p (c t)"), pT3[:, 0:256], (pT3,), (r_qT,))
                        k.cp("dve", r_kTz[:].rearrange("p c t -> p (c t)"), pT3[:, 256:768], (pT3,), (r_kTz,))
                        yield
                        psSc = k.ps()
                        for h in range(4):
                            k.mm(psSc[:, h * 128:(h + 1) * 128], r_kTz[:, h, :], r_qT[:, h // 2, :], True, True,
                                 (r_kTz, r_qT), (psSc,), inc=(h == 3))
                        k.tt("dve", r_scm[:], psSc[:, :].rearrange("p (h s) -> p h s", h=4),
                             bc(cs("triu").unsqueeze(1), [128, 4, 128]), ALU.mult, (psSc, cst), (r_scm,))
                        yield
                        psO = k.ps()
                        for j in range(2):
                            k.mm(psO[:, j * 128:(j + 1) * 128], r_qT[:, j, :], r_S_bf[:, j, :], True, False,
                                 (r_qT, r_S_bf), (psO,), inc=False)
                            for hh in range(2):
                                h = 2 * j + hh
                                k.mm(psO[:, h * 64:(h + 1) * 64], r_scm[:, h, :], r_v_bf[:, h * 64:(h + 1) * 64],
                                     False, hh == 1, (r_scm, r_v_bf), (psO,), inc=(h == 3))
                        psKV = k.ps()
                        for h in range(4):
                            k.mm(psKV[:, h * 64:(h + 1) * 64], r_kz[:, h, :], r_v_bf[:, h * 64:(h + 1) * 64], True, True,
                                 (r_kz, r_v_bf), (psKV,), inc=(h == 3))
                        k.tt("dve", r_St[:].rearrange("p j x -> p (j x)"), psKV[:, 0:256],
                             r_S[:].rearrange("p j x -> p (j x)"), ALU.add, (psKV, r_S), (r_St,))
                        k.tt("dve", r_S[:].rearrange("p j (hh e) -> p (j hh) e", hh=2),
                             r_St[:].rearrange("p j (hh e) -> p (j hh) e", hh=2),
                             bc(cs("g128").unsqueeze(2), [128, 4, 64]), ALU.mult, (r_St, cst), (r_S,))
                        k.cp("act", r_S_bf[:], r_S[:], (r_S,), (r_S_bf,))
                        for h in range(4):
                            k.act(junk[:, h * 64:(h + 1) * 64], psO[:, h * 64:(h + 1) * 64], AF.Square, (psO,),
                                  (junk, r_sso), accum_out=r_sso[:, h:h + 1])
                        rstd_from(r_sso, 64, 4, r_tm, r_rs, f=0.5)
                        k.tt("dve", v4(r_y[:]), v4(psO[:, 0:256]), bc(r_rs[:, 0:4].unsqueeze(2), [128, 4, 64]), ALU.mult,
                             (psO, r_rs), (r_y,))
                        yield
                        k.tt("dve", r_y_bf[:], r_y[:], r_sg[:], ALU.mult, (r_y, r_sg), (r_y_bf,))
                        pT4 = k.psb()
                        for c in range(2):
                            k.tr(pT4[:, c * 128:(c + 1) * 128], r_y_bf[:, c * 128:(c + 1) * 128], ident[:],
                                 (r_y_bf, ident), (pT4,), inc=(c == 1))
                        k.cp("act", yT[:, 0:2, :].rearrange("p c t -> p (c t)"), pT4[:, 0:256], (pT4,), (yT,))

                    def dn_head(pp):
                        d_sz, d_ab = d_sz_2[pp], d_ab_2[pp]
                        L0, Z0, d_P0, d_atT = d_L0_2[pp], d_Z0_2[pp], d_P0_2[pp], d_atT_2[pp]
                        d_kbz, d_ktz, d_vb, d_qd, d_szg, d_edl = d_kbz_2[pp], d_ktz_2[pp], d_vb_2[pp], d_qd_2[pp], d_szg_2[pp], d_edl_2[pp]
                        cw = P("convw").rearrange("p (c j) -> p c j", c=6)
                        k.tt("dve", d_acc[:], d_xc[:, :, 0:128], bc(cw[:, :, 0:1], [128, 6, 128]), ALU.mult, (d_xc, lp), (d_acc,))
                        for j in range(1, 4):
                            k.tt("dve", d_tmp[:], d_xc[:, :, j:j + 128], bc(cw[:, :, j:j + 1], [128, 6, 128]), ALU.mult,
                                 (d_xc, lp), (d_tmp,))
                            k.tt("dve", d_acc[:], d_acc[:], d_tmp[:], ALU.add, (d_acc, d_tmp), (d_acc,))
                        k.cp("pool", d_xc[:, :, 0:3], d_xc[:, :, 128:131], (d_xc,), (d_xc,))
                        yield
                        k.act(d_tmp[:], d_acc[:], AF.Tanh, (d_acc,), (d_tmp,), scale=0.5)
                        k.stt("dve", d_qkv[:], d_tmp[:], 1.0, d_acc[:], ALU.add, ALU.mult, (d_tmp, d_acc), (d_qkv,))
                        k.tt("dve", v4(d_szg[:]), v4(d_sz[:]), bc(P("normg").unsqueeze(1), [128, 4, 64]),
                             ALU.mult, (d_sz, lp), (d_szg,))
                        k.act(d_sc[:, 12:16], d_ab[:, 4:8], AF.Exp, (d_ab,), (d_sc,), scale=-1.0)
                        k.act(d_sc[:, 4:8], d_ab[:, 0:4], AF.Exp, (d_ab,), (d_sc,))
                        k.act(d_sc[:, 8:12], d_sc[:, 4:8], AF.Ln, (d_sc,), (d_sc,), bias=1.0)
                        k.tt("dve", d_g[:], d_sc[:, 8:12], nexpA[:], ALU.mult, (d_sc, nexpA), (d_g,))
                        k.ts("dve", d_sc[:, 12:16], d_sc[:, 12:16], 1.0, None, ALU.add, None, (d_sc,), (d_sc,))
                        k.op("dve", lambda e: e.reciprocal(out=d_sc[:, 16:20], in_=d_sc[:, 12:16]), (d_sc,), (d_sc,))
                        k.ts("dve", d_hb[:, 0:4], d_sc[:, 16:20], 0.5, None, ALU.mult, None, (d_sc,), (d_hb,))
                        yield
                        for h in range(4):
                            k.ts("dve", d_Ug[:, h, :], cs("triu"), d_g[:, h:h + 1], None, ALU.mult, None, (cst, d_g), (d_Ug,))
                        psC = k.ps()
                        k.mm(psC[:, 0:4], cs("triu"), d_g[:, 0:4], True, True, (cst, d_g), (psC,), inc=False)
                        k.mm(psC[:, 4:8], cs("ones"), d_g[:, 0:4], True, True, (cst, d_g), (psC,), inc=True)
                        psR = k.ps()
                        k.mm(psR[:, :], cs("ones"), d_Ug[:].rearrange("p h x -> p (h x)"), True, True, (cst, d_Ug), (psR,), inc=True)
                        k.cp("dve", d_dec[:, 0:8], psC[:, 0:8], (psC,), (d_dec,))
                        k.act(d_sc[:, 24:28], d_dec[:, 0:4], AF.Exp, (d_dec,), (d_sc,))
                        k.tt("dve", d_sc[:, 0:4], d_dec[:, 4:8], d_dec[:, 0:4], ALU.subtract, (d_dec,), (d_sc,))
                        k.act(d_sc[:, 28:32], d_sc[:, 0:4], AF.Exp, (d_sc,), (d_sc,))
                        k.act(d_edl[:, 0:4], d_dec[:, 4:8], AF.Exp, (d_dec,), (d_edl,))
                        k.stt("dve", d_sc[:, 20:24], d_sc[:, 16:20], -1.0, d_sc[:, 24:28], ALU.mult, ALU.mult, (d_sc,), (d_sc,))
                        k.tt("dve", d_arg[:], psR[:, :].rearrange("p (h x) -> p h x", h=4),
                             bc(d_dec[:, 0:4].unsqueeze(2), [128, 4, 128]), ALU.subtract, (psR, d_dec), (d_arg,))
                        k.tt("dve", d_arg[:], d_arg[:], bc(cs("negmask").unsqueeze(1), [128, 4, 128]), ALU.add, (d_arg, cst), (d_arg,))
                        k.act(d_E[:], d_arg[:], AF.Exp, (d_arg,), (d_E,), scale=-1.0)
                        k.act(d_eR[:].rearrange("p h x -> p (h x)"), psR[:, :], AF.Exp, (psR,), (d_eR,))
                        yield
                        k.tt("dve", d_Eb[:], d_E[:], bc(cs("strict").unsqueeze(1), [128, 4, 128]), ALU.mult, (d_E, cst), (d_Eb,))
                        k.tt("dve", d_Eb[:], d_Eb[:], bc(d_sc[:, 16:20].unsqueeze(2), [128, 4, 128]), ALU.mult, (d_Eb, d_sc), (d_Eb,))
                        k.act(d_sq[:], d_qkv[:, 0:4, :], AF.Square, (d_qkv,), (d_sq,))
                        psN = k.ps()
                        k.mm(psN[:, :], blk[:], d_sq[:].rearrange("p c t -> p (c t)"), True, True, (blk, d_sq), (psN,), inc=True)
                        k.act(d_rn[:, 0:2, :].rearrange("p c t -> p (c t)"), psN[:, 0:256], AF.Ln, (psN,), (d_rn,),
                              scale=64.0, bias=256.0 * EPS)
                        k.act(d_rn[:, 2:4, :].rearrange("p c t -> p (c t)"), psN[:, 256:512], AF.Ln, (psN,), (d_rn,),
                              scale=1.0, bias=4.0 * EPS)
                        k.act(d_rn[:], d_rn[:], AF.Exp, (d_rn,), (d_rn,), scale=-0.5)
                        k.tt("dve", d_qkn[:], d_qkv[:, 0:4, :], d_rn[:], ALU.mult, (d_qkv, d_rn), (d_qkn,))
                        k.cp("act", d_vT[:], d_qkv[:, 4:6, :], (d_qkv,), (d_vT,))
                        yield
                        for hh in range(2):
                            sl = slice(hh * 64, (hh + 1) * 64)
                            k.cp("act", d_qz[sl, hh::2, :], d_qkn[sl, 0:2, :], (d_qkn,), (d_qz,))
                            k.cp("dve", d_kz[sl, hh::2, :], d_qkn[sl, 2:4, :], (d_qkn,), (d_kz,))
                            k.tt("dve", d_qd[sl, :, :], d_qkn[sl, 0:2, :], d_eR[sl, hh::2, :], ALU.mult, (d_qkn, d_eR), (d_qd,))
                        psKK = k.ps()
                        psQK = k.ps()
                        for h in range(4):
                            k.mm(psKK[:, h * 128:(h + 1) * 128], d_kz[:, h, :], d_qkn[:, 2 + h // 2, :], True, True,
                                 (d_kz, d_qkn), (psKK,), inc=(h == 3))
                        for h in range(4):
                            k.mm(psQK[:, h * 128:(h + 1) * 128], d_qz[:, h, :], d_qkn[:, 2 + h // 2, :], True, True,
                                 (d_qz, d_qkn), (psQK,), inc=(h == 3))
                        k.tt("dve", L0[:].rearrange("p h x -> p (h x)"), psKK[:, :], d_Eb[:].rearrange("p h x -> p (h x)"),
                             ALU.mult, (psKK, d_Eb), (L0,))
                        k.tt("dve", d_at[:].rearrange("p h x -> p (h x)"), psQK[:, :], d_E[:].rearrange("p h x -> p (h x)"),
                             ALU.mult, (psQK, d_E), (d_at,))
                        yield
                        pT5 = k.psb()
                        for h in range(4):
                            k.tr(pT5[:, h * 128:(h + 1) * 128], L0[:, h, :], ident[:], (L0, ident), (pT5,), inc=False)
                        for h in range(4):
                            k.tr(pT5[:, (4 + h) * 128:(5 + h) * 128], d_at[:, h, :], ident[:], (d_at, ident), (pT5,), inc=(h == 3))
                        k.cp("act", Z0[:].rearrange("p h x -> p (h x)"), pT5[:, 0:512], (pT5,), (Z0,))
                        k.cp("dve", d_atT[:].rearrange("p h x -> p (h x)"), pT5[:, 512:1024], (pT5,), (d_atT,))
                        yield
                        pT6 = k.psb()
                        for c in range(2):
                            k.tr(pT6[:, c * 128:(c + 1) * 128], d_qkn[:, 2 + c, :], ident[:], (d_qkn, ident), (pT6,), inc=False)
                        for c in range(2):
                            k.tr(pT6[:, (2 + c) * 128:(3 + c) * 128], d_vT[:, c, :], ident[:], (d_vT, ident), (pT6,), inc=(c == 1))
                        ktok = pT6[:, 0:256].rearrange("p (h x) -> p h x", h=4)
                        for hh in range(2):
                            k.tt("dve", d_kbz[:, hh::2, hh * 64:(hh + 1) * 64], ktok[:, hh::2, :],
                                 bc(d_sc[:, 20 + hh:24:2].unsqueeze(2), [128, 2, 64]), ALU.mult, (pT6, d_sc), (d_kbz,))
                            k.tt("dve", d_ktz[:, hh::2, hh * 64:(hh + 1) * 64], ktok[:, hh::2, :],
                                 bc(d_sc[:, 28 + hh:32:2].unsqueeze(2), [128, 2, 64]), ALU.mult, (pT6, d_sc), (d_ktz,))
                        k.tt("dve", v4(d_vb[:]), pT6[:, 256:512].rearrange("p (h x) -> p h x", h=4),
                             bc(d_hb[:, 0:4].unsqueeze(2), [128, 4, 64]), ALU.mult, (pT6, d_hb), (d_vb,))
                        yield
                        k.stt("dve", d_P0[:], Z0[:], -1.0, bc(ident[:, :].unsqueeze(1), [128, 4, 128]), ALU.mult, ALU.add,
                              (ident, Z0), (d_P0,))

                    def dn_tail(pp):
                        L0, Z0, d_P0, d_atT = d_L0_2[pp], d_Z0_2[pp], d_P0_2[pp], d_atT_2[pp]
                        d_kbz, d_ktz, d_vb, d_qd, d_szg, d_edl = d_kbz_2[pp], d_ktz_2[pp], d_vb_2[pp], d_qd_2[pp], d_szg_2[pp], d_edl_2[pp]
                        psP = k.dedicated
                        Lc, Zc, Pc = L0, Z0, d_P0
                        NST = 5
                        for st in range(1, NST + 1):
                            Ln_, Zn_ = d_L[st % 2], d_Z[st % 2]
                            psA = k.ps()
                            for h in range(4):
                                k.mm(psA[:, h * 128:(h + 1) * 128], Zc[:, h, :], Lc[:, h, :], True, True, (Zc, Lc), (psA,), inc=(h == 3))
                            if st < NST:
                                psB = k.ps()
                                for h in range(4):
                                    k.mm(psB[:, h * 128:(h + 1) * 128], Lc[:, h, :], Zc[:, h, :], True, True, (Zc, Lc), (psB,), inc=(h == 3))
                            k.cp("act", Ln_[:].rearrange("p h x -> p (h x)"), psA[:, :], (psA,), (Ln_,))
                            if st < NST:
                                k.cp("dve", Zn_[:].rearrange("p h x -> p (h x)"), psB[:, :], (psB,), (Zn_,))
                            for h in range(4):
                                k.mm(psP[:, h * 128:(h + 1) * 128], Ln_[:, h, :], Pc[:, h, :], st == 1 and h == 0, st == NST,
                                     (Ln_, Pc), (psP,), inc=(h == 3))
                            Pn = d_Pb[st % 2]
                            k.tt("dve", Pn[:].rearrange("p h x -> p (h x)"), psP[:, :], d_P0[:].rearrange("p h x -> p (h x)"),
                                 ALU.add, (psP, d_P0), (Pn,))
                            Lc, Zc, Pc = Ln_, Zn_, Pn
                            yield
                        psZP = k.ps()
                        for h in range(4):
                            k.mm(psZP[:, h * 128:(h + 1) * 128], L0[:, h, :], Pc[:, h, :], True, False, (L0, Pc), (psZP,), inc=False)
                            k.mm(psZP[:, h * 128:(h + 1) * 128], ident[:], Pc[:, h, :], False, True, (ident, Pc), (psZP,), inc=(h == 3))
                        k.stt("dve", d_Rp[:], psZP[:, :].rearrange("p (h x) -> p h x", h=4), -1.0,
                              bc(ident[:, :].unsqueeze(1), [128, 4, 128]), ALU.mult, ALU.add, (psZP, ident), (d_Rp,))
                        pT8 = k.psb()
                        for h in range(4):
                            k.tr(pT8[:, h * 128:(h + 1) * 128], Pc[:, h, :], ident[:], (Pc, ident), (pT8,), inc=(h == 3))
                        k.cp("act", d_Pt[:].rearrange("p h x -> p (h x)"), pT8[:, 0:512], (pT8,), (d_Pt,))
                        yield
                        psDl = k.ps()
                        for h in range(4):
                            k.mm(psDl[:, h * 128:(h + 1) * 128], d_Pt[:, h, :], d_Rp[:, h, :], True, True, (d_Pt, d_Rp), (psDl,), inc=(h == 3))
                        k.cp("act", d_Dl[:].rearrange("p h x -> p (h x)"), psDl[:, :], (psDl,), (d_Dl,))
                        yield
                        psW = k.ps()
                        for j in range(2):
                            for hh in range(2):
                                h = 2 * j + hh
                                k.mm(psW[:, j * 128:(j + 1) * 128], d_kbz[:, h, :], Pc[:, h, :], hh == 0, False,
                                     (d_kbz, Pc), (psW,), inc=False)
                                k.mm(psW[:, j * 128:(j + 1) * 128], d_kbz[:, h, :], d_Dl[:, h, :], False, hh == 1,
                                     (d_kbz, d_Dl), (psW,), inc=(h == 3))
                        k.cp("act", d_nWT[:].rearrange("p j x -> p (j x)"), psW[:, 0:256], (psW,), (d_nWT,))
                        yield
                        psU = k.ps()
                        for j in range(2):
                            k.mm(psU[:, j * 128:(j + 1) * 128], d_nWT[:, j, :], d_S_bf[:, j, :], True, False,
                                 (d_nWT, d_S_bf), (psU,), inc=False)
                            for hh in range(2):
                                h = 2 * j + hh
                                k.mm(psU[:, h * 64:(h + 1) * 64], Pc[:, h, :], d_vb[:, h * 64:(h + 1) * 64], False, False,
                                     (Pc, d_vb), (psU,), inc=False)
                                k.mm(psU[:, h * 64:(h + 1) * 64], d_Dl[:, h, :], d_vb[:, h * 64:(h + 1) * 64], False, hh == 1,
                                     (d_Dl, d_vb), (psU,), inc=(h == 3))
                        k.cp("act", d_vn[:], psU[:, 0:256], (psU,), (d_vn,))
                        yield
                        psO2 = k.ps()
                        for j in range(2):
                            k.mm(psO2[:, j * 128:(j + 1) * 128], d_qd[:, j, :], d_S_bf[:, j, :], True, False,
                                 (d_qd, d_S_bf), (psO2,), inc=False)
                            for hh in range(2):
                                h = 2 * j + hh
                                k.mm(psO2[:, h * 64:(h + 1) * 64], d_atT[:, h, :], d_vn[:, h * 64:(h + 1) * 64], False, hh == 1,
                                     (d_atT, d_vn), (psO2,), inc=(h == 3))
                        psK2 = k.ps()
                        for h in range(4):
                            k.mm(psK2[:, h * 64:(h + 1) * 64], d_ktz[:, h, :], d_vn[:, h * 64:(h + 1) * 64], True, True,
                                 (d_ktz, d_vn), (psK2,), inc=(h == 3))
                        k.tt("dve", d_St[:].rearrange("p j (hh e) -> p (j hh) e", hh=2),
                             d_S[:].rearrange("p j (hh e) -> p (j hh) e", hh=2),
                             bc(d_edl[:, 0:4].unsqueeze(2), [128, 4, 64]), ALU.mult, (d_S, d_edl), (d_St,))
                        k.tt("dve", d_S[:].rearrange("p j x -> p (j x)"), d_St[:].rearrange("p j x -> p (j x)"), psK2[:, 0:256],
                             ALU.add, (d_St, psK2), (d_S,))
                        k.cp("act", d_S_bf[:], d_S[:], (d_S,), (d_S_bf,))
                        for h in range(4):
                            k.act(junk[:, h * 64:(h + 1) * 64], psO2[:, h * 64:(h + 1) * 64], AF.Square, (psO2,),
                                  (junk, d_sso), accum_out=d_sso[:, h:h + 1])
                        rstd_act(d_sso, 64, 4, d_tm, d_rs, f=0.5)
                        k.tt("dve", v4(d_y[:]), v4(psO2[:, 0:256]), bc(d_rs[:, 0:4].unsqueeze(2), [128, 4, 64]), ALU.mult,
                             (psO2, d_rs), (d_y,))
                        yield
                        k.tt("dve", d_y_bf[:], d_y[:], d_szg[:], ALU.mult, (d_y, d_szg), (d_y_bf,))
                        pT7 = k.psb()
                        for c in range(2):
                            k.tr(pT7[:, c * 128:(c + 1) * 128], d_y_bf[:, c * 128:(c + 1) * 128], ident[:],
                                 (d_y_bf, ident), (pT7,), inc=(c == 1))
                        k.cp("act", yT[:, 4:6, :].rearrange("p c t -> p (c t)"), pT7[:, 0:256], (pT7,), (yT,))

                    if i == 0 and "dn" in mixers:
                        run_all([dn_head(0)])
                    gens = []
                    if "dn" in mixers:
                        gens.append(dn_tail(par))
                    for nm_, fn_ in (("ret", m_ret), ("sgu", m_sgu), ("pool", m_pool)):
                        if nm_ in mixers:
                            gens.append(fn_())
                    if i + 1 < NT:
                        if "dn" in mixers:
                            gens.append(chain(front(i + 1), dn_head((i + 1) % 2)))
                        else:
                            gens.append(front(i + 1))
                    run_all(gens)

                    psY = [k.ps(), k.ps()]
                    for n in range(2):
                        for g in range(4):
                            k.mm(psY[n][:, :], yaT[:, g, :], wo_pool[:, g, n * 512:(n + 1) * 512], g == 0, False,
                                 (yaT, wo_pool), (psY[n],), inc=False)
                        for c in range(6):
                            k.mm(psY[n][:, :], yT[:, c, :], wo[:, c, n * 512:(n + 1) * 512], False, c == 5,
                                 (yT, wo), (psY[n],), inc=(c == 5))
                    for n in range(2):
                        k.act(junk[:, n * 512:(n + 1) * 512], psY[n][:, :], AF.Square, (psY[n],), (junk, o_ss),
                              accum_out=o_ss[:, n:n + 1])
                    k.tt("dve", o_ss[:, 2:3], o_ss[:, 0:1], o_ss[:, 1:2], ALU.add, (o_ss,), (o_ss,))
                    k.act(o_tm[:, 0:1], o_ss[:, 2:3], AF.Ln, (o_ss,), (o_tm,), scale=1.0 / D, bias=EPS)
                    k.act(o_rs[:, 0:1], o_tm[:, 0:1], AF.Exp, (o_tm,), (o_rs,), scale=-0.5)
                    if skew:
                        k.tt("dve", o_rs[:, 0:1], o_rs[:, 0:1], flg[:, 1 + l:2 + l], ALU.mult, (o_rs, flg), (o_rs,))
                    xo = xmo[i % 2]
                    for n in range(2):
                        k.stt("dve", o_t[n][:], psY[n][:, :], o_rs[:, 0:1], P("gpost", n * 512, (n + 1) * 512), ALU.mult, ALU.mult,
                              (psY[n], o_rs, lp), (o_t[n],))
                        k.tt("dve", xo[:, n * 512:(n + 1) * 512], o_t[n][:], x_t[:, n * 512:(n + 1) * 512], ALU.add,
                             (o_t[n], x_t), (xo,))
                    tgt = dx_m if do_ffn else dst
                    k.dma(tgt.ap[i * 128:(i + 1) * 128, :], xo[:], r=(xo,), w=(tgt.tiles[i],))
                    if i + 2 < NT:
                        k.dma(xt[par][:], src.ap[(i + 2) * 128:(i + 3) * 128, :], r=(src.tiles[i + 2],), w=(xt[par],))
                if skew and l < L - 1:
                    k.dma(st_out[:, 0:256], r_S[:].rearrange("p j x -> p (j x)"), r=(r_S,), w=(stb_out,))
                    k.dma(st_out[:, 256:512], d_S[:].rearrange("p j x -> p (j x)"), r=(d_S,), w=(stb_out,))
                    k.cp("dve", r_y[:], a_bf[(NT - 1) % 2][:], (a_bf[(NT - 1) % 2],), (r_y,))
                    k.dma(st_out[:, 512:768], r_y[:], r=(r_y,), w=(stb_out,))
                    k.cp("dve", d_y[:, 0:18].rearrange("p (c j) -> p c j", c=6), d_xc[:, :, 0:3], (d_xc,), (d_y,))
                    k.dma(st_out[:, 768:786], d_y[:, 0:18], r=(d_y,), w=(stb_out,))
                k.barrier()

            if not do_ffn:
                continue
            with ExitStack() as pb:
                k.barrier()

                def sb(name, shape, dt=F32):
                    return k.sb("b_" + name, shape, dt, es=pb)

                stg[:] = [sb("stg%d" % i_, [128, 704], F32) for i_ in range(5)]
                w_up = sb("w_up", [128, 8, 2 * DFF], BF16)
                w_dn = sb("w_dn", [128, NFF, D], BF16)
                lp = sb("lp", [128, nB], F32)
                k.dma(lp[:], lpb_d[l], r=(), w=(lp,))

                def P(name, a=None, b=None):
                    o, n = offB[name]
                    if a is None:
                        return lp[:, o:o + n]
                    return lp[:, o + a:o + b]


                NTB = NT // 2
                k.work = k.ps_f[0:6]
                xb = [sb("xb%d" % i, [128, 2, D]) for i in range(2)]
                junk = sb("junk", [128, D], BF16)
                xn = sb("xn", [128, 2, D], BF16)
                hT = sb("hT", [128, 8, 256], BF16)
                gT = sb("gT", [128, NFF, 256], BF16)
                ss = sb("ss", [128, 4])
                tmp4 = sb("tmp4", [128, 4])
                rstd = sb("rstd", [128, 4])
                ca = [sb("ca%d" % i, [128, 258]) for i in range(2)]
                acc = [sb("acc%d" % i, [128, 256]) for i in range(2)]
                ge = [sb("ge%d" % i, [128, 256]) for i in range(2)]
                halo = sb("halo", [128, NFF, 2])
                k.memset("pool", halo[:], 0.0, (halo,))
                o_ss = sb("o_ss", [128, 4])
                o_tm = sb("o_tm", [128, 4])
                o_rs = sb("o_rs", [128, 4])
                o_t = [sb("o_t%d" % i, [128, 512]) for i in range(2)]
                fcw = P("fcw").rearrange("p (c j) -> p c j", c=NFF)
                if l == 0:
                    print("[build] phase B sbuf bytes remaining/partition:", nc.sbuf_bytes_remaining)
                if skew and l > 0:
                    k.dma(o_t[0][:, 0:44], st_all[0:128, 786:830], r=(stb_all,), w=(o_t[0],))
                    k.ts("dve", halo[:].rearrange("p c j -> p (c j)"), o_t[0][:, 0:44], flg[:, 0:1], None, ALU.mult, None,
                         (o_t[0], flg), (halo,))

                def ldx(i):
                    for s in range(2):
                        ti = 2 * i + s
                        k.dma(xb[i % 2][:, s, :], dx_m.ap[ti * 128:(ti + 1) * 128, :], r=(dx_m.tiles[ti],), w=(xb[i % 2],))

                def prenorm1(xx):
                    for s in range(2):
                        k.act(junk[:], xx[:, s, :], AF.Square, (xx,), (junk, ss), accum_out=ss[:, s:s + 1])
                    rstd_from(ss, D, 2, tmp4, rstd)
                    for s in range(2):
                        k.ts("dve", xn[:, s, :], xx[:, s, :], rstd[:, s:s + 1], None, ALU.mult, None, (xx, rstd), (xn,))

                def prenorm2():
                    for s in range(2):
                        pT = k.psb()
                        for kc in range(8):
                            k.tr(pT[:, kc * 128:(kc + 1) * 128], xn[:, s, kc * 128:(kc + 1) * 128], ident[:],
                                 (xn, ident), (pT,), inc=(kc == 7))
                        k.tt("dve", hT[:, :, s * 128:(s + 1) * 128], pT[:, :].rearrange("p (c t) -> p c t", c=8),
                             bc(P("gpre").unsqueeze(2), [128, 8, 128]), ALU.mult, (pT, lp), (hT,))

                ldx(0)
                if NTB > 1:
                    ldx(1)
                prenorm1(xb[0])
                prenorm2()
                for kc in range(8):
                    for q8 in range(8):
                        load_cast(w_up, w_up[:, kc, q8 * 704:(q8 + 1) * 704],
                                  w_up_d[l, kc * 128:(kc + 1) * 128, q8 * 704:(q8 + 1) * 704], 704)
                for c in range(NFF):
                    for hf in range(2):
                        load_cast(w_dn, w_dn[:, c, hf * 512:(hf + 1) * 512],
                                  w_dn_d[l, c * 128:(c + 1) * 128, hf * 512:(hf + 1) * 512], 512)
                for i in range(NTB):
                    x_t = xb[i % 2]
                    pend = [None]
                    for c in range(NFF):
                        ps = k.ps()
                        for kc in range(8):
                            k.mm(ps[:, 0:256], w_up[:, kc, c * 128:(c + 1) * 128], hT[:, kc, :], kc == 0, kc == 7,
                                 (w_up, hT), (ps,), inc=False)
                        for kc in range(8):
                            k.mm(ps[:, 256:512], w_up[:, kc, DFF + c * 128:DFF + (c + 1) * 128], hT[:, kc, :], kc == 0, kc == 7,
                                 (w_up, hT), (ps,), inc=(kc == 7))
                        ca_, acc_, ge_ = ca[c % 2], acc[c % 2], ge[c % 2]
                        k.cp("pool", ca_[:, 0:2], halo[:, c, :], (halo,), (ca_,))
                        k.cp("act", ca_[:, 2:258], ps[:, 0:256], (ps,), (ca_,))
                        k.cp("pool", halo[:, c, :], ca_[:, 256:258], (ca_,), (halo,))
                        k.ts("dve", acc_[:], ca_[:, 0:256], fcw[:, c, 0:1], P("fcb", c, c + 1), ALU.mult, ALU.add, (ca_, lp), (acc_,))
                        k.stt("dve", acc_[:], ca_[:, 1:257], fcw[:, c, 1:2], acc_[:], ALU.mult, ALU.add, (ca_, lp, acc_), (acc_,))
                        k.stt("dve", acc_[:], ca_[:, 2:258], fcw[:, c, 2:3], acc_[:], ALU.mult, ALU.add, (ca_, lp, acc_), (acc_,))
                        if pend[0] is not None:
                            pend[0]()

                        def fin(c=c, ps=ps, acc_=acc_, ge_=ge_):
                            k.act(ge_[:], acc_[:], AF.Gelu_apprx_tanh, (acc_,), (ge_,))
                            k.tt("dve", gT[:, c, :], ps[:, 256:512], ge_[:], ALU.mult, (ps, ge_), (gT,))
                        pend[0] = fin
                    pend[0]()
                    pend[0] = None
                    if i + 1 < NTB:
                        prenorm1(xb[(i + 1) % 2])
                    for s in range(2):
                        if s == 1 and i + 1 < NTB:
                            prenorm2()
                        psY = [k.ps(), k.ps()]
                        for n in range(2):
                            for c in range(NFF):
                                k.mm(psY[n][:, :], gT[:, c, s * 128:(s + 1) * 128], w_dn[:, c, n * 512:(n + 1) * 512], c == 0,
                                     c == NFF - 1, (gT, w_dn), (psY[n],), inc=(c == NFF - 1))
                        for n in range(2):
                            k.act(junk[:, n * 512:(n + 1) * 512], psY[n][:, :], AF.Square, (psY[n],), (junk, o_ss),
                                  accum_out=o_ss[:, n:n + 1])
                        k.tt("pool", o_ss[:, 2:3], o_ss[:, 0:1], o_ss[:, 1:2], ALU.add, (o_ss,), (o_ss,))
                        k.ts("pool", o_tm[:, 0:1], o_ss[:, 2:3], 1.0 / D, EPS, ALU.mult, ALU.add, (o_ss,), (o_tm,))
                        k.tt("pool", o_rs[:, 0:1], o_tm[:, 0:1], cs("nhalf", 0, 1), ALU.pow, (o_tm, cst), (o_rs,))
                        if skew:
                            k.tt("pool", o_rs[:, 0:1], o_rs[:, 0:1], flg[:, 1 + l:2 + l], ALU.mult, (o_rs, flg), (o_rs,))
                        for n in range(2):
                            k.stt("dve", o_t[n][:], psY[n][:, :], o_rs[:, 0:1], P("gpost", n * 512, (n + 1) * 512), ALU.mult, ALU.mult,
                                  (psY[n], o_rs, lp), (o_t[n],))
                            k.tt("pool", x_t[:, s, n * 512:(n + 1) * 512], o_t[n][:], x_t[:, s, n * 512:(n + 1) * 512], ALU.add,
                                 (o_t[n], x_t), (x_t,))
                        ti = 2 * i + s
                        k.dma(dst.ap[ti * 128:(ti + 1) * 128, :], x_t[:, s, :], r=(x_t,), w=(dst.tiles[ti],))
                    if i + 2 < NTB:
                        ldx(i + 2)
                if skew and l < L - 1:
                    k.dma(st_out[:, 786:830], halo[:].rearrange("p c j -> p (c j)"), r=(halo,), w=(stb_out,))
                    k.collective(GROUPS, st_out_t.ap().opt(), st_all_t.ap().opt(), r=(stb_out,), w=(stb_all,))

        for q in ("sp", "act"):
            i = k.dma_i[q]
            n = k.ndma[q]
            for j in range(n):
                cntj = (i - j + n - 1) // n if i > j else 0
                if cntj > 0:
                    k._wait("sp", ("dma", q, j), 16 * cntj)
        print("[build] instructions=%d waits=%d" % (k.nins, k.nwait))
    return nc


def make_inmaps(inputs, T_, L, batches):
    cst, ropet, pmat = host_consts(T_)
    lpa = np.stack([host_params(inputs, l)[0] for l in range(L)])
    lpb = np.stack([host_params(inputs, l)[1] for l in range(L)])
    f = lambda a: np.ascontiguousarray(np.asarray(a, np.float32))
    flags = np.ones((128, 8), np.float32)
    flags[:, 0] = 0.0
    maps = []
    for b in batches:
        maps.append({
            "x": f(inputs["x"][b, :T_]),
            "w_in": f(inputs["w_in"][:L]), "w_out": f(inputs["w_out"][:L]),
            "w_up": f(inputs["ffn_w_up"][:L]), "w_down": f(inputs["ffn_w_down"][:L]),
            "cst": cst, "ropet": ropet, "pmat": pmat, "lpa": lpa, "lpb": lpb, "flags": flags,
        })
    return maps


def make_inmaps_skew(inputs, S, L):
    Th = S // 2
    f = lambda a: np.ascontiguousarray(np.asarray(a, np.float32))
    lp = [host_params(inputs, l) for l in range(L)]
    maps = []
    B = inputs["x"].shape[0]
    slots_h = ([min(s_, L - 1) for s_ in range(L + 1)], [max(s_ - 1, 0) for s_ in range(L + 1)])
    cache = {}
    for h in range(2):
        cst, ropet, pm = host_consts(Th, pos0=h * Th)
        pmat = pm.copy()
        if h == 1:
            pmat[:, 0:512] = pm[:, 512:1024]
        sl = slots_h[h]
        flags = np.zeros((128, 8), np.float32)
        flags[:, 0] = float(h)
        for s_ in range(L + 1):
            flags[:, 1 + s_] = 1.0 if (s_ < L if h == 0 else s_ >= 1) else 0.0
        cache[h] = dict(cst=cst, ropet=ropet, pmat=pmat, flags=flags,
                        w_in=f(inputs["w_in"][sl]), w_out=f(inputs["w_out"][sl]),
                        w_up=f(inputs["ffn_w_up"][sl]), w_down=f(inputs["ffn_w_down"][sl]),
                        lpa=np.stack([lp[l][0] for l in sl]), lpb=np.stack([lp[l][1] for l in sl]))
    for c in range(2 * B):
        b, h = c // 2, c % 2
        m = dict(cache[h])
        m["x"] = f(inputs["x"][b, h * Th:(h + 1) * Th])
        maps.append(m)
    return maps


def kernel(**inputs):
    inputs = {k_: np.asarray(v) for k_, v in inputs.items()}
    B, S, _ = inputs["x"].shape
    L = inputs["w_in"].shape[0]
    Th = S // 2
    nc = build(Th, L + 1, skew=True)
    maps = make_inmaps_skew(inputs, S, L)
    res = run_bass_kernel_spmd(nc, maps, core_ids=list(range(2 * B)))
    out = np.zeros((B, S, D), np.float32)
    for c in range(2 * B):
        b, h = c // 2, c % 2
        out[b, h * Th:(h + 1) * Th] = np.asarray(res.results[c]["out"]).reshape(Th, D)
    return out
```

```python
import numpy as np
from contextlib import ExitStack
import concourse.bass as bass
import concourse.mybir as mybir
from concourse.bass_utils import run_bass_kernel_spmd

F32 = mybir.dt.float32
BF16 = mybir.dt.bfloat16
AF = mybir.ActivationFunctionType
ALU = mybir.AluOpType

D = 1024
PIN = 2824
DFF = 2816
EPS = 1e-6
NFF = DFF // 128


class Buf:
    __slots__ = ("w", "r", "psum", "name")

    def __init__(self, name, psum=False):
        self.w = None
        self.r = {}
        self.psum = psum
        self.name = name


class T:
    __slots__ = ("t", "b")

    def __init__(self, t, b):
        self.t = t
        self.b = b

    def __getitem__(self, key):
        return self.t[key]


class K:
    def __init__(self, nc, es):
        self.nc = nc
        self.es = es
        self.E = {"pe": nc.tensor, "dve": nc.vector, "act": nc.scalar, "pool": nc.gpsimd, "sp": nc.sync}
        self.semh = {}
        self.cnt = {}
        for e in ("pe", "dve", "act", "pool"):
            self.semh[e] = es.enter_context(nc.semaphore("s_" + e))
            self.cnt[e] = 0
        self.seen = {e: {} for e in self.E}
        self.ndma = {"sp": 8, "act": 4}
        self.dma_i = {"sp": 0, "act": 0}
        for q, n in self.ndma.items():
            for j in range(n):
                self.semh[("dma", q, j)] = es.enter_context(nc.semaphore("d_%s%d" % (q, j)))
        self.ps_f = []
        self.ps_b = []
        self.ps_fi = 0
        self.ps_bi = 0
        self.nwait = 0
        self.nins = 0

    def sb(self, name, shape, dt, es=None):
        self.nsb = getattr(self, "nsb", 0) + 1
        name = "sb%d_%s" % (self.nsb, name)
        t = (es or self.es).enter_context(self.nc.sbuf_tensor(name, list(shape), dt))
        return T(t, Buf(name))

    def init_psum(self, nf=6, nb=2):
        for i in range(nf):
            t = self.es.enter_context(self.nc.psum_tensor("psf%d" % i, [128, 512], F32))
            self.ps_f.append(T(t, Buf("psf%d" % i, psum=True)))
        for i in range(nb):
            t = self.es.enter_context(self.nc.psum_tensor("psb%d" % i, [128, 1024], BF16))
            self.ps_b.append(T(t, Buf("psb%d" % i, psum=True)))
        self.work = self.ps_f[3:6]
        self.proj = self.ps_f[0:2]
        self.pj_i = 0
        self.dedicated = self.ps_f[2]

    def ps(self):
        p = self.work[self.ps_fi % len(self.work)]
        self.ps_fi += 1
        return p

    def psj(self):
        p = self.proj[self.pj_i % 2]
        self.pj_i += 1
        return p

    def psb(self):
        p = self.ps_b[self.ps_bi % len(self.ps_b)]
        self.ps_bi += 1
        return p

    def _wait(self, eng, key, val):
        if self.seen[eng].get(key, 0) >= val:
            return
        self.E[eng].wait_ge(self.semh[key], val)
        self.seen[eng][key] = val
        self.nwait += 1

    def _deps(self, eng, reads, writes):
        deps = {}

        def add(tok):
            if tok is None:
                return
            k_, v = tok
            if deps.get(k_, 0) < v:
                deps[k_] = v

        for t in reads:
            b = t.b
            add(b.w)
            if b.psum:
                for k_, v in b.r.items():
                    add((k_, v))
        for t in writes:
            b = t.b
            add(b.w)
            for k_, v in b.r.items():
                add((k_, v))
        for k_, v in deps.items():
            if eng == "pe" and k_ == "pe":
                continue
            self._wait(eng, k_, v)

    def _mark(self, tok, reads, writes):
        k_, v = tok
        for t in reads:
            b = t.b
            if b.psum:
                b.w = tok
                b.r = {}
            else:
                if b.r.get(k_, 0) < v:
                    b.r[k_] = v
        for t in writes:
            b = t.b
            b.w = tok
            b.r = {}

    def op(self, eng, fn, r=(), w=(), inc=True):
        self._deps(eng, r, w)
        ins = fn(self.E[eng])
        self.nins += 1
        if inc:
            self.cnt[eng] += 1
            ins.then_inc(self.semh[eng], 1)
            tok = (eng, self.cnt[eng])
        else:
            tok = (eng, self.cnt[eng] + 1)
        self._mark(tok, r, w)
        return ins

    def dma(self, out, in_, r=(), w=(), q="sp"):
        i = self.dma_i[q]
        n = self.ndma[q]
        j = i % n
        key = ("dma", q, j)
        if i >= n:
            self._wait(q, key, 16 * (i // n))
        self._deps(q, r, w)
        ins = self.E[q].dma_start(out=out, in_=in_)
        ins.then_inc(self.semh[key], 16)
        self.dma_i[q] = i + 1
        self.nins += 1
        tok = (key, 16 * (i // n + 1))
        self._mark(tok, r, w)
        return tok

    def collective(self, groups, ins_ap, outs_ap, r, w):
        if "cc" not in self.semh:
            self.semh["cc"] = self.es.enter_context(self.nc.semaphore("s_cc"))
            self.ncc = 0
        self._deps("pool", r, w)
        ins = self.nc.gpsimd.collective_compute("AllGather", ALU.bypass, replica_groups=groups,
                                                ins=[ins_ap], outs=[outs_ap])
        self.ncc += 1
        ins.then_inc(self.semh["cc"], 1)
        self.nins += 1
        self._mark(("cc", self.ncc), r, w)

    def barrier(self):
        toks = [(e, self.cnt[e]) for e in ("pe", "dve", "act", "pool") if self.cnt[e] > 0]
        for q, n in self.ndma.items():
            i = self.dma_i[q]
            for j in range(n):
                cj = (i - j + n - 1) // n if i > j else 0
                if cj > 0:
                    toks.append((("dma", q, j), 16 * cj))
        for eng in ("pe", "dve", "act", "pool", "sp"):
            for key, val in toks:
                if key != eng:
                    self._wait(eng, key, val)

    def mm(self, out, lhsT, rhs, start, stop, r, w, inc):
        return self.op("pe", lambda e: e.matmul(out, lhsT=lhsT, rhs=rhs, start=start, stop=stop,
                                                skip_group_check=True), r, w, inc=inc)

    def tr(self, out, in_, ident, r, w, inc):
        return self.op("pe", lambda e: e.transpose(out, in_, ident), r, w, inc=inc)

    def tt(self, eng, out, in0, in1, op, r, w):
        return self.op(eng, lambda e: e.tensor_tensor(out=out, in0=in0, in1=in1, op=op), r, w)

    def ts(self, eng, out, in0, s1, s2, op0, op1, r, w):
        if s2 is None:
            return self.op(eng, lambda e: e.tensor_scalar(out=out, in0=in0, scalar1=s1, scalar2=None, op0=op0), r, w)
        return self.op(eng, lambda e: e.tensor_scalar(out=out, in0=in0, scalar1=s1, scalar2=s2, op0=op0, op1=op1), r, w)

    def stt(self, eng, out, in0, scalar, in1, op0, op1, r, w):
        return self.op(eng, lambda e: e.scalar_tensor_tensor(out=out, in0=in0, scalar=scalar, in1=in1,
                                                             op0=op0, op1=op1), r, w)

    def act(self, out, in_, func, r, w, **kw):
        return self.op("act", lambda e: e.activation(out=out, in_=in_, func=func, **kw), r, w)

    def cp(self, eng, out, in_, r, w):
        if eng == "act":
            return self.op("act", lambda e: e.activation(out=out, in_=in_, func=AF.Copy), r, w)
        return self.op(eng, lambda e: e.tensor_copy(out=out, in_=in_), r, w)

    def memset(self, eng, ap, val, w):
        return self.op(eng, lambda e: e.memset(ap, val), (), w)


def bc(ap, shape):
    return ap.to_broadcast(list(shape))


def const_layout(T_):
    NT = T_ // 128
    off = {}
    c = 0
    for name, n in (("ident", 128), ("triu", 128), ("negmask", 128), ("strict", 128), ("ones", 128),
                    ("blk", 128), ("dq", 4), ("dk", 4), ("g128", 4), ("nhalf", 8)):
        off[name] = (c, n)
        c += n
    return off, c


def rope_layout(T_):
    NT = T_ // 128
    return {"cos": (0, NT * 32), "sin": (NT * 32, NT * 32), "nsin": (2 * NT * 32, NT * 32)}, 3 * NT * 32


PM_OFF = {"ptf": (0, 512), "ptr": (512, 512), "ptp": (1024, 512)}
PM_DEV = {"pt0": (0, 512), "ptr": (512, 512), "ptp": (1024, 512)}
NST = 830
GROUPS = [[0, 1], [2, 3], [4, 5], [6, 7]]


def host_consts(T_, pos0=0):
    off, n = const_layout(T_)
    roff, rn = rope_layout(T_)
    NT = T_ // 128
    C = np.zeros((128, n), np.float32)
    R = np.zeros((128, rn), np.float32)
    PM = np.zeros((128, 1536), np.float32)

    def put(name, arr):
        if name in off:
            o, m = off[name]
            C[:, o:o + m] = np.asarray(arr, np.float32).reshape(128, m)
        elif name in roff:
            o, m = roff[name]
            R[:, o:o + m] = np.asarray(arr, np.float32).reshape(128, m)
        else:
            o, m = PM_OFF[name]
            PM[:, o:o + m] = np.asarray(arr, np.float32).reshape(128, m)

    p = np.arange(128)
    put("ident", np.eye(128))
    put("triu", (p[:, None] <= p[None, :]))
    put("negmask", np.where(p[:, None] >= p[None, :], 0.0, 1e30))
    put("strict", (p[:, None] > p[None, :]))
    put("ones", np.ones((128, 128)))
    put("blk", (p[:, None] // 64 == p[None, :] // 64))
    inv = (1.0 / (np.float32(10000.0) ** (np.arange(0, 64, 2, dtype=np.float32) / np.float32(64)))).astype(np.float32)
    pos = np.arange(pos0, pos0 + T_).astype(np.float32)
    ang = (pos[:, None] * inv[None, :]).astype(np.float32)
    cos = np.cos(ang).astype(np.float32).reshape(NT, 128, 32).transpose(1, 0, 2)
    sin = np.sin(ang).astype(np.float32).reshape(NT, 128, 32).transpose(1, 0, 2)
    put("cos", cos)
    put("sin", sin)
    put("nsin", -sin)
    lg = np.log(1.0 - 2.0 ** (-5.0 - np.arange(4, dtype=np.float64)))
    put("dq", np.exp(lg[None, :] * (p[:, None] + 1.0)))
    put("dk", np.exp(-lg[None, :] * (p[:, None] + 1.0)) * 0.125)
    put("g128", np.broadcast_to(np.exp(lg * 128.0)[None, :], (128, 4)))
    put("nhalf", np.full((128, 8), -0.5))
    wins = (2, 4, 8, 16)
    ptf = np.zeros((128, 4, 128))
    ptr = np.zeros((128, 4, 128))
    ptp = np.zeros((128, 4, 128))
    for g, w in enumerate(wins):
        for t in range(128):
            for s in range(max(0, t - w + 1), t + 1):
                ptf[s, g, t] += 1.0 / min(t + 1, w)
                ptr[s, g, t] += 1.0 / w
            ptf[t, g, t] -= 1.0
            ptr[t, g, t] -= 1.0
            for srel in range(t - w + 1, 0):
                ptp[128 + srel, g, t] += 1.0 / w
    put("ptf", ptf)
    put("ptr", ptr)
    put("ptp", ptp)
    return C, R, PM


LP_A = (("gpre", 8), ("gpost", 1024), ("poolw", 512), ("poolsc", 4), ("lng", 256), ("lnb", 256),
        ("wsT", 512), ("bs", 4), ("convw", 24), ("alog", 4), ("dtb", 4), ("normg", 64))
LP_B = (("gpre", 8), ("gpost", 1024), ("fcw", 66), ("fcb", 22))


def lay(spec):
    off = {}
    c = 0
    for name, n in spec:
        off[name] = (c, n)
        c += n
    return off, c


def host_params(inp, l):
    offA, nA = lay(LP_A)
    offB, nB = lay(LP_B)
    A = np.zeros((128, nA), np.float32)
    B = np.zeros((128, nB), np.float32)

    def put(M, off, name, arr):
        o, m = off[name]
        M[:, o:o + m] = np.asarray(arr, np.float32).reshape(128, m)

    def rep(v):
        return np.broadcast_to(np.asarray(v, np.float32).reshape(1, -1), (128, np.asarray(v).size))

    put(A, offA, "gpre", inp["norm_pre_mix"][l].reshape(8, 128).T)
    put(A, offA, "gpost", rep(inp["norm_post_mix"][l]))
    pw = np.zeros((128, 4, 128), np.float32)
    for g in range(4):
        pw[:64, g, :64] = inp["pool_w"][l][g]
    put(A, offA, "poolw", pw)
    psc = np.zeros((128, 4), np.float32)
    psc[:64, :] = inp["pool_scale"][l].reshape(4, 64).T
    put(A, offA, "poolsc", psc)
    put(A, offA, "lng", rep(inp["sgu_ln_g"][l]))
    put(A, offA, "lnb", rep(inp["sgu_ln_b"][l]))
    put(A, offA, "wsT", inp["sgu_ws"][l].transpose(2, 0, 1))
    put(A, offA, "bs", inp["sgu_bs"][l].T)
    put(A, offA, "convw", inp["dn_conv_w"][l].reshape(4, 6, 128).transpose(2, 1, 0))
    put(A, offA, "alog", rep(inp["dn_a_log"][l]))
    put(A, offA, "dtb", rep(inp["dn_dt_bias"][l]))
    put(A, offA, "normg", rep(inp["dn_norm_g"][l]))
    put(B, offB, "gpre", inp["norm_pre_ffn"][l].reshape(8, 128).T)
    put(B, offB, "gpost", rep(inp["norm_post_ffn"][l]))
    put(B, offB, "fcw", inp["ffn_conv_w"][l].reshape(3, NFF, 128).transpose(2, 1, 0))
    put(B, offB, "fcb", inp["ffn_conv_b"][l].reshape(NFF, 128).T)
    return A, B


def build(T_, L, mixers=("pool", "ret", "sgu", "dn"), do_ffn=True, skew=False):
    NT = T_ // 128
    nc = bass.Bass("TRN2", target_bir_lowering=False)
    coff, ncst = const_layout(T_)
    offA, nA = lay(LP_A)
    offB, nB = lay(LP_B)

    x_in = nc.dram_tensor("x", [T_, D], F32, kind="ExternalInput").ap()
    w_in_d = nc.dram_tensor("w_in", [L, D, PIN], F32, kind="ExternalInput").ap()
    w_out_d = nc.dram_tensor("w_out", [L, D, D], F32, kind="ExternalInput").ap()
    w_up_d = nc.dram_tensor("w_up", [L, D, 2 * DFF], F32, kind="ExternalInput").ap()
    w_dn_d = nc.dram_tensor("w_down", [L, DFF, D], F32, kind="ExternalInput").ap()
    cst_d = nc.dram_tensor("cst", [128, ncst], F32, kind="ExternalInput").ap()
    roff, nrope = rope_layout(T_)
    rope_d = nc.dram_tensor("ropet", [128, nrope], F32, kind="ExternalInput").ap()
    pm_d = nc.dram_tensor("pmat", [128, 1536], F32, kind="ExternalInput").ap()
    lpa_d = nc.dram_tensor("lpa", [L, 128, nA], F32, kind="ExternalInput").ap()
    lpb_d = nc.dram_tensor("lpb", [L, 128, nB], F32, kind="ExternalInput").ap()
    flg_d = nc.dram_tensor("flags", [128, 8], F32, kind="ExternalInput").ap()
    st_out_t = nc.dram_tensor("st_out", [128, NST], F32)
    st_all_t = nc.dram_tensor("st_all", [256, NST], F32)
    st_out, st_all = st_out_t.ap(), st_all_t.ap()
    stb_out = T(None, Buf("st_out"))
    stb_all = T(None, Buf("st_all"))
    out_d = nc.dram_tensor("out", [T_, D], F32, kind="ExternalOutput").ap()
    xm_d = nc.dram_tensor("xm_scr", [T_, D], F32, kind="Internal").ap()
    xs_d = nc.dram_tensor("xs_scr", [T_, D], F32, kind="Internal").ap()

    class DT:
        def __init__(self, name, ap):
            self.ap = ap
            self.tiles = [T(None, Buf("%s%d" % (name, i))) for i in range(NT)]

    dx_in, dx_m, dx_s, dx_o = DT("xin", x_in), DT("xm", xm_d), DT("xs", xs_d), DT("xo", out_d)
    wdram = T(None, Buf("wdram"))

    with ExitStack() as es:
        es.enter_context(nc.allow_low_precision("bf16 matmul operands, fp32 accumulation"))
        k = K(nc, es)
        k.init_psum(6, 2)

        cst = k.sb("cst", [128, ncst], F32)
        k.dma(cst[:], cst_d, r=(), w=(cst,))

        def cs(name, a=None, b=None):
            o, n = coff[name]
            if a is None:
                return cst[:, o:o + n]
            return cst[:, o + a:o + b]

        flg = k.sb("flg", [128, 8], F32)
        k.dma(flg[:], flg_d, r=(), w=(flg,))
        ident = k.sb("ident", [128, 128], BF16)
        k.cp("dve", ident[:], cs("ident"), (cst,), (ident,))
        blk = k.sb("blk", [128, 128], BF16)
        k.cp("dve", blk[:], cs("blk"), (cst,), (blk,))

        SW = 1412
        stg = []
        stg_i = [0]
        cast_engs = ("dve", "act", "dve")

        def load_cast(dst_t, dst_ap, src_ap, ncols, nrows=128):
            i = stg_i[0]
            stg_i[0] += 1
            s = stg[i % len(stg)]
            k.dma(s[0:nrows, 0:ncols], src_ap, r=(wdram,), w=(s,))
            k.cp(cast_engs[i % 3], dst_ap, s[0:nrows, 0:ncols], (s,), (dst_t,))

        def rstd_from(ss, n, ncol, tmp, out, f=1.0):
            k.ts("pool", tmp[:, 0:ncol], ss[:, 0:ncol], 1.0 / (n * f * f), EPS / (f * f), ALU.mult, ALU.add, (ss,), (tmp,))
            k.tt("pool", out[:, 0:ncol], tmp[:, 0:ncol], cs("nhalf", 0, ncol), ALU.pow, (tmp, cst), (out,))

        def rstd_act(ss, n, ncol, tmp, out, f=1.0):
            k.act(tmp[:, 0:ncol], ss[:, 0:ncol], AF.Ln, (ss,), (tmp,), scale=1.0 / (n * f * f), bias=EPS / (f * f))
            k.act(out[:, 0:ncol], tmp[:, 0:ncol], AF.Exp, (tmp,), (out,), scale=-0.5)

        for l in range(L):
            src = dx_in if l == 0 else dx_s
            dst = dx_o if l == L - 1 else dx_s
            k.barrier()
            with ExitStack() as pa:
                k.work = k.ps_f[3:6]

                def sb(name, shape, dt=F32):
                    return k.sb("a_" + name, shape, dt, es=pa)

                stg[:] = [sb("stg%d" % i_, [128, 706], F32) for i_ in range(4)]
                w_in = sb("w_in", [128, 8, PIN], BF16)
                wo_pool = sb("wo_pool", [128, 4, D], BF16)
                wo = sb("wo", [128, 6, D], BF16)
                lp = sb("lp", [128, nA], F32)
                k.dma(lp[:], lpa_d[l], r=(), w=(lp,))
                ropet = sb("ropet", [128, nrope], F32)
                k.dma(ropet[:], rope_d, r=(), w=(ropet,))

                def rp(name, a, b):
                    o, n = roff[name]
                    return ropet[:, o + a:o + b]

                ptf = sb("ptf", [128, 512], BF16)
                ptr = sb("ptr", [128, 512], BF16)
                ptp = sb("ptp", [128, 512], BF16)
                for nm_, t_ in (("pt0", ptf), ("ptr", ptr), ("ptp", ptp)):
                    o_, n_ = PM_DEV[nm_]
                    load_cast(t_, t_[:, :], pm_d[:, o_:o_ + n_], n_)

                def P(name, a=None, b=None):
                    o, n = offA[name]
                    if a is None:
                        return lp[:, o:o + n]
                    return lp[:, o + a:o + b]


                poolw = sb("poolw", [128, 4, 128], BF16)
                k.cp("dve", poolw[:].rearrange("p g d -> p (g d)"), P("poolw"), (lp,), (poolw,))
                wmT = sb("wmT", [128, 4, 128], BF16)
                k.tt("dve", wmT[:], P("wsT").rearrange("p (g t) -> p g t", g=4),
                     bc(cs("triu").unsqueeze(1), [128, 4, 128]), ALU.mult, (lp, cst), (wmT,))
                nexpA = sb("nexpA", [128, 4])
                k.act(nexpA[:], P("alog"), AF.Exp, (lp,), (nexpA,))
                k.ts("dve", nexpA[:], nexpA[:], -1.0, None, ALU.mult, None, (nexpA,), (nexpA,))

                xt = [sb("xt%d" % i, [128, D]) for i in range(2)]
                xmo0 = sb("xmo0", [128, D])
                xmo = [xmo0, xmo0]
                junk = sb("junk", [128, D], BF16)
                xn = sb("xn", [128, D], BF16)
                hT = sb("hT", [128, 8, 128], BF16)
                ss = sb("ss", [128, 4])
                tmp4 = sb("tmp4", [128, 4])
                rstd = sb("rstd", [128, 4])
                a_bf = [sb("a_bf%d" % i, [128, 256], BF16) for i in range(2)]
                dT_bf = sb("dT_bf", [128, 4, 128], BF16)
                yaT = sb("yaT", [128, 4, 128], BF16)
                k.memset("pool", dT_bf[:], 0.0, (dT_bf,))
                k.memset("pool", yaT[:], 0.0, (yaT,))
                yT = sb("yT", [128, 6, 128], BF16)
                k.memset("pool", yT[:], 0.0, (yT,))
                gu_2 = [sb("gu%d" % i_, [128, 256]) for i_ in range(2)]
                gv_2 = [sb("gv%d" % i_, [128, 256]) for i_ in range(2)]
                vln = sb("vln", [128, 256])
                vln2 = vln
                vln_bf = sb("vln_bf", [128, 256], BF16)
                s_sum = sb("s_sum", [128, 4])
                s_nm = sb("s_nm", [128, 4])
                s_ss = sb("s_ss", [128, 4])
                s_tmp = sb("s_tmp", [128, 4])
                s_rs = sb("s_rs", [128, 4])
                yc_t = sb("yc_t", [128, 256])
                yc_bf = sb("yc_bf", [128, 256], BF16)
                r_qd_2 = [sb("r_qd%d" % i_, [128, 256]) for i_ in range(2)]
                r_t1 = sb("r_t1", [128, 256])
                r_t2 = sb("r_t2", [128, 256])
                r_kd_2 = [sb("r_kd%d" % i_, [128, 256]) for i_ in range(2)]
                r_u1 = r_t1
                r_u2 = r_t2
                r_q_bf = sb("r_q_bf", [128, 256], BF16)
                r_kz = sb("r_kz", [128, 4, 128], BF16)
                r_v_bf_2 = [sb("r_v_bf%d" % i_, [128, 256], BF16) for i_ in range(2)]
                r_sg_2 = [sb("r_sg%d" % i_, [128, 256]) for i_ in range(2)]
                r_qT = sb("r_qT", [128, 2, 128], BF16)
                r_kTz = sb("r_kTz", [128, 4, 128], BF16)
                r_scm = sb("r_scm", [128, 4, 128], BF16)
                r_S = sb("r_S", [128, 2, 128])
                r_S_bf = sb("r_S_bf", [128, 2, 128], BF16)
                r_St = sb("r_St", [128, 2, 128])
                r_sso = sb("r_sso", [128, 4])
                r_tm = sb("r_tm", [128, 4])
                r_rs = sb("r_rs", [128, 4])
                r_y = sb("r_y", [128, 256])
                r_y_bf = sb("r_y_bf", [128, 256], BF16)
                k.memset("pool", r_kz[:], 0.0, (r_kz,))
                k.memset("pool", r_kTz[:], 0.0, (r_kTz,))
                k.memset("pool", r_S[:], 0.0, (r_S,))
                k.memset("pool", r_S_bf[:], 0.0, (r_S_bf,))
                for t_ in a_bf:
                    k.memset("pool", t_[:], 0.0, (t_,))
                d_xc = sb("d_xc", [128, 6, 131])
                d_acc = sb("d_acc", [128, 6, 128])
                d_tmp = sb("d_tmp", [128, 6, 128])
                d_qkv = d_acc
                d_sq = sb("d_sq", [128, 4, 128], BF16)
                d_qkn = sb("d_qkn", [128, 4, 128], BF16)
                d_qz = sb("d_qz", [128, 4, 128], BF16)
                d_kz = sb("d_kz", [128, 4, 128], BF16)
                d_vT = sb("d_vT", [128, 2, 128], BF16)
                d_sc = sb("d_sc", [128, 32])
                d_ab_2 = [sb("d_ab%d" % i_, [128, 8]) for i_ in range(2)]
                d_g = sb("d_g", [128, 4])
                d_hb = sb("d_hb", [128, 4])
                d_Ug = sb("d_Ug", [128, 4, 128])
                d_rn = d_Ug
                d_dec = sb("d_dec", [128, 8])
                d_arg = sb("d_arg", [128, 4, 128])
                d_E = d_arg
                d_Eb = sb("d_Eb", [128, 4, 128])
                d_eR = sb("d_eR", [128, 4, 128])
                d_qd_2 = [sb("d_qd%d" % i_, [128, 2, 128], BF16) for i_ in range(2)]
                d_L0_2 = [sb("d_L0%d" % i_, [128, 4, 128], BF16) for i_ in range(2)]
                d_Z0_2 = [sb("d_Z0%d" % i_, [128, 4, 128], BF16) for i_ in range(2)]
                d_edl_2 = [sb("d_edl%d" % i_, [128, 4]) for i_ in range(2)]
                d_Pt = sb("d_Pt", [128, 4, 128], BF16)
                d_Rp = sb("d_Rp", [128, 4, 128], BF16)
                d_Dl = sb("d_Dl", [128, 4, 128], BF16)
                d_L = [sb("d_L%d" % i, [128, 4, 128], BF16) for i in range(2)]
                d_Z = [sb("d_Z%d" % i, [128, 4, 128], BF16) for i in range(2)]
                d_P0_2 = [sb("d_P0%d" % i_, [128, 4, 128], BF16) for i_ in range(2)]
                d_Pb = [sb("d_Pb%d" % i, [128, 4, 128], BF16) for i in range(2)]
                d_at = sb("d_at", [128, 4, 128], BF16)
                d_atT_2 = [sb("d_atT%d" % i_, [128, 4, 128], BF16) for i_ in range(2)]
                d_kbz_2 = [sb("d_kbz%d" % i_, [128, 4, 128], BF16) for i_ in range(2)]
                d_ktz_2 = [sb("d_ktz%d" % i_, [128, 4, 128], BF16) for i_ in range(2)]
                d_vb_2 = [sb("d_vb%d" % i_, [128, 256], BF16) for i_ in range(2)]
                d_nWT = sb("d_nWT", [128, 2, 128], BF16)
                d_vn = sb("d_vn", [128, 256], BF16)
                d_S = sb("d_S", [128, 2, 128])
                d_St = sb("d_St", [128, 2, 128])
                d_S_bf = sb("d_S_bf", [128, 2, 128], BF16)
                d_sz_2 = [sb("d_sz%d" % i_, [128, 256]) for i_ in range(2)]
                d_szg_2 = [sb("d_szg%d" % i_, [128, 256]) for i_ in range(2)]
                d_sso = sb("d_sso", [128, 4])
                d_tm = sb("d_tm", [128, 4])
                d_rs = sb("d_rs", [128, 4])
                d_y = sb("d_y", [128, 256])
                d_y_bf = sb("d_y_bf", [128, 256], BF16)
                for t_ in (d_qz, d_kz, d_kbz_2[0], d_kbz_2[1], d_ktz_2[0], d_ktz_2[1]):
                    k.memset("pool", t_[:], 0.0, (t_,))
                k.memset("pool", d_xc[:], 0.0, (d_xc,))
                k.memset("pool", d_S[:], 0.0, (d_S,))
                k.memset("pool", d_S_bf[:], 0.0, (d_S_bf,))
                if skew and l > 0:
                    hfl = flg[:, 0:1]
                    k.dma(r_St[:].rearrange("p j x -> p (j x)"), st_all[0:128, 0:256], r=(stb_all,), w=(r_St,))
                    k.ts("dve", r_S[:], r_St[:], hfl, None, ALU.mult, None, (r_St, flg), (r_S,))
                    k.cp("dve", r_S_bf[:], r_S[:], (r_S,), (r_S_bf,))
                    k.dma(d_St[:].rearrange("p j x -> p (j x)"), st_all[0:128, 256:512], r=(stb_all,), w=(d_St,))
                    k.ts("dve", d_S[:], d_St[:], hfl, None, ALU.mult, None, (d_St, flg), (d_S,))
                    k.cp("dve", d_S_bf[:], d_S[:], (d_S,), (d_S_bf,))
                    k.dma(r_y[:], st_all[0:128, 512:768], r=(stb_all,), w=(r_y,))
                    k.ts("dve", a_bf[1][:], r_y[:], hfl, None, ALU.mult, None, (r_y, flg), (a_bf[1],))
                    k.dma(d_y[:, 0:18], st_all[0:128, 768:786], r=(stb_all,), w=(d_y,))
                    k.ts("dve", d_xc[:, :, 0:3], d_y[:, 0:18].rearrange("p (c j) -> p c j", c=6), hfl, None, ALU.mult, None,
                         (d_y, flg), (d_xc,))
                o_ss = sb("o_ss", [128, 4])
                o_tm = sb("o_tm", [128, 4])
                o_rs = sb("o_rs", [128, 4])
                o_t = [sb("o_t%d" % i, [128, 512]) for i in range(2)]

                v4 = lambda ap: ap.rearrange("p (h x) -> p h x", h=4)
                if l == 0:
                    print("[build] phase A sbuf bytes remaining/partition:", nc.sbuf_bytes_remaining)

                def front(i):
                    par = i % 2
                    x_t = xt[par]
                    gu, gv, r_qd, r_kd, r_sg, d_sz = gu_2[par], gv_2[par], r_qd_2[par], r_kd_2[par], r_sg_2[par], d_sz_2[par]
                    r_v_bf, d_ab = r_v_bf_2[par], d_ab_2[par]
                    k.act(junk[:], x_t[:], AF.Square, (x_t,), (junk, ss), accum_out=ss[:, 0:1])
                    rstd_act(ss, D, 1, tmp4, rstd)
                    k.ts("dve", xn[:], x_t[:], rstd[:, 0:1], None, ALU.mult, None, (x_t, rstd), (xn,))
                    pT = k.psb()
                    for kc in range(8):
                        k.tr(pT[:, kc * 128:(kc + 1) * 128], xn[:, kc * 128:(kc + 1) * 128], ident[:],
                             (xn, ident), (pT,), inc=(kc == 7))
                    k.tt("dve", hT[:], pT[:, :].rearrange("p (c t) -> p c t", c=8),
                         bc(P("gpre").unsqueeze(2), [128, 8, 128]), ALU.mult, (pT, lp), (hT,))
                    yield

                    def proj(c0, c1):
                        ps = k.psj()
                        for kc in range(8):
                            k.mm(ps[:, 0:c1 - c0], hT[:, kc, :], w_in[:, kc, c0:c1], kc == 0, kc == 7,
                                 (hT, w_in), (ps,), inc=(kc == 7))
                        return ps

                    ab = a_bf[par]
                    G0 = proj(0, 512)
                    k.cp("act", ab[:], G0[:, 0:256], (G0,), (ab,))
                    k.tt("dve", v4(r_qd[:]), v4(G0[:, 256:512]), bc(cs("dq").unsqueeze(2), [128, 4, 64]),
                         ALU.mult, (G0, cst), (r_qd,))
                    yield
                    G1 = proj(512, 1024)
                    k.tt("dve", v4(r_kd[:]), v4(G1[:, 0:256]), bc(cs("dk").unsqueeze(2), [128, 4, 64]),
                         ALU.mult, (G1, cst), (r_kd,))
                    k.cp("act", r_v_bf[:], G1[:, 256:512], (G1,), (r_v_bf,))
                    yield
                    G3 = proj(1536, 1792)
                    k.act(gv[:], G3[:, 0:256], AF.Gelu_apprx_tanh, (G3,), (gv, s_sum), accum_out=s_sum[:, par:par + 1])
                    G2 = proj(1024, 1536)
                    k.act(gu[:], G2[:, 256:512], AF.Gelu_apprx_tanh, (G2,), (gu,))
                    k.act(r_sg[:], G2[:, 0:256], AF.Tanh, (G2,), (r_sg,), scale=0.5)
                    k.stt("dve", r_sg[:], r_sg[:], 1.0, G2[:, 0:256], ALU.add, ALU.mult, (r_sg, G2), (r_sg,))
                    yield
                    G4 = proj(2560, 2824)
                    k.act(d_sz[:], G4[:, 0:256], AF.Tanh, (G4,), (d_sz,), scale=0.5)
                    k.stt("dve", d_sz[:], d_sz[:], 1.0, G4[:, 0:256], ALU.add, ALU.mult, (d_sz, G4), (d_sz,))
                    k.tt("dve", d_ab[:, 0:4], G4[:, 260:264], P("dtb"), ALU.add, (G4, lp), (d_ab,))
                    k.cp("dve", d_ab[:, 4:8], G4[:, 256:260], (G4,), (d_ab,))
                    yield
                    if "dn" in mixers:
                        psQ = [k.ps(), k.ps()]
                        for fc in range(6):
                            pq = psQ[fc // 4]
                            for kc in range(8):
                                k.mm(pq[:, (fc % 4) * 128:(fc % 4 + 1) * 128],
                                     w_in[:, kc, 1792 + fc * 128:1792 + (fc + 1) * 128], hT[:, kc, :], kc == 0, kc == 7,
                                     (w_in, hT), (pq,), inc=(kc == 7 and fc in (3, 5)))
                        k.cp("act", d_xc[:, 0:4, 3:131], psQ[0][:, :].rearrange("p (c t) -> p c t", c=4), (psQ[0],), (d_xc,))
                        k.cp("act", d_xc[:, 4:6, 3:131], psQ[1][:, 0:256].rearrange("p (c t) -> p c t", c=2), (psQ[1],), (d_xc,))

                def run_all(gens):
                    gens = list(gens)
                    while gens:
                        for g_ in list(gens):
                            try:
                                next(g_)
                            except StopIteration:
                                gens.remove(g_)

                k.dma(xt[0][:], src.ap[0:128, :], r=(src.tiles[0],), w=(xt[0],))
                if NT > 1:
                    k.dma(xt[1][:], src.ap[128:256, :], r=(src.tiles[1],), w=(xt[1],))
                def chain(*gs):
                    for g_ in gs:
                        yield from g_

                g0 = front(0)
                next(g0)
                for kc in range(8):
                    for hf in range(4):
                        load_cast(w_in, w_in[:, kc, hf * 706:(hf + 1) * 706],
                                  w_in_d[l, kc * 128:(kc + 1) * 128, hf * 706:(hf + 1) * 706], 706)
                k.memset("pool", wo_pool[:], 0.0, (wo_pool,))
                for g in range(4):
                    for hf in range(2):
                        load_cast(wo_pool, wo_pool[0:64, g, hf * 512:(hf + 1) * 512],
                                  w_out_d[l, g * 64:(g + 1) * 64, hf * 512:(hf + 1) * 512], 512, nrows=64)
                for c in range(6):
                    for hf in range(2):
                        load_cast(wo, wo[:, c, hf * 512:(hf + 1) * 512],
                                  w_out_d[l, 256 + c * 128:256 + (c + 1) * 128, hf * 512:(hf + 1) * 512], 512)
                run_all([g0])
                for i in range(NT):
                    par = i % 2
                    x_t = xt[par]
                    gu, gv, r_qd, r_kd, r_sg, d_sz = gu_2[par], gv_2[par], r_qd_2[par], r_kd_2[par], r_sg_2[par], d_sz_2[par]
                    r_v_bf, d_ab = r_v_bf_2[par], d_ab_2[par]
                    ab = a_bf[par]
                    ap_ = a_bf[(i + 1) % 2]

                    def m_pool():
                        psD = k.ps()
                        ptm = ptf if i == 0 else ptr
                        for g in range(4):
                            k.mm(psD[0:64, g * 128:(g + 1) * 128], ab[:, g * 64:(g + 1) * 64],
                                 ptm[:, g * 128:(g + 1) * 128], True, False, (ab, ptm), (psD,),
                                 inc=False)
                            if True:
                                k.mm(psD[0:64, g * 128:(g + 1) * 128], ap_[:, g * 64:(g + 1) * 64],
                                     ptp[:, g * 128:(g + 1) * 128], False, True, (ap_, ptp), (psD,), inc=(g == 3))
                        k.cp("act", dT_bf[0:64, :, :].rearrange("p g t -> p (g t)"), psD[0:64, :], (psD,), (dT_bf,))
                        yield
                        psYa = k.ps()
                        for g in range(4):
                            k.mm(psYa[0:64, g * 128:(g + 1) * 128], poolw[:, g, 0:64], dT_bf[:, g, :], True, True,
                                 (poolw, dT_bf), (psYa,), inc=(g == 3))
                        k.tt("dve", yaT[0:64, :, :], psYa[0:64, :].rearrange("p (g t) -> p g t", g=4),
                             bc(P("poolsc")[0:64, :].unsqueeze(2), [64, 4, 128]), ALU.mult, (psYa, lp), (yaT,))

                    def m_sgu():
                        k.ts("dve", s_nm[:, 0:1], s_sum[:, par:par + 1], -1.0 / 256, None, ALU.mult, None, (s_sum,), (s_nm,))
                        k.act(junk[:, 0:256], gv[:], AF.Square, (gv, s_nm), (junk, s_ss), bias=s_nm[:, 0:1],
                              accum_out=s_ss[:, 0:1])
                        rstd_from(s_ss, 256, 1, s_tmp, s_rs)
                        k.ts("dve", vln[:], gv[:], s_nm[:, 0:1], s_rs[:, 0:1], ALU.add, ALU.mult, (gv, s_nm, s_rs), (vln,))
                        k.tt("dve", vln2[:], vln[:], P("lng"), ALU.mult, (vln, lp), (vln2,))
                        k.tt("dve", vln_bf[:], vln2[:], P("lnb"), ALU.add, (vln2, lp), (vln_bf,))
                        yield
                        psS = k.ps()
                        for g in range(4):
                            k.mm(psS[:, g * 64:(g + 1) * 64], wmT[:, g, :], vln_bf[:, g * 64:(g + 1) * 64], True, True,
                                 (wmT, vln_bf), (psS,), inc=(g == 3))
                        k.tt("dve", v4(yc_t[:]), v4(psS[:, 0:256]), bc(P("bs").unsqueeze(2), [128, 4, 64]), ALU.add,
                             (psS, lp), (yc_t,))
                        k.tt("dve", yc_bf[:], yc_t[:], gu[:], ALU.mult, (yc_t, gu), (yc_bf,))
                        yield
                        pT2 = k.psb()
                        for c in range(2):
                            k.tr(pT2[:, c * 128:(c + 1) * 128], yc_bf[:, c * 128:(c + 1) * 128], ident[:],
                                 (yc_bf, ident), (pT2,), inc=(c == 1))
                        k.cp("act", yT[:, 2:4, :].rearrange("p c t -> p (c t)"), pT2[:, 0:256], (pT2,), (yT,))

                    def m_ret():
                        cosb = bc(rp("cos", i * 32, (i + 1) * 32).unsqueeze(1).unsqueeze(1), [128, 4, 2, 32])
                        sinb = bc(rp("sin", i * 32, (i + 1) * 32).unsqueeze(1), [128, 4, 32])
                        nsinb = bc(rp("nsin", i * 32, (i + 1) * 32).unsqueeze(1), [128, 4, 32])
                        v42 = lambda ap: ap.rearrange("p (h two x) -> p h two x", h=4, two=2)

                        def rope(xd, t1, t2, eng2):
                            k.tt(eng2, v42(t1[:]), v42(xd[:]), cosb, ALU.mult, (xd, ropet), (t1,))
                            k.tt(eng2, v42(t2[:])[:, :, 0, :], v42(xd[:])[:, :, 1, :], nsinb, ALU.mult, (xd, ropet), (t2,))
                            k.tt(eng2, v42(t2[:])[:, :, 1, :], v42(xd[:])[:, :, 0, :], sinb, ALU.mult, (xd, ropet), (t2,))

                        rope(r_qd, r_t1, r_t2, "pool")
                        k.tt("dve", r_q_bf[:], r_t1[:], r_t2[:], ALU.add, (r_t1, r_t2), (r_q_bf,))
                        yield
                        rope(r_kd, r_u1, r_u2, "pool")
                        for hh in range(2):
                            k.tt("dve", r_kz[:, hh::2, hh * 64:(hh + 1) * 64],
                                 v4(r_u1[:])[:, hh::2, :], v4(r_u2[:])[:, hh::2, :], ALU.add, (r_u1, r_u2), (r_kz,))
                        pT3 = k.psb()
                        for c in range(2):
                            k.tr(pT3[:, c * 128:(c + 1) * 128], r_q_bf[:, c * 128:(c + 1) * 128], ident[:],
                                 (r_q_bf, ident), (pT3,), inc=False)
                        for h in range(4):
                            k.tr(pT3[:, (2 + h) * 128:(3 + h) * 128], r_kz[:, h, :], ident[:],
                                 (r_kz, ident), (pT3,), inc=(h == 3))
                        k.cp("act", r_qT[:].rearrange("p c t -> p (c t)"), pT3[:, 0:256], (pT3,), (r_qT,))
                        k.cp("dve", r_kTz[:].rearrange("p c t -> p (c t)"), pT3[:, 256:768], (pT3,), (r_kTz,))
                        yield
                        psSc = k.ps()
                        for h in range(4):
                            k.mm(psSc[:, h * 128:(h + 1) * 128], r_kTz[:, h, :], r_qT[:, h // 2, :], True, True,
                                 (r_kTz, r_qT), (psSc,), inc=(h == 3))
                        k.tt("dve", r_scm[:], psSc[:, :].rearrange("p (h s) -> p h s", h=4),
                             bc(cs("triu").unsqueeze(1), [128, 4, 128]), ALU.mult, (psSc, cst), (r_scm,))
                        yield
                        psO = k.ps()
                        for j in range(2):
                            k.mm(psO[:, j * 128:(j + 1) * 128], r_qT[:, j, :], r_S_bf[:, j, :], True, False,
                                 (r_qT, r_S_bf), (psO,), inc=False)
                            for hh in range(2):
                                h = 2 * j + hh
                                k.mm(psO[:, h * 64:(h + 1) * 64], r_scm[:, h, :], r_v_bf[:, h * 64:(h + 1) * 64],
                                     False, hh == 1, (r_scm, r_v_bf), (psO,), inc=(h == 3))
                        psKV = k.ps()
                        for h in range(4):
                            k.mm(psKV[:, h * 64:(h + 1) * 64], r_kz[:, h, :], r_v_bf[:, h * 64:(h + 1) * 64], True, True,
                                 (r_kz, r_v_bf), (psKV,), inc=(h == 3))
                        k.tt("dve", r_St[:].rearrange("p j x -> p (j x)"), psKV[:, 0:256],
                             r_S[:].rearrange("p j x -> p (j x)"), ALU.add, (psKV, r_S), (r_St,))
                        k.tt("dve", r_S[:].rearrange("p j (hh e) -> p (j hh) e", hh=2),
                             r_St[:].rearrange("p j (hh e) -> p (j hh) e", hh=2),
                             bc(cs("g128").unsqueeze(2), [128, 4, 64]), ALU.mult, (r_St, cst), (r_S,))
                        k.cp("act", r_S_bf[:], r_S[:], (r_S,), (r_S_bf,))
                        for h in range(4):
                            k.act(junk[:, h * 64:(h + 1) * 64], psO[:, h * 64:(h + 1) * 64], AF.Square, (psO,),
                                  (junk, r_sso), accum_out=r_sso[:, h:h + 1])
                        rstd_from(r_sso, 64, 4, r_tm, r_rs, f=0.5)
                        k.tt("dve", v4(r_y[:]), v4(psO[:, 0:256]), bc(r_rs[:, 0:4].unsqueeze(2), [128, 4, 64]), ALU.mult,
                             (psO, r_rs), (r_y,))
                        yield
                        k.tt("dve", r_y_bf[:], r_y[:], r_sg[:], ALU.mult, (r_y, r_sg), (r_y_bf,))
                        pT4 = k.psb()
                        for c in range(2):
                            k.tr(pT4[:, c * 128:(c + 1) * 128], r_y_bf[:, c * 128:(c + 1) * 128], ident[:],
                                 (r_y_bf, ident), (pT4,), inc=(c == 1))
                        k.cp("act", yT[:, 0:2, :].rearrange("p c t -> p (c t)"), pT4[:, 0:256], (pT4,), (yT,))

                    def dn_head(pp):
                        d_sz, d_ab = d_sz_2[pp], d_ab_2[pp]
                        L0, Z0, d_P0, d_atT = d_L0_2[pp], d_Z0_2[pp], d_P0_2[pp], d_atT_2[pp]
                        d_kbz, d_ktz, d_vb, d_qd, d_szg, d_edl = d_kbz_2[pp], d_ktz_2[pp], d_vb_2[pp], d_qd_2[pp], d_szg_2[pp], d_edl_2[pp]
                        cw = P("convw").rearrange("p (c j) -> p c j", c=6)
                        k.tt("dve", d_acc[:], d_xc[:, :, 0:128], bc(cw[:, :, 0:1], [128, 6, 128]), ALU.mult, (d_xc, lp), (d_acc,))
                        for j in range(1, 4):
                            k.tt("dve", d_tmp[:], d_xc[:, :, j:j + 128], bc(cw[:, :, j:j + 1], [128, 6, 128]), ALU.mult,
                                 (d_xc, lp), (d_tmp,))
                            k.tt("dve", d_acc[:], d_acc[:], d_tmp[:], ALU.add, (d_acc, d_tmp), (d_acc,))
                        k.cp("pool", d_xc[:, :, 0:3], d_xc[:, :, 128:131], (d_xc,), (d_xc,))
                        yield
                        k.act(d_tmp[:], d_acc[:], AF.Tanh, (d_acc,), (d_tmp,), scale=0.5)
                        k.stt("dve", d_qkv[:], d_tmp[:], 1.0, d_acc[:], ALU.add, ALU.mult, (d_tmp, d_acc), (d_qkv,))
                        k.tt("dve", v4(d_szg[:]), v4(d_sz[:]), bc(P("normg").unsqueeze(1), [128, 4, 64]),
                             ALU.mult, (d_sz, lp), (d_szg,))
                        k.act(d_sc[:, 12:16], d_ab[:, 4:8], AF.Exp, (d_ab,), (d_sc,), scale=-1.0)
                        k.act(d_sc[:, 4:8], d_ab[:, 0:4], AF.Exp, (d_ab,), (d_sc,))
                        k.act(d_sc[:, 8:12], d_sc[:, 4:8], AF.Ln, (d_sc,), (d_sc,), bias=1.0)
                        k.tt("dve", d_g[:], d_sc[:, 8:12], nexpA[:], ALU.mult, (d_sc, nexpA), (d_g,))
                        k.ts("dve", d_sc[:, 12:16], d_sc[:, 12:16], 1.0, None, ALU.add, None, (d_sc,), (d_sc,))
                        k.op("dve", lambda e: e.reciprocal(out=d_sc[:, 16:20], in_=d_sc[:, 12:16]), (d_sc,), (d_sc,))
                        k.ts("dve", d_hb[:, 0:4], d_sc[:, 16:20], 0.5, None, ALU.mult, None, (d_sc,), (d_hb,))
                        yield
                        for h in range(4):
                            k.ts("dve", d_Ug[:, h, :], cs("triu"), d_g[:, h:h + 1], None, ALU.mult, None, (cst, d_g), (d_Ug,))
                        psC = k.ps()
                        k.mm(psC[:, 0:4], cs("triu"), d_g[:, 0:4], True, True, (cst, d_g), (psC,), inc=False)
                        k.mm(psC[:, 4:8], cs("ones"), d_g[:, 0:4], True, True, (cst, d_g), (psC,), inc=True)
                        psR = k.ps()
                        k.mm(psR[:, :], cs("ones"), d_Ug[:].rearrange("p h x -> p (h x)"), True, True, (cst, d_Ug), (psR,), inc=True)
                        k.cp("dve", d_dec[:, 0:8], psC[:, 0:8], (psC,), (d_dec,))
                        k.act(d_sc[:, 24:28], d_dec[:, 0:4], AF.Exp, (d_dec,), (d_sc,))
                        k.tt("dve", d_sc[:, 0:4], d_dec[:, 4:8], d_dec[:, 0:4], ALU.subtract, (d_dec,), (d_sc,))
                        k.act(d_sc[:, 28:32], d_sc[:, 0:4], AF.Exp, (d_sc,), (d_sc,))
                        k.act(d_edl[:, 0:4], d_dec[:, 4:8], AF.Exp, (d_dec,), (d_edl,))
                        k.stt("dve", d_sc[:, 20:24], d_sc[:, 16:20], -1.0, d_sc[:, 24:28], ALU.mult, ALU.mult, (d_sc,), (d_sc,))
                        k.tt("dve", d_arg[:], psR[:, :].rearrange("p (h x) -> p h x", h=4),
                             bc(d_dec[:, 0:4].unsqueeze(2), [128, 4, 128]), ALU.subtract, (psR, d_dec), (d_arg,))
                        k.tt("dve", d_arg[:], d_arg[:], bc(cs("negmask").unsqueeze(1), [128, 4, 128]), ALU.add, (d_arg, cst), (d_arg,))
                        k.act(d_E[:], d_arg[:], AF.Exp, (d_arg,), (d_E,), scale=-1.0)
                        k.act(d_eR[:].rearrange("p h x -> p (h x)"), psR[:, :], AF.Exp, (psR,), (d_eR,))
                        yield
                        k.tt("dve", d_Eb[:], d_E[:], bc(cs("strict").unsqueeze(1), [128, 4, 128]), ALU.mult, (d_E, cst), (d_Eb,))
                        k.tt("dve", d_Eb[:], d_Eb[:], bc(d_sc[:, 16:20].unsqueeze(2), [128, 4, 128]), ALU.mult, (d_Eb, d_sc), (d_Eb,))
                        k.act(d_sq[:], d_qkv[:, 0:4, :], AF.Square, (d_qkv,), (d_sq,))
                        psN = k.ps()
                        k.mm(psN[:, :], blk[:], d_sq[:].rearrange("p c t -> p (c t)"), True, True, (blk, d_sq), (psN,), inc=True)
                        k.act(d_rn[:, 0:2, :].rearrange("p c t -> p (c t)"), psN[:, 0:256], AF.Ln, (psN,), (d_rn,),
                              scale=64.0, bias=256.0 * EPS)
                        k.act(d_rn[:, 2:4, :].rearrange("p c t -> p (c t)"), psN[:, 256:512], AF.Ln, (psN,), (d_rn,),
                              scale=1.0, bias=4.0 * EPS)
                        k.act(d_rn[:], d_rn[:], AF.Exp, (d_rn,), (d_rn,), scale=-0.5)
                        k.tt("dve", d_qkn[:], d_qkv[:, 0:4, :], d_rn[:], ALU.mult, (d_qkv, d_rn), (d_qkn,))
                        k.cp("act", d_vT[:], d_qkv[:, 4:6, :], (d_qkv,), (d_vT,))
                        yield
                        for hh in range(2):
                            sl = slice(hh * 64, (hh + 1) * 64)
                            k.cp("act", d_qz[sl, hh::2, :], d_qkn[sl, 0:2, :], (d_qkn,), (d_qz,))
                            k.cp("dve", d_kz[sl, hh::2, :], d_qkn[sl, 2:4, :], (d_qkn,), (d_kz,))
                            k.tt("dve", d_qd[sl, :, :], d_qkn[sl, 0:2, :], d_eR[sl, hh::2, :], ALU.mult, (d_qkn, d_eR), (d_qd,))
                        psKK = k.ps()
                        psQK = k.ps()
                        for h in range(4):
                            k.mm(psKK[:, h * 128:(h + 1) * 128], d_kz[:, h, :], d_qkn[:, 2 + h // 2, :], True, True,
                                 (d_kz, d_qkn), (psKK,), inc=(h == 3))
                        for h in range(4):
                            k.mm(psQK[:, h * 128:(h + 1) * 128], d_qz[:, h, :], d_qkn[:, 2 + h // 2, :], True, True,
                                 (d_qz, d_qkn), (psQK,), inc=(h == 3))
                        k.tt("dve", L0[:].rearrange("p h x -> p (h x)"), psKK[:, :], d_Eb[:].rearrange("p h x -> p (h x)"),
                             ALU.mult, (psKK, d_Eb), (L0,))
                        k.tt("dve", d_at[:].rearrange("p h x -> p (h x)"), psQK[:, :], d_E[:].rearrange("p h x -> p (h x)"),
                             ALU.mult, (psQK, d_E), (d_at,))
                        yield
                        pT5 = k.psb()
                        for h in range(4):
                            k.tr(pT5[:, h * 128:(h + 1) * 128], L0[:, h, :], ident[:], (L0, ident), (pT5,), inc=False)
                        for h in range(4):
                            k.tr(pT5[:, (4 + h) * 128:(5 + h) * 128], d_at[:, h, :], ident[:], (d_at, ident), (pT5,), inc=(h == 3))
                        k.cp("act", Z0[:].rearrange("p h x -> p (h x)"), pT5[:, 0:512], (pT5,), (Z0,))
                        k.cp("dve", d_atT[:].rearrange("p h x -> p (h x)"), pT5[:, 512:1024], (pT5,), (d_atT,))
                        yield
                        pT6 = k.psb()
                        for c in range(2):
                            k.tr(pT6[:, c * 128:(c + 1) * 128], d_qkn[:, 2 + c, :], ident[:], (d_qkn, ident), (pT6,), inc=False)
                        for c in range(2):
                            k.tr(pT6[:, (2 + c) * 128:(3 + c) * 128], d_vT[:, c, :], ident[:], (d_vT, ident), (pT6,), inc=(c == 1))
                        ktok = pT6[:, 0:256].rearrange("p (h x) -> p h x", h=4)
                        for hh in range(2):
                            k.tt("dve", d_kbz[:, hh::2, hh * 64:(hh + 1) * 64], ktok[:, hh::2, :],
                                 bc(d_sc[:, 20 + hh:24:2].unsqueeze(2), [128, 2, 64]), ALU.mult, (pT6, d_sc), (d_kbz,))
                            k.tt("dve", d_ktz[:, hh::2, hh * 64:(hh + 1) * 64], ktok[:, hh::2, :],
                                 bc(d_sc[:, 28 + hh:32:2].unsqueeze(2), [128, 2, 64]), ALU.mult, (pT6, d_sc), (d_ktz,))
                        k.tt("dve", v4(d_vb[:]), pT6[:, 256:512].rearrange("p (h x) -> p h x", h=4),
                             bc(d_hb[:, 0:4].unsqueeze(2), [128, 4, 64]), ALU.mult, (pT6, d_hb), (d_vb,))
                        yield
                        k.stt("dve", d_P0[:], Z0[:], -1.0, bc(ident[:, :].unsqueeze(1), [128, 4, 128]), ALU.mult, ALU.add,
                              (ident, Z0), (d_P0,))

                    def dn_tail(pp):
                        L0, Z0, d_P0, d_atT = d_L0_2[pp], d_Z0_2[pp], d_P0_2[pp], d_atT_2[pp]
                        d_kbz, d_ktz, d_vb, d_qd, d_szg, d_edl = d_kbz_2[pp], d_ktz_2[pp], d_vb_2[pp], d_qd_2[pp], d_szg_2[pp], d_edl_2[pp]
                        psP = k.dedicated
                        Lc, Zc, Pc = L0, Z0, d_P0
                        NST = 5
                        for st in range(1, NST + 1):
                            Ln_, Zn_ = d_L[st % 2], d_Z[st % 2]
                            psA = k.ps()
                            for h in range(4):
                                k.mm(psA[:, h * 128:(h + 1) * 128], Zc[:, h, :], Lc[:, h, :], True, True, (Zc, Lc), (psA,), inc=(h == 3))
                            if st < NST:
                                psB = k.ps()
                                for h in range(4):
                                    k.mm(psB[:, h * 128:(h + 1) * 128], Lc[:, h, :], Zc[:, h, :], True, True, (Zc, Lc), (psB,), inc=(h == 3))
                            k.cp("act", Ln_[:].rearrange("p h x -> p (h x)"), psA[:, :], (psA,), (Ln_,))
                            if st < NST:
                                k.cp("dve", Zn_[:].rearrange("p h x -> p (h x)"), psB[:, :], (psB,), (Zn_,))
                            for h in range(4):
                                k.mm(psP[:, h * 128:(h + 1) * 128], Ln_[:, h, :], Pc[:, h, :], st == 1 and h == 0, st == NST,
                                     (Ln_, Pc), (psP,), inc=(h == 3))
                            Pn = d_Pb[st % 2]
                            k.tt("dve", Pn[:].rearrange("p h x -> p (h x)"), psP[:, :], d_P0[:].rearrange("p h x -> p (h x)"),
                                 ALU.add, (psP, d_P0), (Pn,))
                            Lc, Zc, Pc = Ln_, Zn_, Pn
                            yield
                        psZP = k.ps()
                        for h in range(4):
                            k.mm(psZP[:, h * 128:(h + 1) * 128], L0[:, h, :], Pc[:, h, :], True, False, (L0, Pc), (psZP,), inc=False)
                            k.mm(psZP[:, h * 128:(h + 1) * 128], ident[:], Pc[:, h, :], False, True, (ident, Pc), (psZP,), inc=(h == 3))
                        k.stt("dve", d_Rp[:], psZP[:, :].rearrange("p (h x) -> p h x", h=4), -1.0,
                              bc(ident[:, :].unsqueeze(1), [128, 4, 128]), ALU.mult, ALU.add, (psZP, ident), (d_Rp,))
                        pT8 = k.psb()
                        for h in range(4):
                            k.tr(pT8[:, h * 128:(h + 1) * 128], Pc[:, h, :], ident[:], (Pc, ident), (pT8,), inc=(h == 3))
                        k.cp("act", d_Pt[:].rearrange("p h x -> p (h x)"), pT8[:, 0:512], (pT8,), (d_Pt,))
                        yield
                        psDl = k.ps()
                        for h in range(4):
                            k.mm(psDl[:, h * 128:(h + 1) * 128], d_Pt[:, h, :], d_Rp[:, h, :], True, True, (d_Pt, d_Rp), (psDl,), inc=(h == 3))
                        k.cp("act", d_Dl[:].rearrange("p h x -> p (h x)"), psDl[:, :], (psDl,), (d_Dl,))
                        yield
                        psW = k.ps()
                        for j in range(2):
                            for hh in range(2):
                                h = 2 * j + hh
                                k.mm(psW[:, j * 128:(j + 1) * 128], d_kbz[:, h, :], Pc[:, h, :], hh == 0, False,
                                     (d_kbz, Pc), (psW,), inc=False)
                                k.mm(psW[:, j * 128:(j + 1) * 128], d_kbz[:, h, :], d_Dl[:, h, :], False, hh == 1,
                                     (d_kbz, d_Dl), (psW,), inc=(h == 3))
                        k.cp("act", d_nWT[:].rearrange("p j x -> p (j x)"), psW[:, 0:256], (psW,), (d_nWT,))
                        yield
                        psU = k.ps()
                        for j in range(2):
                            k.mm(psU[:, j * 128:(j + 1) * 128], d_nWT[:, j, :], d_S_bf[:, j, :], True, False,
                                 (d_nWT, d_S_bf), (psU,), inc=False)
                            for hh in range(2):
                                h = 2 * j + hh
                                k.mm(psU[:, h * 64:(h + 1) * 64], Pc[:, h, :], d_vb[:, h * 64:(h + 1) * 64], False, False,
                                     (Pc, d_vb), (psU,), inc=False)
                                k.mm(psU[:, h * 64:(h + 1) * 64], d_Dl[:, h, :], d_vb[:, h * 64:(h + 1) * 64], False, hh == 1,
                                     (d_Dl, d_vb), (psU,), inc=(h == 3))
                        k.cp("act", d_vn[:], psU[:, 0:256], (psU,), (d_vn,))
                        yield
                        psO2 = k.ps()
                        for j in range(2):
                            k.mm(psO2[:, j * 128:(j + 1) * 128], d_qd[:, j, :], d_S_bf[:, j, :], True, False,
                                 (d_qd, d_S_bf), (psO2,), inc=False)
                            for hh in range(2):
                                h = 2 * j + hh
                                k.mm(psO2[:, h * 64:(h + 1) * 64], d_atT[:, h, :], d_vn[:, h * 64:(h + 1) * 64], False, hh == 1,
                                     (d_atT, d_vn), (psO2,), inc=(h == 3))
                        psK2 = k.ps()
                        for h in range(4):
                            k.mm(psK2[:, h * 64:(h + 1) * 64], d_ktz[:, h, :], d_vn[:, h * 64:(h + 1) * 64], True, True,
                                 (d_ktz, d_vn), (psK2,), inc=(h == 3))
                        k.tt("dve", d_St[:].rearrange("p j (hh e) -> p (j hh) e", hh=2),
                             d_S[:].rearrange("p j (hh e) -> p (j hh) e", hh=2),
                             bc(d_edl[:, 0:4].unsqueeze(2), [128, 4, 64]), ALU.mult, (d_S, d_edl), (d_St,))
                        k.tt("dve", d_S[:].rearrange("p j x -> p (j x)"), d_St[:].rearrange("p j x -> p (j x)"), psK2[:, 0:256],
                             ALU.add, (d_St, psK2), (d_S,))
                        k.cp("act", d_S_bf[:], d_S[:], (d_S,), (d_S_bf,))
                        for h in range(4):
                            k.act(junk[:, h * 64:(h + 1) * 64], psO2[:, h * 64:(h + 1) * 64], AF.Square, (psO2,),
                                  (junk, d_sso), accum_out=d_sso[:, h:h + 1])
                        rstd_act(d_sso, 64, 4, d_tm, d_rs, f=0.5)
                        k.tt("dve", v4(d_y[:]), v4(psO2[:, 0:256]), bc(d_rs[:, 0:4].unsqueeze(2), [128, 4, 64]), ALU.mult,
                             (psO2, d_rs), (d_y,))
                        yield
                        k.tt("dve", d_y_bf[:], d_y[:], d_szg[:], ALU.mult, (d_y, d_szg), (d_y_bf,))
                        pT7 = k.psb()
                        for c in range(2):
                            k.tr(pT7[:, c * 128:(c + 1) * 128], d_y_bf[:, c * 128:(c + 1) * 128], ident[:],
                                 (d_y_bf, ident), (pT7,), inc=(c == 1))
                        k.cp("act", yT[:, 4:6, :].rearrange("p c t -> p (c t)"), pT7[:, 0:256], (pT7,), (yT,))

                    if i == 0 and "dn" in mixers:
                        run_all([dn_head(0)])
                    gens = []
                    if "dn" in mixers:
                        gens.append(dn_tail(par))
                    for nm_, fn_ in (("ret", m_ret), ("sgu", m_sgu), ("pool", m_pool)):
                        if nm_ in mixers:
                            gens.append(fn_())
                    if i + 1 < NT:
                        if "dn" in mixers:
                            gens.append(chain(front(i + 1), dn_head((i + 1) % 2)))
                        else:
                            gens.append(front(i + 1))
                    run_all(gens)

                    psY = [k.ps(), k.ps()]
                    for n in range(2):
                        for g in range(4):
                            k.mm(psY[n][:, :], yaT[:, g, :], wo_pool[:, g, n * 512:(n + 1) * 512], g == 0, False,
                                 (yaT, wo_pool), (psY[n],), inc=False)
                        for c in range(6):
                            k.mm(psY[n][:, :], yT[:, c, :], wo[:, c, n * 512:(n + 1) * 512], False, c == 5,
                                 (yT, wo), (psY[n],), inc=(c == 5))
                    for n in range(2):
                        k.act(junk[:, n * 512:(n + 1) * 512], psY[n][:, :], AF.Square, (psY[n],), (junk, o_ss),
                              accum_out=o_ss[:, n:n + 1])
                    k.tt("dve", o_ss[:, 2:3], o_ss[:, 0:1], o_ss[:, 1:2], ALU.add, (o_ss,), (o_ss,))
                    k.act(o_tm[:, 0:1], o_ss[:, 2:3], AF.Ln, (o_ss,), (o_tm,), scale=1.0 / D, bias=EPS)
                    k.act(o_rs[:, 0:1], o_tm[:, 0:1], AF.Exp, (o_tm,), (o_rs,), scale=-0.5)
                    if skew:
                        k.tt("dve", o_rs[:, 0:1], o_rs[:, 0:1], flg[:, 1 + l:2 + l], ALU.mult, (o_rs, flg), (o_rs,))
                    xo = xmo[i % 2]
                    for n in range(2):
                        k.stt("dve", o_t[n][:], psY[n][:, :], o_rs[:, 0:1], P("gpost", n * 512, (n + 1) * 512), ALU.mult, ALU.mult,
                              (psY[n], o_rs, lp), (o_t[n],))
                        k.tt("dve", xo[:, n * 512:(n + 1) * 512], o_t[n][:], x_t[:, n * 512:(n + 1) * 512], ALU.add,
                             (o_t[n], x_t), (xo,))
                    tgt = dx_m if do_ffn else dst
                    k.dma(tgt.ap[i * 128:(i + 1) * 128, :], xo[:], r=(xo,), w=(tgt.tiles[i],))
                    if i + 2 < NT:
                        k.dma(xt[par][:], src.ap[(i + 2) * 128:(i + 3) * 128, :], r=(src.tiles[i + 2],), w=(xt[par],))
                if skew and l < L - 1:
                    k.dma(st_out[:, 0:256], r_S[:].rearrange("p j x -> p (j x)"), r=(r_S,), w=(stb_out,))
                    k.dma(st_out[:, 256:512], d_S[:].rearrange("p j x -> p (j x)"), r=(d_S,), w=(stb_out,))
                    k.cp("dve", r_y[:], a_bf[(NT - 1) % 2][:], (a_bf[(NT - 1) % 2],), (r_y,))
                    k.dma(st_out[:, 512:768], r_y[:], r=(r_y,), w=(stb_out,))
                    k.cp("dve", d_y[:, 0:18].rearrange("p (c j) -> p c j", c=6), d_xc[:, :, 0:3], (d_xc,), (d_y,))
                    k.dma(st_out[:, 768:786], d_y[:, 0:18], r=(d_y,), w=(stb_out,))
                k.barrier()

            if not do_ffn:
                continue
            with ExitStack() as pb:
                k.barrier()

                def sb(name, shape, dt=F32):
                    return k.sb("b_" + name, shape, dt, es=pb)

                stg[:] = [sb("stg%d" % i_, [128, 704], F32) for i_ in range(5)]
                w_up = sb("w_up", [128, 8, 2 * DFF], BF16)
                w_dn = sb("w_dn", [128, NFF, D], BF16)
                lp = sb("lp", [128, nB], F32)
                k.dma(lp[:], lpb_d[l], r=(), w=(lp,))

                def P(name, a=None, b=None):
                    o, n = offB[name]
                    if a is None:
                        return lp[:, o:o + n]
                    return lp[:, o + a:o + b]


                NTB = NT // 2
                k.work = k.ps_f[0:6]
                xb = [sb("xb%d" % i, [128, 2, D]) for i in range(2)]
                junk = sb("junk", [128, D], BF16)
                xn = sb("xn", [128, 2, D], BF16)
                hT = sb("hT", [128, 8, 256], BF16)
                gT = sb("gT", [128, NFF, 256], BF16)
                ss = sb("ss", [128, 4])
                tmp4 = sb("tmp4", [128, 4])
                rstd = sb("rstd", [128, 4])
                ca = [sb("ca%d" % i, [128, 258]) for i in range(2)]
                acc = [sb("acc%d" % i, [128, 256]) for i in range(2)]
                ge = [sb("ge%d" % i, [128, 256]) for i in range(2)]
                halo = sb("halo", [128, NFF, 2])
                k.memset("pool", halo[:], 0.0, (halo,))
                o_ss = sb("o_ss", [128, 4])
                o_tm = sb("o_tm", [128, 4])
                o_rs = sb("o_rs", [128, 4])
                o_t = [sb("o_t%d" % i, [128, 512]) for i in range(2)]
                fcw = P("fcw").rearrange("p (c j) -> p c j", c=NFF)
                if l == 0:
                    print("[build] phase B sbuf bytes remaining/partition:", nc.sbuf_bytes_remaining)
                if skew and l > 0:
                    k.dma(o_t[0][:, 0:44], st_all[0:128, 786:830], r=(stb_all,), w=(o_t[0],))
                    k.ts("dve", halo[:].rearrange("p c j -> p (c j)"), o_t[0][:, 0:44], flg[:, 0:1], None, ALU.mult, None,
                         (o_t[0], flg), (halo,))

                def ldx(i):
                    for s in range(2):
                        ti = 2 * i + s
                        k.dma(xb[i % 2][:, s, :], dx_m.ap[ti * 128:(ti + 1) * 128, :], r=(dx_m.tiles[ti],), w=(xb[i % 2],))

                def prenorm1(xx):
                    for s in range(2):
                        k.act(junk[:], xx[:, s, :], AF.Square, (xx,), (junk, ss), accum_out=ss[:, s:s + 1])
                    rstd_from(ss, D, 2, tmp4, rstd)
                    for s in range(2):
                        k.ts("dve", xn[:, s, :], xx[:, s, :], rstd[:, s:s + 1], None, ALU.mult, None, (xx, rstd), (xn,))

                def prenorm2():
                    for s in range(2):
                        pT = k.psb()
                        for kc in range(8):
                            k.tr(pT[:, kc * 128:(kc + 1) * 128], xn[:, s, kc * 128:(kc + 1) * 128], ident[:],
                                 (xn, ident), (pT,), inc=(kc == 7))
                        k.tt("dve", hT[:, :, s * 128:(s + 1) * 128], pT[:, :].rearrange("p (c t) -> p c t", c=8),
                             bc(P("gpre").unsqueeze(2), [128, 8, 128]), ALU.mult, (pT, lp), (hT,))

                ldx(0)
                if NTB > 1:
                    ldx(1)
                prenorm1(xb[0])
                prenorm2()
                for kc in range(8):
                    for q8 in range(8):
                        load_cast(w_up, w_up[:, kc, q8 * 704:(q8 + 1) * 704],
                                  w_up_d[l, kc * 128:(kc + 1) * 128, q8 * 704:(q8 + 1) * 704], 704)
                for c in range(NFF):
                    for hf in range(2):
                        load_cast(w_dn, w_dn[:, c, hf * 512:(hf + 1) * 512],
                                  w_dn_d[l, c * 128:(c + 1) * 128, hf * 512:(hf + 1) * 512], 512)
                for i in range(NTB):
                    x_t = xb[i % 2]
                    pend = [None]
                    for c in range(NFF):
                        ps = k.ps()
                        for kc in range(8):
                            k.mm(ps[:, 0:256], w_up[:, kc, c * 128:(c + 1) * 128], hT[:, kc, :], kc == 0, kc == 7,
                                 (w_up, hT), (ps,), inc=False)
                        for kc in range(8):
                            k.mm(ps[:, 256:512], w_up[:, kc, DFF + c * 128:DFF + (c + 1) * 128], hT[:, kc, :], kc == 0, kc == 7,
                                 (w_up, hT), (ps,), inc=(kc == 7))
                        ca_, acc_, ge_ = ca[c % 2], acc[c % 2], ge[c % 2]
                        k.cp("pool", ca_[:, 0:2], halo[:, c, :], (halo,), (ca_,))
                        k.cp("act", ca_[:, 2:258], ps[:, 0:256], (ps,), (ca_,))
                        k.cp("pool", halo[:, c, :], ca_[:, 256:258], (ca_,), (halo,))
                        k.ts("dve", acc_[:], ca_[:, 0:256], fcw[:, c, 0:1], P("fcb", c, c + 1), ALU.mult, ALU.add, (ca_, lp), (acc_,))
                        k.stt("dve", acc_[:], ca_[:, 1:257], fcw[:, c, 1:2], acc_[:], ALU.mult, ALU.add, (ca_, lp, acc_), (acc_,))
                        k.stt("dve", acc_[:], ca_[:, 2:258], fcw[:, c, 2:3], acc_[:], ALU.mult, ALU.add, (ca_, lp, acc_), (acc_,))
                        if pend[0] is not None:
                            pend[0]()

                        def fin(c=c, ps=ps, acc_=acc_, ge_=ge_):
                            k.act(ge_[:], acc_[:], AF.Gelu_apprx_tanh, (acc_,), (ge_,))
                            k.tt("dve", gT[:, c, :], ps[:, 256:512], ge_[:], ALU.mult, (ps, ge_), (gT,))
                        pend[0] = fin
                    pend[0]()
                    pend[0] = None
                    if i + 1 < NTB:
                        prenorm1(xb[(i + 1) % 2])
                    for s in range(2):
                        if s == 1 and i + 1 < NTB:
                            prenorm2()
                        psY = [k.ps(), k.ps()]
                        for n in range(2):
                            for c in range(NFF):
                                k.mm(psY[n][:, :], gT[:, c, s * 128:(s + 1) * 128], w_dn[:, c, n * 512:(n + 1) * 512], c == 0,
                                     c == NFF - 1, (gT, w_dn), (psY[n],), inc=(c == NFF - 1))
                        for n in range(2):
                            k.act(junk[:, n * 512:(n + 1) * 512], psY[n][:, :], AF.Square, (psY[n],), (junk, o_ss),
                                  accum_out=o_ss[:, n:n + 1])
                        k.tt("pool", o_ss[:, 2:3], o_ss[:, 0:1], o_ss[:, 1:2], ALU.add, (o_ss,), (o_ss,))
                        k.ts("pool", o_tm[:, 0:1], o_ss[:, 2:3], 1.0 / D, EPS, ALU.mult, ALU.add, (o_ss,), (o_tm,))
                        k.tt("pool", o_rs[:, 0:1], o_tm[:, 0:1], cs("nhalf", 0, 1), ALU.pow, (o_tm, cst), (o_rs,))
                        if skew:
                            k.tt("pool", o_rs[:, 0:1], o_rs[:, 0:1], flg[:, 1 + l:2 + l], ALU.mult, (o_rs, flg), (o_rs,))
                        for n in range(2):
                            k.stt("dve", o_t[n][:], psY[n][:, :], o_rs[:, 0:1], P("gpost", n * 512, (n + 1) * 512), ALU.mult, ALU.mult,
                                  (psY[n], o_rs, lp), (o_t[n],))
                            k.tt("pool", x_t[:, s, n * 512:(n + 1) * 512], o_t[n][:], x_t[:, s, n * 512:(n + 1) * 512], ALU.add,
                                 (o_t[n], x_t), (x_t,))
                        ti = 2 * i + s
                        k.dma(dst.ap[ti * 128:(ti + 1) * 128, :], x_t[:, s, :], r=(x_t,), w=(dst.tiles[ti],))
                    if i + 2 < NTB:
                        ldx(i + 2)
                if skew and l < L - 1:
                    k.dma(st_out[:, 786:830], halo[:].rearrange("p c j -> p (c j)"), r=(halo,), w=(stb_out,))
                    k.collective(GROUPS, st_out_t.ap().opt(), st_all_t.ap().opt(), r=(stb_out,), w=(stb_all,))

        for q in ("sp", "act"):
            i = k.dma_i[q]
            n = k.ndma[q]
            for j in range(n):
                cntj = (i - j + n - 1) // n if i > j else 0
                if cntj > 0:
                    k._wait("sp", ("dma", q, j), 16 * cntj)
        print("[build] instructions=%d waits=%d" % (k.nins, k.nwait))
    return nc


def make_inmaps(inputs, T_, L, batches):
    cst, ropet, pmat = host_consts(T_)
    lpa = np.stack([host_params(inputs, l)[0] for l in range(L)])
    lpb = np.stack([host_params(inputs, l)[1] for l in range(L)])
    f = lambda a: np.ascontiguousarray(np.asarray(a, np.float32))
    flags = np.ones((128, 8), np.float32)
    flags[:, 0] = 0.0
    maps = []
    for b in batches:
        maps.append({
            "x": f(inputs["x"][b, :T_]),
            "w_in": f(inputs["w_in"][:L]), "w_out": f(inputs["w_out"][:L]),
            "w_up": f(inputs["ffn_w_up"][:L]), "w_down": f(inputs["ffn_w_down"][:L]),
            "cst": cst, "ropet": ropet, "pmat": pmat, "lpa": lpa, "lpb": lpb, "flags": flags,
        })
    return maps


def make_inmaps_skew(inputs, S, L):
    Th = S // 2
    f = lambda a: np.ascontiguousarray(np.asarray(a, np.float32))
    lp = [host_params(inputs, l) for l in range(L)]
    maps = []
    B = inputs["x"].shape[0]
    slots_h = ([min(s_, L - 1) for s_ in range(L + 1)], [max(s_ - 1, 0) for s_ in range(L + 1)])
    cache = {}
    for h in range(2):
        cst, ropet, pm = host_consts(Th, pos0=h * Th)
        pmat = pm.copy()
        if h == 1:
            pmat[:, 0:512] = pm[:, 512:1024]
        sl = slots_h[h]
        flags = np.zeros((128, 8), np.float32)
        flags[:, 0] = float(h)
        for s_ in range(L + 1):
            flags[:, 1 + s_] = 1.0 if (s_ < L if h == 0 else s_ >= 1) else 0.0
        cache[h] = dict(cst=cst, ropet=ropet, pmat=pmat, flags=flags,
                        w_in=f(inputs["w_in"][sl]), w_out=f(inputs["w_out"][sl]),
                        w_up=f(inputs["ffn_w_up"][sl]), w_down=f(inputs["ffn_w_down"][sl]),
                        lpa=np.stack([lp[l][0] for l in sl]), lpb=np.stack([lp[l][1] for l in sl]))
    for c in range(2 * B):
        b, h = c // 2, c % 2
        m = dict(cache[h])
        m["x"] = f(inputs["x"][b, h * Th:(h + 1) * Th])
        maps.append(m)
    return maps


def kernel(**inputs):
    inputs = {k_: np.asarray(v) for k_, v in inputs.items()}
    B, S, _ = inputs["x"].shape
    L = inputs["w_in"].shape[0]
    Th = S // 2
    nc = build(Th, L + 1, skew=True)
    maps = make_inmaps_skew(inputs, S, L)
    res = run_bass_kernel_spmd(nc, maps, core_ids=list(range(2 * B)))
    out = np.zeros((B, S, D), np.float32)
    for c in range(2 * B):
        b, h = c // 2, c % 2
        out[b, h * Th:(h + 1) * Th] = np.asarray(res.results[c]["out"]).reshape(Th, D)
    return out
```
